# Optimizing a Trainium2 kernel written in Bass

```python
import functools
import jax
import jax.numpy as jnp
from jax import lax
import numpy as np

D_MODEL = 1024
BATCH = 16
SEQ = 2048
DEPTH = 4

GRID_W = 64
CTX_LEN = 256
N_BRANCH = 4
MIX_W = 256
HEAD_DIM = 64
N_HEADS = MIX_W // HEAD_DIM
D_FF = 2816
N_MOD = 9
ALPHA = (2 * DEPTH) ** 0.25
BETA = (8 * DEPTH) ** -0.25
LN_EPS = 1e-5
GN_EPS = 1e-5
RWKV_GN_EPS = 64e-5
RWKV_W_LORA = 64
RWKV_A_LORA = 64
RWKV_G_LORA = 160
RET_CHUNK = 128
ROPE_BASE = 10000.0
S5_GROUP = 16
S5_GROUPS = MIX_W // S5_GROUP
S5_STATE = 64
HGRN_CHUNK = 16
RWKV_SPLITS = (MIX_W, MIX_W, MIX_W, RWKV_W_LORA, RWKV_A_LORA, RWKV_G_LORA)
RET_SPLITS = (MIX_W, MIX_W, MIX_W, MIX_W)
HGRN_SPLITS = (MIX_W, MIX_W, MIX_W, MIX_W, MIX_W)
MIXER_SPLITS = (sum(RWKV_SPLITS), sum(RET_SPLITS), MIX_W, sum(HGRN_SPLITS))
P_IN = sum(MIXER_SPLITS)

kernel_name = 'hybrid_rwkv7_retnet_s5_hgrn2_diffusion_block'


def split_cols(p, sizes):
    return jnp.split(p, [int(s) for s in np.cumsum(sizes)[:-1]], axis=-1)


def to_heads(t):
    return t.reshape(t.shape[:-1] + (N_HEADS, HEAD_DIM))


def flip_time(t):
    return jnp.flip(t, axis=1)


def layer_norm(x, g, b):
    xf = x.astype(jnp.float32)
    xc = xf - jnp.mean(xf, -1, keepdims=True)
    var = jnp.mean(xc * xc, -1, keepdims=True)
    return xc * lax.rsqrt(var + LN_EPS) * g + b


def head_norm(y, g, eps, center):
    yf = y.astype(jnp.float32)
    if center:
        yf = yf - jnp.mean(yf, -1, keepdims=True)
    yf = yf * lax.rsqrt(jnp.mean(yf * yf, -1, keepdims=True) + eps)
    return yf.reshape(yf.shape[:2] + (-1,)) * g


def modulate(x, shift, scale):
    return x * (1.0 + scale[:, None]) + shift[:, None]


def swiglu(h, w1, w3, w2):
    return (jax.nn.silu(h @ w1) * (h @ w3)) @ w2


def centred_shift(p):
    prev = jnp.pad(p[:, :-1], ((0, 0), (1, 0), (0, 0)))
    nxt = jnp.pad(p[:, 1:], ((0, 0), (0, 1), (0, 0)))
    return 0.5 * (prev + nxt)


def axial_rope(t, rows, cols):
    nf = HEAD_DIM // 4
    inv = ROPE_BASE ** (-jnp.arange(nf, dtype=jnp.float32) / nf)
    ang = jnp.concatenate([rows[:, None] * inv, cols[:, None] * inv], -1)[None, :, None, :]
    cos, sin = jnp.cos(ang), jnp.sin(ang)
    t1, t2 = t[..., :HEAD_DIM // 2], t[..., HEAD_DIM // 2:]
    return jnp.concatenate([t1 * cos - t2 * sin, t1 * sin + t2 * cos], -1)


def two_stream_scan(scan_fn, ins_c, ins_l, s0, reverse):
    if reverse:
        ins_c = [flip_time(t) for t in ins_c]
        ins_l = [flip_time(t) for t in ins_l]
    y_c, s_c = scan_fn(*ins_c, s0)
    y_l, _ = scan_fn(*ins_l, s_c)
    if reverse:
        y_c, y_l = jax.tree_util.tree_map(flip_time, (y_c, y_l))
    return y_c, y_l


def rwkv7_scan(r, w, k, v, kk, b, s0):
    def step(S, inp):
        r_t, w_t, k_t, v_t, kk_t, b_t = inp
        sa = jnp.einsum('bhvk,bhk->bhv', S, kk_t)
        S = S * w_t[:, :, None, :] - sa[..., None] * b_t[:, :, None, :] + v_t[..., None] * k_t[:, :, None, :]
        return S, jnp.einsum('bhvk,bhk->bhv', S, r_t)
    xs = tuple(jnp.moveaxis(t.astype(jnp.float32), 1, 0) for t in (r, w, k, v, kk, b))
    S, ys = lax.scan(step, s0, xs)
    return jnp.moveaxis(ys, 0, 1), S


def rwkv7_mixer(p_c, p_l, mu, w0, w2, a0, a2, g2, k_k, k_a, r_k, gn_g):
    def prep(p):
        p = p + mu * (centred_shift(p) - p)
        r, k, v, wd, ad, gd = split_cols(p, RWKV_SPLITS)
        kk = to_heads(k * k_k)
        kk = kk * lax.rsqrt(jnp.maximum(jnp.sum(kk * kk, -1, keepdims=True), 1e-12))
        dirs = []
        for d in range(2):
            logw = -jax.nn.softplus(-(w0[d] + jnp.tanh(wd) @ w2[d])) - 0.5
            a = to_heads(jax.nn.sigmoid(a0[d] + ad @ a2[d]))
            kd = to_heads(k) * (1.0 + (a - 1.0) * to_heads(k_a))
            dirs.append((to_heads(jnp.exp(-jnp.exp(logw))), kd, a))
        return to_heads(r), to_heads(v), kk, gd, dirs

    r_c, v_c, kk_c, gd_c, dirs_c = prep(p_c)
    r_l, v_l, kk_l, gd_l, dirs_l = prep(p_l)
    s0 = jnp.zeros((p_c.shape[0], N_HEADS, HEAD_DIM, HEAD_DIM), jnp.float32)
    rk = to_heads(r_k)
    y_c = y_l = bonus_c = bonus_l = 0.0
    for d in range(2):
        w_c, kd_c, a_c = dirs_c[d]
        w_l, kd_l, a_l = dirs_l[d]
        o_c, o_l = two_stream_scan(rwkv7_scan, (r_c, w_c, kd_c, v_c, kk_c, kk_c * a_c),
                                   (r_l, w_l, kd_l, v_l, kk_l, kk_l * a_l), s0, d == 1)
        y_c, y_l = y_c + o_c, y_l + o_l
        bonus_c = bonus_c + jnp.sum(r_c * kd_c * rk, -1, keepdims=True) * v_c
        bonus_l = bonus_l + jnp.sum(r_l * kd_l * rk, -1, keepdims=True) * v_l

    def out(y, bonus, gd):
        y = head_norm(y, gn_g, RWKV_GN_EPS, True) + bonus.reshape(bonus.shape[:2] + (MIX_W,))
        return y * (jax.nn.sigmoid(gd) @ g2)
    return out(y_c, bonus_c, gd_c), out(y_l, bonus_l, gd_l)


def retention_chunked(q, k, v, s0, log_gamma, include_diag):
    bsz, L, H, _ = q.shape
    n = L // RET_CHUNK
    qc, kc, vc = (t.astype(jnp.float32).reshape(bsz, n, RET_CHUNK, H, -1) for t in (q, k, v))
    idx = jnp.arange(RET_CHUNK, dtype=jnp.float32)
    dist = idx[:, None] - idx[None, :]
    mask = dist >= 0 if include_diag else dist > 0
    decay = jnp.where(mask, jnp.exp(log_gamma[:, None, None] * jnp.maximum(dist, 0.0)), 0.0)
    scores = jnp.einsum('bnthd,bnshd->bnhts', qc, kc) * decay
    o = jnp.einsum('bnhts,bnshv->bnthv', scores, vc)
    q_dec = jnp.exp(log_gamma[None, :] * (idx[:, None] + 1.0))
    k_dec = jnp.exp(log_gamma[None, :] * (RET_CHUNK - 1.0 - idx[:, None]))
    kv = jnp.einsum('bnshd,bnshv->nbhdv', kc * k_dec[:, :, None], vc)
    g_chunk = jnp.exp(log_gamma * RET_CHUNK)[:, None, None]

    def step(S, kv_n):
        return S * g_chunk + kv_n, S
    S_last, S_prev = lax.scan(step, s0.astype(jnp.float32), kv)
    o = o + jnp.einsum('bnthd,nbhdv->bnthv', qc * q_dec[:, :, None], S_prev)
    return o.reshape(bsz, L, H, -1), S_last


def retention_mixer(p_c, p_l, gn_g, rows, cols):
    log_gamma = jnp.log(1.0 - 2.0 ** (-5.0 - jnp.arange(N_HEADS, dtype=jnp.float32)))

    def prep(p, rotary):
        q, k, v, g = split_cols(p, RET_SPLITS)
        q, k, v = to_heads(q), to_heads(k) * HEAD_DIM ** -0.5, to_heads(v)
        if rotary:
            q, k = axial_rope(q, rows, cols), axial_rope(k, rows, cols)
        return (q, k, v), g

    ins_c, g_c = prep(p_c, False)
    ins_l, g_l = prep(p_l, True)
    s0 = jnp.zeros((p_c.shape[0], N_HEADS, HEAD_DIM, HEAD_DIM), jnp.float32)
    fwd = two_stream_scan(functools.partial(retention_chunked, log_gamma=log_gamma, include_diag=True),
                          ins_c, ins_l, s0, False)
    bwd = two_stream_scan(functools.partial(retention_chunked, log_gamma=log_gamma, include_diag=False),
                          ins_c, ins_l, s0, True)

    def out(o, g):
        return head_norm(o, gn_g, GN_EPS, True) * jax.nn.silu(g)
    return out(fwd[0] + bwd[0], g_c), out(fwd[1] + bwd[1], g_l)


def s5_scan(lb_re, lb_im, bu_re, bu_im, s0):
    s0_re, s0_im = s0
    bu_re = bu_re.at[:, 0].add(lb_re * s0_re - lb_im * s0_im)
    bu_im = bu_im.at[:, 0].add(lb_re * s0_im + lb_im * s0_re)
    a_re = jnp.broadcast_to(lb_re, bu_re.shape)
    a_im = jnp.broadcast_to(lb_im, bu_im.shape)

    def combine(e1, e2):
        a1r, a1i, b1r, b1i = e1
        a2r, a2i, b2r, b2i = e2
        return (a1r * a2r - a1i * a2i, a1r * a2i + a1i * a2r,
                a2r * b1r - a2i * b1i + b2r, a2r * b1i + a2i * b1r + b2i)
    _, _, s_re, s_im = lax.associative_scan(combine, (a_re, a_im, bu_re, bu_im), axis=1)
    return (s_re, s_im), (s_re[:, -1], s_im[:, -1])


def s5_mixer(u_c, u_l, lam_re, lam_im, log_dt, b_re, b_im, c_re, c_im, d_skip, w_glu, b_glu):
    u_c, u_l = u_c.astype(jnp.float32), u_l.astype(jnp.float32)
    bsz = u_c.shape[0]
    y_c, y_l = d_skip * u_c, d_skip * u_l
    s0 = (jnp.zeros((bsz, S5_GROUPS, S5_STATE), jnp.float32),) * 2
    for d in range(2):
        lr, li = lam_re[d], lam_im[d]
        dt = jnp.exp(log_dt[d])[:, None]
        mag = jnp.exp(lr * dt)
        lb_re, lb_im = mag * jnp.cos(li * dt), mag * jnp.sin(li * dt)
        den = lr * lr + li * li
        f_re = ((lb_re - 1.0) * lr + lb_im * li) / den
        f_im = (lb_im * lr - (lb_re - 1.0) * li) / den
        bb_re = f_re[..., None] * b_re[d] - f_im[..., None] * b_im[d]
        bb_im = f_re[..., None] * b_im[d] + f_im[..., None] * b_re[d]

        def drive(u):
            ug = u.reshape(u.shape[:2] + (S5_GROUPS, S5_GROUP))
            return (jnp.einsum('gpi,blgi->blgp', bb_re, ug), jnp.einsum('gpi,blgi->blgp', bb_im, ug))

        st_c, st_l = two_stream_scan(functools.partial(s5_scan, lb_re, lb_im),
                                     drive(u_c), drive(u_l), s0, d == 1)

        def read(st):
            s_re, s_im = st
            y = jnp.einsum('gop,blgp->blgo', c_re[d], s_re) - jnp.einsum('gop,blgp->blgo', c_im[d], s_im)
            return y.reshape(y.shape[:2] + (MIX_W,))
        y_c = y_c + read(st_c)
        y_l = y_l + read(st_l)

    def glu(y):
        y = jax.nn.gelu(y)
        return y * jax.nn.sigmoid(y @ w_glu + b_glu)
    return glu(y_c), glu(y_l)


def gated_chunked(q, k, v, logf, s0):
    bsz, L, H, _ = q.shape
    n = L // HGRN_CHUNK
    qc, kc, vc, lf = (t.astype(jnp.float32).reshape(bsz, n, HGRN_CHUNK, H, -1) for t in (q, k, v, logf))
    b = jnp.cumsum(lf, axis=2)
    idx = jnp.arange(HGRN_CHUNK)
    mask = (idx[:, None] >= idx[None, :])[None, None, :, :, None, None]
    rel = b[:, :, :, None] - b[:, :, None, :]
    dec = jnp.where(mask, jnp.exp(jnp.minimum(rel, 0.0)), 0.0)
    scores = jnp.einsum('bnthd,bnshd,bntshd->bnhts', qc, kc, dec)
    o = jnp.einsum('bnhts,bnshv->bnthv', scores, vc)
    b_last = b[:, :, -1]
    kv = jnp.einsum('bnshd,bnshv->nbhdv', kc * jnp.exp(b_last[:, :, None] - b), vc)

    def step(S, inp):
        g_n, kv_n = inp
        return S * g_n[..., None] + kv_n, S
    S_last, S_prev = lax.scan(step, s0.astype(jnp.float32), (jnp.moveaxis(jnp.exp(b_last), 1, 0), kv))
    o = o + jnp.einsum('bnthd,nbhdv->bnthv', qc * jnp.exp(b), S_prev)
    return o.reshape(bsz, L, H, -1), S_last


def hgrn2_mixer(p_c, p_l, lb, f_bias, gn_g):
    def prep(p):
        q, zf, zb, i, g = split_cols(p, HGRN_SPLITS)
        dirs = []
        for d, z in enumerate((zf, zb)):
            f = lb + (1.0 - lb) * jax.nn.sigmoid((z + f_bias[d]).astype(jnp.float32))
            dirs.append((to_heads(1.0 - f), to_heads(jnp.log(f))))
        return to_heads(q), to_heads(i), g, dirs

    q_c, i_c, g_c, dirs_c = prep(p_c)
    q_l, i_l, g_l, dirs_l = prep(p_l)
    s0 = jnp.zeros((p_c.shape[0], N_HEADS, HEAD_DIM, HEAD_DIM), jnp.float32)
    o_c = o_l = 0.0
    for d in range(2):
        (k_c, lf_c), (k_l, lf_l) = dirs_c[d], dirs_l[d]
        y_c, y_l = two_stream_scan(gated_chunked, (q_c, k_c, i_c, lf_c), (q_l, k_l, i_l, lf_l), s0, d == 1)
        o_c, o_l = o_c + y_c, o_l + y_l

    def out(o, g):
        return head_norm(o, gn_g, GN_EPS, False) * jax.nn.silu(g)
    return out(o_c, g_c), out(o_l, g_l)


def gated_merge(h, ys, w_branch, w_gate, b_gate, w_out):
    bsz, L, _ = h.shape
    gates = jax.nn.sigmoid(h @ w_gate + b_gate).reshape(bsz, L, N_BRANCH, D_MODEL)
    branches = jnp.einsum('blnw,nwd->blnd', jnp.stack(ys, axis=2), w_branch)
    return jnp.einsum('blnd,blnd->bld', gates, branches) @ w_out


def setup_inputs(seed: int = 0) -> dict:
    key = jax.random.key(seed)
    ks = iter(jax.random.split(key, 48))

    def nrm(shape, scale=1.0):
        return scale * jax.random.normal(next(ks), shape, jnp.float32)

    def uni(shape, lo, hi):
        return jax.random.uniform(next(ks), shape, jnp.float32, lo, hi)

    Ld, G, P, W = DEPTH, S5_GROUPS, S5_STATE, MIX_W
    return {
        'x': nrm((BATCH, SEQ, D_MODEL)),
        'c': nrm((BATCH, D_MODEL)),
        'ctx': nrm((BATCH, CTX_LEN, D_MODEL)),
        'c_ctx': nrm((D_MODEL,)),
        'w_ada': nrm((Ld, D_MODEL, N_MOD * D_MODEL), D_MODEL ** -0.5),
        'b_ada': nrm((Ld, N_MOD * D_MODEL), 0.02),
        'ln_g': 1.0 + nrm((Ld, 3, D_MODEL), 0.02),
        'ln_b': nrm((Ld, 3, D_MODEL), 0.02),
        'ffn_w1': nrm((Ld, 2, D_MODEL, D_FF), D_MODEL ** -0.5),
        'ffn_w3': nrm((Ld, 2, D_MODEL, D_FF), D_MODEL ** -0.5),
        'ffn_w2': nrm((Ld, 2, D_FF, D_MODEL), BETA * D_FF ** -0.5),
        'w_in': nrm((Ld, D_MODEL, P_IN), D_MODEL ** -0.5),
        'rwkv_mu': uni((Ld, MIXER_SPLITS[0]), 0.0, 1.0),
        'rwkv_w0': uni((Ld, 2, W), -6.0, -1.0),
        'rwkv_w2': nrm((Ld, 2, RWKV_W_LORA, W), 0.1),
        'rwkv_a0': nrm((Ld, 2, W), 0.1),
        'rwkv_a2': nrm((Ld, 2, RWKV_A_LORA, W), 0.5 * RWKV_A_LORA ** -0.5),
        'rwkv_g2': nrm((Ld, RWKV_G_LORA, W), RWKV_G_LORA ** -0.5),
        'rwkv_kk': 0.85 + nrm((Ld, W), 0.02),
        'rwkv_ka': 1.0 + nrm((Ld, W), 0.02),
        'rwkv_rk': nrm((Ld, W), 0.1),
        'rwkv_gn': 1.0 + nrm((Ld, W), 0.02),
        'ret_gn': 1.0 + nrm((Ld, W), 0.02),
        's5_lam_re': -0.5 + nrm((Ld, 2, G, P), 0.01),
        's5_lam_im': jnp.pi * jnp.arange(P, dtype=jnp.float32) + nrm((Ld, 2, G, P), 0.01),
        's5_log_dt': uni((Ld, 2, G), float(np.log(0.001)), float(np.log(0.1))),
        's5_b_re': nrm((Ld, 2, G, P, S5_GROUP), (2.0 * S5_GROUP) ** -0.5),
        's5_b_im': nrm((Ld, 2, G, P, S5_GROUP), (2.0 * S5_GROUP) ** -0.5),
        's5_c_re': nrm((Ld, 2, G, S5_GROUP, P), (2.0 * P) ** -0.5),
        's5_c_im': nrm((Ld, 2, G, S5_GROUP, P), (2.0 * P) ** -0.5),
        's5_d': nrm((Ld, W)),
        's5_w_glu': nrm((Ld, W, W), W ** -0.5),
        's5_b_glu': nrm((Ld, W), 0.02),
        'hgrn_lb_logits': nrm((Ld, W)),
        'hgrn_f_bias': nrm((Ld, 2, W), 0.1),
        'hgrn_gn': 1.0 + nrm((Ld, W), 0.02),
        'w_branch': nrm((Ld, N_BRANCH, W, D_MODEL), BETA * W ** -0.5),
        'w_gate': nrm((Ld, D_MODEL, N_BRANCH * D_MODEL), D_MODEL ** -0.5),
        'b_gate': nrm((Ld, N_BRANCH * D_MODEL), 0.02),
        'w_out': nrm((Ld, D_MODEL, D_MODEL), BETA * D_MODEL ** -0.5),
    }


def reference(x, c, ctx, c_ctx, w_ada, b_ada, ln_g, ln_b, ffn_w1, ffn_w3, ffn_w2, w_in,
              rwkv_mu, rwkv_w0, rwkv_w2, rwkv_a0, rwkv_a2, rwkv_g2, rwkv_kk, rwkv_ka, rwkv_rk, rwkv_gn,
              ret_gn, s5_lam_re, s5_lam_im, s5_log_dt, s5_b_re, s5_b_im, s5_c_re, s5_c_im, s5_d,
              s5_w_glu, s5_b_glu, hgrn_lb_logits, hgrn_f_bias, hgrn_gn, w_branch, w_gate, b_gate, w_out):
    L = x.shape[1]
    rows_n = L // GRID_W
    rows = jnp.repeat(jnp.arange(rows_n, dtype=jnp.float32), GRID_W)
    cols = jnp.tile(jnp.arange(GRID_W, dtype=jnp.float32), rows_n)
    lb_soft = jax.nn.softmax(hgrn_lb_logits.astype(jnp.float32), axis=0)
    lower_bounds = jnp.cumsum(lb_soft, axis=0) - lb_soft[0]

    for l in range(DEPTH):
        last = l == DEPTH - 1
        m_l = jnp.split(jax.nn.silu(c) @ w_ada[l] + b_ada[l], N_MOD, axis=-1)
        m_c = jnp.split(jax.nn.silu(c_ctx)[None] @ w_ada[l] + b_ada[l], N_MOD, axis=-1)

        def ffn_sublayer(s, m, j):
            h = modulate(s, m[3 * j], m[3 * j + 1])
            upd = 0.5 * m[3 * j + 2][:, None] * swiglu(h, ffn_w1[l, j // 2], ffn_w3[l, j // 2], ffn_w2[l, j // 2])
            return layer_norm(ALPHA * s + upd, ln_g[l, j], ln_b[l, j])

        x = ffn_sublayer(x, m_l, 0)
        ctx = ffn_sublayer(ctx, m_c, 0)

        h_l = modulate(x, m_l[3], m_l[4])
        h_c = modulate(ctx, m_c[3], m_c[4])
        pa_c, pb_c, pc_c, pd_c = split_cols(h_c @ w_in[l], MIXER_SPLITS)
        pa_l, pb_l, pc_l, pd_l = split_cols(h_l @ w_in[l], MIXER_SPLITS)
        ya_c, ya_l = rwkv7_mixer(pa_c, pa_l, rwkv_mu[l], rwkv_w0[l], rwkv_w2[l], rwkv_a0[l], rwkv_a2[l],
                                 rwkv_g2[l], rwkv_kk[l], rwkv_ka[l], rwkv_rk[l], rwkv_gn[l])
        yb_c, yb_l = retention_mixer(pb_c, pb_l, ret_gn[l], rows, cols)
        yc_c, yc_l = s5_mixer(pc_c, pc_l, s5_lam_re[l], s5_lam_im[l], s5_log_dt[l], s5_b_re[l], s5_b_im[l],
                              s5_c_re[l], s5_c_im[l], s5_d[l], s5_w_glu[l], s5_b_glu[l])
        yd_c, yd_l = hgrn2_mixer(pd_c, pd_l, lower_bounds[l], hgrn_f_bias[l], hgrn_gn[l])

        mix_l = gated_merge(h_l, (ya_l, yb_l, yc_l, yd_l), w_branch[l], w_gate[l], b_gate[l], w_out[l])
        x = layer_norm(ALPHA * x + m_l[5][:, None] * mix_l, ln_g[l, 1], ln_b[l, 1])
        x = ffn_sublayer(x, m_l, 2)

        if not last:
            mix_c = gated_merge(h_c, (ya_c, yb_c, yc_c, yd_c), w_branch[l], w_gate[l], b_gate[l], w_out[l])
            ctx = layer_norm(ALPHA * ctx + m_c[5][:, None] * mix_c, ln_g[l, 1], ln_b[l, 1])
            ctx = ffn_sublayer(ctx, m_c, 2)
    return x
```

```python
import contextlib
import numpy as np
import concourse.bass as bass
import concourse.mybir as mybir
from concourse.bass_utils import run_bass_kernel_spmd

F32 = mybir.dt.float32
ALU = mybir.AluOpType
AF = mybir.ActivationFunctionType

D = 1024
DC = 8
CTX = 256
SEQ = 2048
TT = CTX + SEQ
NB = 2
DFF = 2816
FC = 22
NL = 4
PIN = 3616
PINC = 29
ALPHA = float(8.0 ** 0.25)
LN_EPS = 1e-5
NCORES = 8
NDS = 24
STORE_DEFER = 4


class Prog:
    def __init__(self, nc, es):
        self.nc = nc
        self.es = es
        self.eng = dict(pe=nc.tensor, dve=nc.vector, act=nc.scalar, pool=nc.gpsimd, sp=nc.sync)
        self.sem = {k: es.enter_context(nc.semaphore("s_" + k)) for k in ("pe", "dve", "act", "pool")}
        self.cnt = {k: 0 for k in self.sem}
        self.dsem = [es.enter_context(nc.semaphore("d%d" % i)) for i in range(NDS)]
        self.dval = [0] * NDS
        self.dnext = 0
        self.seen = {e: {} for e in self.eng}
        self.lw = {}
        self.rd = {}
        self.banks = [es.enter_context(nc.psum_tensor("bank%d" % i, [128, 512], F32)) for i in range(8)]
        self.bnext = 0
        self.scopes = [es]

    def sb(self, name, shape):
        self.uid = getattr(self, "uid", 0) + 1
        return self.scopes[-1].enter_context(self.nc.sbuf_tensor("sb%d_%s" % (self.uid, name), list(shape), F32))

    @contextlib.contextmanager
    def phase(self):
        st = contextlib.ExitStack()
        self.scopes.append(st)
        try:
            yield
        finally:
            self.barrier()
            self.scopes.pop()
            st.close()

    def barrier(self):
        self.flush_stores()
        for e in self.eng:
            for e2 in self.sem:
                if e2 != e and self.cnt[e2] > 0:
                    self._wait(e, (e2, self.cnt[e2]))
            for j in range(NDS):
                if self.dval[j] > 0:
                    self._wait(e, (("d", j), self.dval[j]))

    def bank(self):
        fr = getattr(self, "free_banks", None)
        if fr is None:
            fr = self.free_banks = list(range(8))
        self.bnext = (self.bnext + 1) % len(fr)
        i = fr[self.bnext]
        return self.banks[i], ("bank", i)

    def reserve(self, n):
        self.bank()
        out = [self.free_banks.pop() for _ in range(n)]
        return [(self.banks[i], ("bank", i)) for i in out]

    def release(self):
        self.free_banks = list(range(8))

    def _semof(self, k):
        return self.dsem[k[1]] if isinstance(k, tuple) else self.sem[k]

    def _wait(self, e, dep):
        k, v = dep
        if self.seen[e].get(k, 0) >= v:
            return
        self.seen[e][k] = v
        self.eng[e].wait_ge(self._semof(k), v)

    def _deps(self, e, reads, writes):
        deps = set()
        for r in reads:
            if r in self.lw:
                deps.add(self.lw[r])
        for w in writes:
            if w in self.lw:
                deps.add(self.lw[w])
            for d in self.rd.get(w, ()):
                deps.add(d)
        for d in deps:
            if e == "pe" and d[0] == "pe":
                continue
            self._wait(e, d)

    def _commit(self, me, reads, writes):
        for r in reads:
            lst = self.rd.setdefault(r, [])
            lst[:] = [d for d in lst if d[0] != me[0]]
            lst.append(me)
        for w in writes:
            self.lw[w] = me
            self.rd[w] = []

    def _flush_conflicts(self, reads, writes):
        pend = getattr(self, "pending", None)
        if not pend:
            return
        ws = set(writes)
        rs = set(reads) | ws
        for (o_, i_, r_, w_, age) in pend:
            if ws.intersection(r_) or rs.intersection(w_):
                self.flush_stores()
                return

    def flush_stores(self, min_age=0):
        pend = getattr(self, "pending", None)
        if not pend:
            return
        keep = []
        self.pending = []
        for it in pend:
            if it[4] >= min_age and not keep:
                self._dma_now("sp", it[0], it[1], it[2], it[3])
            else:
                keep.append(it)
        self.pending = keep + self.pending

    def op(self, e, fn, reads, writes):
        self._flush_conflicts(reads, writes)
        self._deps(e, reads, writes)
        ins = fn()
        self.cnt[e] += 1
        ins.then_inc(self.sem[e], 1)
        self._commit((e, self.cnt[e]), reads, writes)

    def dma(self, q, out, in_, reads, writes):
        if not hasattr(self, "pending"):
            self.pending = []
        self._flush_conflicts(reads, writes)
        if q == "pool":
            self.pending.append([out, in_, list(reads), list(writes), 0])
            return
        self._dma_now(q, out, in_, reads, writes)
        for it in self.pending:
            it[4] += 1
        self.flush_stores(min_age=STORE_DEFER)

    def _dma_now(self, q, out, in_, reads, writes):
        j = self.dnext
        self.dnext = (j + 1) % NDS
        if self.dval[j] > 0:
            self._wait(q, (("d", j), self.dval[j]))
        self._deps(q, reads, writes)
        self.dval[j] += 16
        self.eng[q].dma_start(out=out, in_=in_).then_inc(self.dsem[j], 16)
        self._commit((("d", j), self.dval[j]), reads, writes)

    def finish(self):
        self.flush_stores()
        for j in range(NDS):
            if self.dval[j] > 0:
                self.nc.sync.wait_ge(self.dsem[j], self.dval[j])
        for e in ("pe", "dve", "act", "pool"):
            if self.cnt[e] > 0:
                self.nc.sync.wait_ge(self.sem[e], self.cnt[e])


class Pool_:
    def __init__(self, P, name, shape, n):
        self.t = [P.sb("%s_%d" % (name, i), shape) for i in range(n)]
        self.k = [(name, i) for i in range(n)]
        self.i = 0

    def next(self):
        i = self.i
        self.i = (i + 1) % len(self.t)
        return self.t[i], self.k[i]


def bc(ap, n):
    a = ap.ap
    return bass.AP(ap.tensor, ap.offset, [list(a[0]), [0, n]])


def fm(v):
    v = np.asarray(v, np.float32)
    n = v.shape[-1] // 128
    v = v.reshape(v.shape[:-1] + (n, 128))
    return np.moveaxis(v, -1, 0)


class Pack:
    def __init__(self):
        self.cols = []
        self.off = {}
        self.n = 0

    def add(self, name, arr):
        arr = np.ascontiguousarray(arr, np.float32).reshape(128, -1)
        self.off[name] = (self.n, arr.shape[1])
        self.cols.append(arr)
        self.n += arr.shape[1]

    def build(self):
        return np.ascontiguousarray(np.concatenate(self.cols, axis=1))


def pack_layout():
    pk = Pack()
    z = lambda *s: np.zeros(s, np.float32)
    pk.add("b_ada", z(128, NL * 72))
    pk.add("ln_g", z(128, NL * 3 * 8))
    pk.add("ln_b", z(128, NL * 3 * 8))
    pk.add("b_gate", z(128, NL * 32))
    pk.add("hg_lbl", z(128, 2 * NL))
    pk.add("hg_fb", z(128, NL * 2 * 2))
    pk.add("hg_gn", z(128, NL * 2))
    pk.add("ret_gn", z(128, NL * 2))
    pk.add("ret_gam", z(128, 2))
    pk.add("ret_lg", z(128, 2))
    pk.add("rw_mu", z(128, NL * 9))
    pk.add("rw_w0", z(128, NL * 4))
    pk.add("rw_a0", z(128, NL * 4))
    pk.add("rw_kk", z(128, NL * 2))
    pk.add("rw_ka", z(128, NL * 2))
    pk.add("rw_rk", z(128, NL * 2))
    pk.add("rw_gn", z(128, NL * 2))
    pk.add("s5_lr", z(128, NL * 16))
    pk.add("s5_li", z(128, NL * 16))
    pk.add("s5_ldt", z(128, NL * 16))
    pk.add("s5_d", z(128, NL * 2))
    pk.add("s5_bglu", z(128, NL * 2))
    return pk


def pack_params(inp):
    pk = Pack()
    pk.add("b_ada", fm(inp["b_ada"]).reshape(128, -1))
    pk.add("ln_g", fm(inp["ln_g"]).reshape(128, -1))
    pk.add("ln_b", fm(inp["ln_b"]).reshape(128, -1))
    pk.add("b_gate", fm(inp["b_gate"]).reshape(128, -1))
    pk.add("hg_lbl", fm(inp["hgrn_lb_logits"]).transpose(0, 2, 1))
    pk.add("hg_fb", fm(inp["hgrn_f_bias"]))
    pk.add("hg_gn", fm(inp["hgrn_gn"]))
    pk.add("ret_gn", fm(inp["ret_gn"]))
    gam = 1.0 - 2.0 ** (-5.0 - np.arange(4, dtype=np.float32))
    pk.add("ret_gam", fm(np.repeat(gam, 64)[None])[:, 0, :])
    pk.add("ret_lg", fm(np.repeat(np.log(gam).astype(np.float32), 64)[None])[:, 0, :])
    mu = np.zeros((NL, 9, 128), np.float32)
    mu[:, :8] = inp["rwkv_mu"][:, :1024].reshape(NL, 8, 128)
    mu[:, 8, 64:96] = inp["rwkv_mu"][:, 1024:1056]
    pk.add("rw_mu", mu.transpose(2, 0, 1))
    pk.add("rw_w0", fm(inp["rwkv_w0"]))
    pk.add("rw_a0", fm(inp["rwkv_a0"]))
    pk.add("rw_kk", fm(inp["rwkv_kk"]))
    pk.add("rw_ka", fm(inp["rwkv_ka"]))
    pk.add("rw_rk", fm(inp["rwkv_rk"]))
    pk.add("rw_gn", fm(inp["rwkv_gn"]))
    pk.add("s5_lr", fm(inp["s5_lam_re"].reshape(NL, 2, 1024)))
    pk.add("s5_li", fm(inp["s5_lam_im"].reshape(NL, 2, 1024)))
    pk.add("s5_ldt", fm(np.repeat(inp["s5_log_dt"], 64, axis=-1)))
    pk.add("s5_d", fm(inp["s5_d"]))
    pk.add("s5_bglu", fm(inp["s5_b_glu"]))
    return pk


def host_weights(inp):
    w = {}
    f = lambda a: np.ascontiguousarray(a, np.float32)
    w1 = inp["ffn_w1"].reshape(NL, 2, 8, 128, FC, 128).transpose(0, 1, 4, 3, 2, 5)
    w3 = inp["ffn_w3"].reshape(NL, 2, 8, 128, FC, 128).transpose(0, 1, 4, 3, 2, 5)
    w["w13"] = f(np.stack([w1, w3], axis=4))
    w["w2"] = f(inp["ffn_w2"].reshape(NL, 2, FC, 128, 8, 128).transpose(0, 1, 4, 3, 2, 5))
    win = np.zeros((NL, D, PINC * 128), np.float32)
    win[:, :, :PIN] = inp["w_in"]
    w["win"] = f(win.reshape(NL, 8, 128, PINC, 128).transpose(0, 3, 2, 1, 4))
    w["wg"] = f(inp["w_gate"].reshape(NL, 8, 128, 32, 128).transpose(0, 3, 2, 1, 4))
    w["wb"] = f(inp["w_branch"].reshape(NL, 4, 2, 128, 8, 128).transpose(0, 1, 4, 3, 2, 5))
    w["wo"] = f(inp["w_out"].reshape(NL, 8, 128, 8, 128).transpose(0, 3, 2, 1, 4))
    w["wada"] = f(inp["w_ada"])
    B = np.zeros((NL, 128, 2, 8, 2, 128), np.float32)
    C = np.zeros((NL, 128, 2, 8, 2, 64), np.float32)
    for ri, (bn, cn) in enumerate((("s5_b_re", "s5_c_re"), ("s5_b_im", "s5_c_im"))):
        bsrc, csrc = inp[bn], inp[cn]
        for j in range(8):
            base = 32 * (j % 4)
            for gg in range(2):
                g = 2 * j + gg
                B[:, base + gg * 16:base + gg * 16 + 16, :, j, ri, gg * 64:(gg + 1) * 64] = bsrc[:, :, g].transpose(0, 3, 1, 2)
                co = 32 * (j % 2) + gg * 16
                C[:, gg * 64:(gg + 1) * 64, :, j, ri, co:co + 16] = csrc[:, :, g].transpose(0, 3, 1, 2)
    rwW = np.zeros((NL, 128, 2, 256), np.float32)
    rwW[:, 0:64] = inp["rwkv_w2"].transpose(0, 2, 1, 3)
    rwW[:, 64:128] = inp["rwkv_a2"].transpose(0, 2, 1, 3)
    w["rwW"] = rwW
    rwG = np.zeros((NL, 128, 2, 256), np.float32)
    rwG[:, :, 0] = inp["rwkv_g2"][:, 0:128]
    rwG[:, 64:96, 1] = inp["rwkv_g2"][:, 128:160]
    w["rwG"] = rwG
    p = np.arange(128)[:, None]
    fr = np.arange(128)[None, :]
    lo_s, up_s, lo_i, up_i = (fr < p), (fr > p), (fr <= p), (fr >= p)
    M = np.zeros((128, 2, 1280), np.float32)
    M[:, 0] = np.concatenate([-1.0 * lo_s, -1.0 * lo_s, -1.0 * up_s, -1.0 * up_s, up_s, up_s, up_i, up_i, -1.0 * up_i, -1.0 * up_i], axis=1)
    M[:, 1] = np.concatenate([-1.0 * up_s, -1.0 * up_s, -1.0 * lo_s, -1.0 * lo_s, lo_s, lo_s, lo_i, lo_i, -1.0 * lo_i, -1.0 * lo_i], axis=1)
    w["rwM"] = M
    tt = np.arange(128)[:, None]
    ss = np.arange(128)[None, :]
    M2 = np.zeros((128, 2, 4, 256), np.float32)
    for d in range(2):
        before = (ss < tt) if d == 0 else (ss > tt)
        pats = [before & ((tt // 16) == (ss // 16))]
        for kk_ in range(3):
            m = 16 * 2 ** kk_
            same = (tt // (2 * m)) == (ss // (2 * m))
            if d == 0:
                pats.append(same & ((tt % (2 * m)) >= m) & ((ss % (2 * m)) < m))
            else:
                pats.append(same & ((tt % (2 * m)) < m) & ((ss % (2 * m)) >= m))
        for i, pm in enumerate(pats):
            M2[:, d, i, 0:128] = -1.0 * pm
            M2[:, d, i, 128:256] = -1.0 * pm.T
    w["rwM2"] = M2
    w["ident"] = np.eye(128, dtype=np.float32)
    w["s5B"] = B
    w["s5C"] = C
    w["s5W"] = f(inp["s5_w_glu"].reshape(NL, 2, 128, 256).transpose(0, 2, 1, 3))
    return w


class Ctx:
    pass


def build_program(cfg):
    nc = bass.Bass("TRN2", target_bir_lowering=False)
    es = contextlib.ExitStack()
    G = Ctx()
    G.nc = nc
    G.cfg = cfg
    G.dbgdone = set()
    dump = cfg.get("dump", set())
    WL = 1 if cfg.get("small_w") else NL

    ext_in = cfg.get("ext_in", set())

    def dram(name, shape, kind=None):
        if kind is None:
            kind = "ExternalOutput" if name in dump else ("ExternalInput" if name in ext_in else "Internal")
        return nc.dram_tensor(name, list(shape), F32, kind=kind).ap()

    pkl = pack_layout()
    G.off = pkl.off
    G.xin = dram("xin", [NB, D, TT], "ExternalInput")
    G.cT = dram("cT", [128, 8, 3], "ExternalInput")
    G.pp_d = dram("pp", [128, pkl.n], "ExternalInput")
    G.wada = dram("wada", [WL, D, 9 * D], "ExternalInput")
    G.w13 = dram("w13", [WL, 2, FC, 128, 2, 8, 128], "ExternalInput")
    G.w2 = dram("w2", [WL, 2, 8, 128, FC, 128], "ExternalInput")
    G.win = dram("win", [WL, PINC, 128, 8, 128], "ExternalInput")
    G.wg = dram("wg", [WL, 32, 128, 8, 128], "ExternalInput")
    G.wb = dram("wb", [WL, 4, 8, 128, 2, 128], "ExternalInput")
    G.wo = dram("wo", [WL, 8, 128, 8, 128], "ExternalInput")
    G.rwW = dram("rwW", [NL, 128, 2, 256], "ExternalInput")
    G.rwG = dram("rwG", [NL, 128, 2, 256], "ExternalInput")
    G.rwM = dram("rwM", [128, 2, 1280], "ExternalInput")
    G.ident_d = dram("ident", [128, 128], "ExternalInput")
    G.rwM2 = dram("rwM2", [128, 2, 4, 256], "ExternalInput")
    G.s5B = dram("s5B", [NL, 128, 2, 8, 2, 128], "ExternalInput")
    G.s5C = dram("s5C", [NL, 128, 2, 8, 2, 64], "ExternalInput")
    G.s5W = dram("s5W", [NL, 128, 2, 256], "ExternalInput")
    G.rope_d = dram("rope", [128, 2, SEQ], "ExternalInput")
    G.perm_d = dram("perm", [128, 128], "ExternalInput")
    G.out = dram("out", [NB, D, SEQ], "ExternalOutput")
    G.xs = dram("xs", [NB, D, TT])
    G.pT = dram("pT", [NB, PINC * 128, TT])
    G.yT = dram("yT", [NB, D, TT])

    with es:
        P = Prog(nc, es)
        G.P = P
        setup_tiles(G)
        emit_all(G, cfg)
        P.finish()
    return nc


def setup_tiles(G):
    P = G.P
    nc = G.nc
    G.pp = P.sb("pp", [128, pack_layout().n])
    P.dma("sp", out=G.pp[:], in_=G.pp_d, reads=[], writes=["pp"])
    G.scT = P.sb("scT", [128, 8, 3])
    P.dma("sp", out=G.scT[:], in_=G.cT, reads=[], writes=["scT"])
    P.op("act", lambda: nc.scalar.activation(out=G.scT[:], in_=G.scT[:], func=AF.Silu), ["scT"], ["scT"])
    G.modTs = [P.sb("modT%d" % l_, [128, 72, 3]) for l_ in range(NL)]
    G.modT = G.modTs[0]
    G.onesM = P.sb("onesM", [128, 128])
    P.op("dve", lambda: nc.vector.memset(G.onesM[:], 1.0 / D), [], ["onesM"])


def dense_tiles(G):
    P = G.P
    G.sT = Pool_(P, "sT", [128, 8, 512], 2)
    G.hT = Pool_(P, "hT", [128, 8, 512], 1)
    G.uT = Pool_(P, "uT", [128, 8, 512], 1)
    G.qT = Pool_(P, "qT", [128, 8, 512], 1)
    G.gT = P.sb("gT", [128, FC, 512])
    G.w13t = Pool_(P, "w13t", [128, 2, 8, 128], 3)
    G.w2t = Pool_(P, "w2t", [128, FC, 128], 2)
    G.wat = Pool_(P, "wat", [128, 8, 512], 1)
    G.tmp = Pool_(P, "tmp", [128, 512], 4)
    G.stat = Pool_(P, "stat", [128, 512], 2)


def merge_tiles(G):
    P = G.P
    G.sT = Pool_(P, "sT", [128, 8, 512], 2)
    G.hT = Pool_(P, "hT", [128, 8, 512], 1)
    G.uT = Pool_(P, "uT", [128, 8, 512], 1)
    G.qT = Pool_(P, "qT", [128, 8, 512], 2)
    G.aT = Pool_(P, "aT", [128, 8, 512], 1)
    G.w4t = Pool_(P, "w4t", [128, 8, 128], 4)
    G.wbt = Pool_(P, "wbt", [128, 2, 128], 3)
    G.tmp = Pool_(P, "tmp", [128, 512], 6)
    G.stat = Pool_(P, "stat", [128, 512], 2)


def ppcol(G, name, idx):
    o, n = G.off[name]
    return G.pp[:, o + idx:o + idx + 1]


def ada_gen(G, l, bufs, CB):
    P, nc = G.P, G.nc
    modT = G.modTs[l]
    nblk = 9 * D // CB
    nq = CB // 128

    def load(blk):
        wt, wk = bufs[blk % len(bufs)]
        src = G.wada[l, :, blk * CB:(blk + 1) * CB].rearrange("(c p) n -> p c n", p=128)
        P.dma("sp", out=wt[:, :, :CB], in_=src, reads=[], writes=[wk])
    load(0)
    for blk in range(nblk):
        if blk + 1 < nblk and len(bufs) > 1:
            load(blk + 1)
        wt, wk = bufs[blk % len(bufs)]
        ps, pk = P.bank()
        for q in range(nq):
            for c in range(8):
                P.op("pe", lambda: nc.tensor.matmul(ps[:, q * 4:q * 4 + 3], lhsT=wt[:, c, q * 128:(q + 1) * 128],
                                                    rhs=G.scT[:, c, :], start=(c == 0), stop=(c == 7)),
                     [wk, "scT"], [pk])
        for q in range(nq):
            j = blk * nq + q
            P.op("dve", lambda: nc.vector.tensor_scalar(out=modT[:, j, :], in0=ps[:, q * 4:q * 4 + 3],
                                                        scalar1=ppcol(G, "b_ada", l * 72 + j), scalar2=None,
                                                        op0=ALU.add), [pk, "pp"], ["modT"])
        if blk + 1 < nblk and len(bufs) == 1:
            load(blk + 1)
        yield
    for i in (1, 4, 7):
        P.op("dve", lambda: nc.vector.tensor_scalar(out=modT[:, i * 8:(i + 1) * 8, :], in0=modT[:, i * 8:(i + 1) * 8, :],
                                                    scalar1=1.0, scalar2=None, op0=ALU.add), ["modT"], ["modT"])
    for i in (2, 8):
        P.op("dve", lambda: nc.vector.tensor_scalar(out=modT[:, i * 8:(i + 1) * 8, :], in0=modT[:, i * 8:(i + 1) * 8, :],
                                                    scalar1=0.5, scalar2=None, op0=ALU.mult), ["modT"], ["modT"])
    yield


def mod(G, i, c, r):
    return G.modT[:, i * 8 + c, r:r + 1]


def load_block(G, src, skey, TB):
    P = G.P
    sT, sk = G.sT.next()
    P.dma("sp", out=sT[:, :, :TB], in_=src.rearrange("(c p) t -> p c t", p=128), reads=[skey], writes=[sk])
    return sT, sk


def modulate(G, sT, sk, TB, i_shift, r):
    P, nc = G.P, G.nc
    hT, hk = G.hT.next()
    for c in range(8):
        P.op("act", lambda: nc.scalar.activation(out=hT[:, c, :TB], in_=sT[:, c, :TB], func=AF.Identity,
                                                 bias=mod(G, i_shift, c, r), scale=mod(G, i_shift + 1, c, r)),
             [sk, "modT"], [hk])
    return hT, hk


def epilogue(G, l, jln, uT, uk, TB, dst, dkey):
    P, nc = G.P, G.nc
    mean, mk = P.bank()
    for c in range(8):
        P.op("pe", lambda: nc.tensor.matmul(mean[:, :TB], lhsT=G.onesM[:], rhs=uT[:, c, :TB], start=(c == 0), stop=(c == 7)),
             [uk, "onesM"], [mk])
    qT, qk = G.qT.next()
    for c in range(8):
        P.op("dve", lambda: nc.vector.tensor_tensor(out=uT[:, c, :TB], in0=uT[:, c, :TB], in1=mean[:, :TB], op=ALU.subtract),
             [uk, mk], [uk])
        P.op("act", lambda: nc.scalar.activation(out=qT[:, c, :TB], in_=uT[:, c, :TB], func=AF.Square), [uk], [qk])
    var, vk = P.bank()
    for c in range(8):
        P.op("pe", lambda: nc.tensor.matmul(var[:, :TB], lhsT=G.onesM[:], rhs=qT[:, c, :TB], start=(c == 0), stop=(c == 7)),
             [qk, "onesM"], [vk])
    rs, rk = G.stat.next()
    P.op("act", lambda: nc.scalar.activation(out=rs[:, :TB], in_=var[:, :TB], func=AF.Sqrt, bias=G.epsc[:, 0:1]), [vk, "epsc"], [rk])
    P.op("dve", lambda: nc.vector.reciprocal(out=rs[:, :TB], in_=rs[:, :TB]), [rk], [rk])
    for c in range(8):
        P.op("dve", lambda: nc.vector.tensor_tensor(out=uT[:, c, :TB], in0=uT[:, c, :TB], in1=rs[:, :TB], op=ALU.mult),
             [uk, rk], [uk])
        P.op("act", lambda: nc.scalar.activation(out=uT[:, c, :TB], in_=uT[:, c, :TB], func=AF.Identity,
                                                 bias=ppcol(G, "ln_b", (l * 3 + jln) * 8 + c),
                                                 scale=ppcol(G, "ln_g", (l * 3 + jln) * 8 + c)), [uk, "pp"], [uk])
    P.dma("pool", out=dst.rearrange("(c p) t -> p c t", p=128), in_=uT[:, :, :TB], reads=[uk], writes=[dkey])


def emit_ffn_block(G, l, jln, src, skey, dst, dkey, TB, r):
    P, nc = G.P, G.nc
    jj = jln // 2
    i0 = 3 * jln
    sT, sk = load_block(G, src, skey, TB)
    hT, hk = modulate(G, sT, sk, TB, i0, r)
    for m in range(FC):
        wt, wk = G.w13t.next()
        P.dma("sp", out=wt[:], in_=G.w13[l, jj, m], reads=[], writes=[wk])
        pa, pak = P.bank()
        pb, pbk = P.bank()
        for c in range(8):
            P.op("pe", lambda: nc.tensor.matmul(pa[:, :TB], lhsT=wt[:, 0, c, :], rhs=hT[:, c, :TB], start=(c == 0), stop=(c == 7)),
                 [wk, hk], [pak])
        for c in range(8):
            P.op("pe", lambda: nc.tensor.matmul(pb[:, :TB], lhsT=wt[:, 1, c, :], rhs=hT[:, c, :TB], start=(c == 0), stop=(c == 7)),
                 [wk, hk], [pbk])
        tm, tk = G.tmp.next()
        P.op("act", lambda: nc.scalar.activation(out=tm[:, :TB], in_=pa[:, :TB], func=AF.Silu), [pak], [tk])
        P.op("dve", lambda: nc.vector.tensor_tensor(out=G.gT[:, m, :TB], in0=tm[:, :TB], in1=pb[:, :TB], op=ALU.mult),
             [tk, pbk], [("gT", m)])
    uT, uk = G.uT.next()
    gkeys = [("gT", m) for m in range(FC)]
    for n in range(8):
        wt, wk = G.w2t.next()
        P.dma("sp", out=wt[:], in_=G.w2[l, jj, n], reads=[], writes=[wk])
        py, pyk = P.bank()
        for m in range(FC):
            P.op("pe", lambda: nc.tensor.matmul(py[:, :TB], lhsT=wt[:, m, :], rhs=G.gT[:, m, :TB], start=(m == 0), stop=(m == FC - 1)),
                 [wk] + (gkeys if m == 0 else []), [pyk])
        P.op("act", lambda: nc.scalar.activation(out=uT[:, n, :TB], in_=py[:, :TB], func=AF.Identity, scale=mod(G, i0 + 2, n, r)),
             [pyk, "modT"], [uk])
        P.op("dve", lambda: nc.vector.scalar_tensor_tensor(out=uT[:, n, :TB], in0=sT[:, n, :TB], scalar=ALPHA, in1=uT[:, n, :TB],
                                                           op0=ALU.mult, op1=ALU.add), [sk, uk], [uk])
    epilogue(G, l, jln, uT, uk, TB, dst, dkey)


def emit_mixin_block(G, l, b, bi, t0, TB, r):
    P, nc = G.P, G.nc
    sT, sk = load_block(G, G.xs[b, :, t0:t0 + TB], ("xs", b, bi), TB)
    hT, hk = modulate(G, sT, sk, TB, 3, r)
    for m in range(PINC):
        wt, wk = G.w4t.next()
        P.dma("sp", out=wt[:], in_=G.win[l, m], reads=[], writes=[wk])
        ps, pk = P.bank()
        for c in range(8):
            P.op("pe", lambda: nc.tensor.matmul(ps[:, :TB], lhsT=wt[:, c, :], rhs=hT[:, c, :TB], start=(c == 0), stop=(c == 7)),
                 [wk, hk], [pk])
        tm, tk = G.tmp.next()
        if m % 2 == 0:
            P.op("act", lambda: nc.scalar.copy(out=tm[:, :TB], in_=ps[:, :TB]), [pk], [tk])
        else:
            P.op("dve", lambda: nc.vector.tensor_copy(out=tm[:, :TB], in_=ps[:, :TB]), [pk], [tk])
        P.dma("pool", out=G.pT[b, m * 128:(m + 1) * 128, t0:t0 + TB], in_=tm[:, :TB], reads=[tk], writes=[("pT", b, m, bi)])


def emit_merge_block(G, l, b, bi, t0, TB, r):
    P, nc = G.P, G.nc
    sT, sk = load_block(G, G.xs[b, :, t0:t0 + TB], ("xs", b, bi), TB)
    hT, hk = modulate(G, sT, sk, TB, 3, r)
    yt, yk = G.qT.next()
    P.dma("sp", out=yt[:, :, :TB], in_=G.yT[b, :, t0:t0 + TB].rearrange("(c p) t -> p c t", p=128), reads=[("yT", b)], writes=[yk])
    aT, ak = G.aT.next()
    for n in range(8):
        for br in range(4):
            wt, wk = G.w4t.next()
            P.dma("sp", out=wt[:], in_=G.wg[l, br * 8 + n], reads=[], writes=[wk])
            pg, pgk = P.bank()
            for c in range(8):
                P.op("pe", lambda: nc.tensor.matmul(pg[:, :TB], lhsT=wt[:, c, :], rhs=hT[:, c, :TB], start=(c == 0), stop=(c == 7)),
                     [wk, hk], [pgk])
            gt, gk = G.tmp.next()
            P.op("act", lambda: nc.scalar.activation(out=gt[:, :TB], in_=pg[:, :TB], func=AF.Sigmoid,
                                                     bias=ppcol(G, "b_gate", l * 32 + br * 8 + n)), [pgk, "pp"], [gk])
            wb, wbk = G.wbt.next()
            P.dma("sp", out=wb[:], in_=G.wb[l, br, n], reads=[], writes=[wbk])
            pb, pbk = P.bank()
            for kc in range(2):
                P.op("pe", lambda: nc.tensor.matmul(pb[:, :TB], lhsT=wb[:, kc, :], rhs=yt[:, br * 2 + kc, :TB], start=(kc == 0), stop=(kc == 1)),
                     [wbk, yk], [pbk])
            if br == 0:
                P.op("dve", lambda: nc.vector.tensor_tensor(out=aT[:, n, :TB], in0=gt[:, :TB], in1=pb[:, :TB], op=ALU.mult), [gk, pbk], [ak])
            else:
                P.op("dve", lambda: nc.vector.tensor_tensor(out=gt[:, :TB], in0=gt[:, :TB], in1=pb[:, :TB], op=ALU.mult), [gk, pbk], [gk])
                P.op("pool", lambda: nc.gpsimd.tensor_tensor(out=aT[:, n, :TB], in0=aT[:, n, :TB], in1=gt[:, :TB], op=ALU.add), [gk, ak], [ak])
    uT, uk = G.uT.next()
    for n in range(8):
        wt, wk = G.w4t.next()
        P.dma("sp", out=wt[:], in_=G.wo[l, n], reads=[], writes=[wk])
        po, pok = P.bank()
        for c in range(8):
            P.op("pe", lambda: nc.tensor.matmul(po[:, :TB], lhsT=wt[:, c, :], rhs=aT[:, c, :TB], start=(c == 0), stop=(c == 7)),
                 [wk, ak], [pok])
        P.op("act", lambda: nc.scalar.activation(out=uT[:, n, :TB], in_=po[:, :TB], func=AF.Identity, scale=mod(G, 5, n, r)),
             [pok, "modT"], [uk])
        P.op("dve", lambda: nc.vector.scalar_tensor_tensor(out=uT[:, n, :TB], in0=sT[:, n, :TB], scalar=ALPHA, in1=uT[:, n, :TB],
                                                           op0=ALU.mult, op1=ALU.add), [sk, uk], [uk])
    epilogue(G, l, 1, uT, uk, TB, G.xs[b, :, t0:t0 + TB], ("xs", b, bi))


def emit_blocks(G, fn, l, skip_ctx=False):
    for b in range(NB):
        for bi, (t0, TB) in enumerate(BLOCKS):
            if bi == 0 and skip_ctx:
                continue
            fn(G, l, b, bi, t0, TB, 2 if bi == 0 else b)


BLOCKS = [(0, 256)] + [(CTX + i * 512, 512) for i in range(4)]


def emit_ffn(G, l, jln, src_t, src_name, dst_t, dst_name, skip_ctx=False, final=False):
    for b in range(NB):
        for bi, (t0, TB) in enumerate(BLOCKS):
            if bi == 0 and skip_ctx:
                continue
            r = 2 if bi == 0 else b
            src = src_t[b, :, t0:t0 + TB]
            if final:
                dst = dst_t[b, :, t0 - CTX:t0 - CTX + TB]
            else:
                dst = dst_t[b, :, t0:t0 + TB]
            emit_ffn_block(G, l, jln, src, (src_name, b, bi), dst, (dst_name, b, bi), TB, r)


def emit_all(G, cfg):
    P, nc = G.P, G.nc
    setup_consts(G)
    phases = cfg.get("phases", {"ada", "ffn0", "mixin", "mixers", "merge", "ffn2"})
    for l in cfg.get("layers", range(NL)):
        last = l == NL - 1
        G.modT = G.modTs[l]
        with P.phase():
            dense_tiles(G)
            if "ada" in phases and (l == 0 or not G.cfg.get("ada_ahead", False) or l - 1 not in cfg.get("layers", range(NL)) or "mixers" not in phases):
                bufs = [(G.wat.t[0], G.wat.k[0]), (G.uT.t[0], G.uT.k[0]), (G.qT.t[0], G.qT.k[0])]
                for _ in ada_gen(G, l, bufs, 512):
                    pass
            if "ffn0" in phases:
                emit_ffn(G, l, 0, G.xin if l == 0 else G.xs, "xin" if l == 0 else "xs", G.xs, "xs")
        with P.phase():
            merge_tiles(G)
            if "mixin" in phases:
                emit_blocks(G, emit_mixin_block, l)
        if "mixers" in phases:
            emit_mixers(G, l, cfg.get("mixers", ("hgrn", "ret", "s5", "rwkv")))
        with P.phase():
            merge_tiles(G)
            if "merge" in phases:
                emit_blocks(G, emit_merge_block, l, skip_ctx=last)
        with P.phase():
            dense_tiles(G)
            if "ffn2" in phases:
                if last:
                    emit_ffn(G, l, 2, G.xs, "xs", G.out, "out", skip_ctx=True, final=True)
                else:
                    emit_ffn(G, l, 2, G.xs, "xs", G.xs, "xs")


TBLK = [(0, 512), (512, 512), (1024, 512), (1536, 512), (2048, 256)]
OFF_RWKV, OFF_RET, OFF_S5, OFF_HG = 0, 1056, 2080, 2336


def rev(t, a, b):
    ap = t[:, b - 1:b]
    return bass.AP(ap.tensor, ap.offset, [list(ap.ap[0]), [-1, b - a]])


def setup_consts(G):
    P, nc = G.P, G.nc
    G.epsc = P.sb("epsc", [128, 2])
    P.op("dve", lambda: nc.vector.memset(G.epsc[:, 0:1], LN_EPS), [], ["epsc"])
    P.op("dve", lambda: nc.vector.memset(G.epsc[:, 1:2], 64e-5), [], ["epsc"])
    G.ones9 = P.sb("ones9", [128, 128])
    P.op("dve", lambda: nc.vector.memset(G.ones9[:], 1.0), [], ["ones9"])
    G.onesH = P.sb("onesH", [128, 128])
    P.op("dve", lambda: nc.vector.memset(G.onesH[:], 0.0), [], ["onesH"])
    P.op("dve", lambda: nc.vector.memset(G.onesH[0:64, 0:64], 1.0 / 64), [], ["onesH"])
    P.op("dve", lambda: nc.vector.memset(G.onesH[64:128, 64:128], 1.0 / 64), [], ["onesH"])
    G.hlb = P.sb("hlb", [128, 2, NL])
    G.homl = P.sb("homl", [128, 2, NL])
    e = P.sb("hlb_e", [128, 2, NL])
    ssum = P.sb("hlb_s", [128, 2])
    o, n = G.off["hg_lbl"]
    P.op("act", lambda: nc.scalar.activation(out=e[:].rearrange("p a b -> p (a b)"), in_=G.pp[:, o:o + n], func=AF.Exp), ["pp"], ["hlb_e"])
    for hp in range(2):
        P.op("dve", lambda: nc.vector.tensor_reduce(out=ssum[:, hp:hp + 1], in_=e[:, hp, :], axis=mybir.AxisListType.X, op=ALU.add),
             ["hlb_e"], ["hlb_s"])
    P.op("dve", lambda: nc.vector.reciprocal(out=ssum[:], in_=ssum[:]), ["hlb_s"], ["hlb_s"])
    for hp in range(2):
        P.op("dve", lambda: nc.vector.tensor_scalar(out=e[:, hp, :], in0=e[:, hp, :], scalar1=ssum[:, hp:hp + 1], scalar2=None, op0=ALU.mult),
             ["hlb_e", "hlb_s"], ["hlb_e"])
        P.op("dve", lambda: nc.vector.memset(G.hlb[:, hp, 0:1], 0.0), [], ["hlb"])
        for l in range(1, NL):
            P.op("dve", lambda: nc.vector.tensor_tensor(out=G.hlb[:, hp, l:l + 1], in0=G.hlb[:, hp, l - 1:l], in1=e[:, hp, l:l + 1], op=ALU.add),
                 ["hlb", "hlb_e"], ["hlb"])
    P.op("dve", lambda: nc.vector.tensor_scalar(out=G.homl[:].rearrange("p a b -> p (a b)"), in0=G.hlb[:].rearrange("p a b -> p (a b)"),
                                                scalar1=-1.0, scalar2=1.0, op0=ALU.mult, op1=ALU.add), ["hlb"], ["homl"])


def head_norm_blk(G, x, xk, n, center, eps_i):
    P, nc = G.P, G.nc
    if center:
        mean, mk = P.bank()
        P.op("pe", lambda: nc.tensor.matmul(mean[:, :n], lhsT=G.onesH[:], rhs=x, start=True, stop=True), [xk, "onesH"], [mk])
        P.op("dve", lambda: nc.vector.tensor_tensor(out=x, in0=x, in1=mean[:, :n], op=ALU.subtract), [xk, mk], [xk])
    sq, sqk = G.tmp.next()
    P.op("act", lambda: nc.scalar.activation(out=sq[:, :n], in_=x, func=AF.Square), [xk], [sqk])
    var, vk = P.bank()
    P.op("pe", lambda: nc.tensor.matmul(var[:, :n], lhsT=G.onesH[:], rhs=sq[:, :n], start=True, stop=True), [sqk, "onesH"], [vk])
    P.op("act", lambda: nc.scalar.activation(out=sq[:, :n], in_=var[:, :n], func=AF.Sqrt, bias=G.epsc[:, eps_i:eps_i + 1]), [vk, "epsc"], [sqk])
    P.op("dve", lambda: nc.vector.reciprocal(out=sq[:, :n], in_=sq[:, :n]), [sqk], [sqk])
    P.op("dve", lambda: nc.vector.tensor_tensor(out=x, in0=x, in1=sq[:, :n], op=ALU.mult), [xk, sqk], [xk])


def gla_tiles(G):
    P = G.P
    G.gq = Pool_(P, "gq", [128, TT], 1)
    G.gk = Pool_(P, "gk", [128, TT], 2)
    G.gf = Pool_(P, "gf", [128, TT], 2)
    G.gg = Pool_(P, "gg", [128, TT], 1)
    G.go = Pool_(P, "go", [128, TT], 1)
    G.vb = Pool_(P, "vb", [128, TT], 2)
    G.kv = Pool_(P, "kv", [128, TT], 2)
    G.S = Pool_(P, "S", [128, TT], 2)
    G.qs = Pool_(P, "qs", [128, TT], 2)
    G.tmp = Pool_(P, "tmp", [128, 512], 4)


def gla_hp(G, b, q, qk, kk, ff, vrow, excl, accs):
    P, nc = G.P, G.nc
    for v in range(G.cfg.get("nv", 64)):
        vb, vbk = G.vb.next()
        for h in range(2):
            src = G.pT[b, vrow + h * 64 + v:vrow + h * 64 + v + 1, :]
            P.dma("sp", out=vb[h * 64:(h + 1) * 64, :], in_=bass.AP(src.tensor, src.offset, [[0, 64], [1, TT]]), reads=[], writes=[vbk])
        for d in range(2):
            kt, ktk = kk[d]
            kv, kvk = G.kv.next()
            P.op("pool", lambda: nc.gpsimd.tensor_tensor(out=kv[:], in0=kt[:], in1=vb[:], op=ALU.mult), [ktk, vbk], [kvk])
            S, Sk = G.S.next()
            kind, fobj, fkey = ff[d]
            if d == 0:
                d0 = fobj[:, 0:TT] if kind == "t" else bc(fobj, TT)
                P.op("dve", lambda: nc.vector.tensor_tensor_scan(out=S[:, 0:TT], data0=d0, data1=kv[:, 0:TT], initial=0.0,
                                                                 op0=ALU.mult, op1=ALU.add), [fkey, kvk], [Sk])
            else:
                d0 = rev(fobj, 0, CTX) if kind == "t" else bc(fobj, CTX)
                P.op("dve", lambda: nc.vector.tensor_tensor_scan(out=rev(S, 0, CTX), data0=d0, data1=rev(kv, 0, CTX), initial=0.0,
                                                                 op0=ALU.mult, op1=ALU.add), [fkey, kvk], [Sk])
                d0 = rev(fobj, CTX, TT) if kind == "t" else bc(fobj, SEQ)
                P.op("dve", lambda: nc.vector.tensor_tensor_scan(out=rev(S, CTX, TT), data0=d0, data1=rev(kv, CTX, TT), initial=S[:, 0:1],
                                                                 op0=ALU.mult, op1=ALU.add), [fkey, kvk, Sk], [Sk])
            if excl[d]:
                P.op("pool", lambda: nc.gpsimd.tensor_tensor(out=S[:], in0=S[:], in1=kv[:], op=ALU.subtract), [Sk, kvk], [Sk])
            qs, qsk = G.qs.next()
            P.op("dve", lambda: nc.vector.tensor_tensor(out=qs[:], in0=q[:], in1=S[:], op=ALU.mult), [qk, Sk], [qsk])
            for bi, (t0, n) in enumerate(TBLK):
                acc, ak = accs[bi]
                P.op("pe", lambda: nc.tensor.matmul(acc[:, :n], lhsT=G.G2[:, 63 - v:191 - v], rhs=qs[:, t0:t0 + n],
                                                    start=(v == 0 and d == 0), stop=(v == G.cfg.get("nv", 64) - 1 and d == 1)), [qsk, "G2"], [ak])


def emit_hgrn(G, l, b):
    P, nc = G.P, G.nc
    for hp in range(2):
        r0 = OFF_HG + hp * 128
        q, qk = G.gq.next()
        P.dma("sp", out=q[:], in_=G.pT[b, r0:r0 + 128, :], reads=[], writes=[qk])
        g, gk = G.gg.next()
        P.dma("sp", out=g[:], in_=G.pT[b, r0 + 1024:r0 + 1024 + 128, :], reads=[], writes=[gk])
        kk, ff = [], []
        for d in range(2):
            f, fk = G.gf.next()
            P.dma("sp", out=f[:], in_=G.pT[b, r0 + 256 * (d + 1):r0 + 256 * (d + 1) + 128, :], reads=[], writes=[fk])
            o_, _ = G.off["hg_fb"]
            P.op("act", lambda: nc.scalar.activation(out=f[:], in_=f[:], func=AF.Sigmoid,
                                                     bias=G.pp[:, o_ + l * 4 + d * 2 + hp:o_ + l * 4 + d * 2 + hp + 1]), [fk, "pp"], [fk])
            P.op("dve", lambda: nc.vector.tensor_scalar(out=f[:], in0=f[:], scalar1=G.homl[:, hp, l:l + 1], scalar2=G.hlb[:, hp, l:l + 1],
                                                        op0=ALU.mult, op1=ALU.add), [fk, "homl", "hlb"], [fk])
            k, kk_ = G.gk.next()
            P.op("dve", lambda: nc.vector.tensor_scalar(out=k[:], in0=f[:], scalar1=-1.0, scalar2=1.0, op0=ALU.mult, op1=ALU.add), [fk], [kk_])
            kk.append((k, kk_))
            ff.append(("t", f, fk))
        accs = P.reserve(5)
        gla_hp(G, b, q, qk, kk, ff, r0 + 768, [False, False], accs)
        o, ok = G.go.next()
        for bi, (t0, n) in enumerate(TBLK):
            acc, ak = accs[bi]
            P.op("act", lambda: nc.scalar.copy(out=o[:, t0:t0 + n], in_=acc[:, :n]), [ak], [ok])
        P.release()
        P.op("act", lambda: nc.scalar.activation(out=g[:], in_=g[:], func=AF.Silu), [gk], [gk])
        for bi, (t0, n) in enumerate(TBLK):
            head_norm_blk(G, o[:, t0:t0 + n], ok, n, False, 0)
        P.op("dve", lambda: nc.vector.scalar_tensor_tensor(out=o[:], in0=o[:], scalar=ppcol(G, "hg_gn", l * 2 + hp), in1=g[:],
                                                           op0=ALU.mult, op1=ALU.mult), [ok, gk, "pp"], [ok])
        P.dma("pool", out=G.yT[b, 768 + hp * 128:768 + (hp + 1) * 128, :], in_=o[:], reads=[ok], writes=[("yT", b, 3, hp)])


GLA_CAP = 80.0


def gla2_tiles(G):
    P = G.P
    T9 = lambda n: P.sb("g2_" + n, [128, TT])
    G.g2 = dict(q=T9("q"), g=T9("g"), vv=T9("vv"), yacc=T9("yacc"),
                vtok=P.sb("g2_vtok", [128, 18, 128]),
                ident=P.sb("g2_ident", [128, 128]), Mk=P.sb("g2_M", [128, 2, 1280]))
    for d in range(2):
        G.g2[d] = dict(LD=T9("LD%d" % d), KK=T9("KK%d" % d), QT=T9("QT%d" % d), KE=T9("KE%d" % d), QH=T9("QH%d" % d),
                       Rref=P.sb("g2_Rref%d" % d, [128, 18, 8]), S0=P.sb("g2_S0%d" % d, [128, 64]),
                       sc=P.sb("g2_sc%d" % d, [128, 256]), ktok=P.sb("g2_ktok%d" % d, [128, 128]), gc=P.sb("g2_gc%d" % d, [128, 1]),
                       Z=Pool_(P, "g2Z%d" % d, [128, 8, 128], 2))
    G.tmp = Pool_(P, "tmp", [128, 512], 4)
    P.dma("sp", out=G.g2["ident"][:], in_=G.ident_d, reads=[], writes=["g2ident"])
    P.dma("sp", out=G.g2["Mk"][:], in_=G.rwM, reads=[], writes=["g2M"])


def gla2_core(G, d, excl):
    P, nc = G.P, G.nc
    V = nc.vector
    t = dict(G.g2)
    t.update(G.g2[d])
    q, KK, B, QT, KE, QH, Rref, S0, vtok, yacc = t["q"], t["KK"], t["LD"], t["QT"], t["KE"], t["QH"], t["Rref"], t["S0"], t["vtok"], t["yacc"]
    kkn = t.get("KKkey", "g2KK%d" % d)
    for c in range(18):
        if d == 0:
            P.op("dve", lambda: V.tensor_tensor_scan(out=B[:, c * 128:(c + 1) * 128], data0=G.ones9[:, 0:128], data1=B[:, c * 128:(c + 1) * 128], initial=0.0,
                                                     op0=ALU.mult, op1=ALU.add), ["g2LD%d" % d, "ones9"], ["g2LD%d" % d])
        else:
            P.op("dve", lambda: V.tensor_tensor_scan(out=rev(B, c * 128, (c + 1) * 128), data0=G.ones9[:, 0:128], data1=rev(B, c * 128, (c + 1) * 128), initial=0.0,
                                                     op0=ALU.mult, op1=ALU.add), ["g2LD%d" % d, "ones9"], ["g2LD%d" % d])
    bap = B[:, 0:1]
    ps_ = list(bap.ap[0])
    mk = lambda off, dims: bass.AP(bap.tensor, bap.offset + off, [ps_] + dims)
    if d == 0:
        P.op("dve", lambda: V.memset(Rref[:, :, 0:1], 0.0), [], ["g2R%d" % d])
        P.op("dve", lambda: V.tensor_copy(out=Rref[:, :, 1:8], in_=mk(15, [[128, 18], [16, 7]])), ["g2LD%d" % d], ["g2R%d" % d])
        bc_off = 127
    else:
        P.op("dve", lambda: V.memset(Rref[:, :, 7:8], 0.0), [], ["g2R%d" % d])
        P.op("dve", lambda: V.tensor_copy(out=Rref[:, :, 0:7], in_=mk(16, [[128, 18], [16, 7]])), ["g2LD%d" % d], ["g2R%d" % d])
        bc_off = 0
    P.op("act", lambda: nc.scalar.activation(out=QT[:], in_=B[:], func=AF.Exp), ["g2LD%d" % d], ["g2QT%d" % d])
    P.op("pool", lambda: nc.gpsimd.tensor_tensor(out=QT[:], in0=QT[:], in1=q[:], op=ALU.mult), ["g2QT%d" % d, "g2q"], ["g2QT%d" % d])
    v3 = lambda ap: ap.rearrange("p (c s) -> p c s", c=18)
    P.op("dve", lambda: V.tensor_tensor(out=v3(KE[:]), in0=mk(bc_off, [[128, 18], [0, 128]]), in1=v3(B[:]), op=ALU.subtract), ["g2LD%d" % d], ["g2KE%d" % d])
    P.op("act", lambda: nc.scalar.activation(out=KE[:], in_=KE[:], func=AF.Exp), ["g2KE%d" % d], ["g2KE%d" % d])
    P.op("dve", lambda: V.tensor_tensor(out=KE[:], in0=KE[:], in1=KK[:], op=ALU.mult), ["g2KE%d" % d, kkn], ["g2KE%d" % d])
    rap = Rref[:, 0:1, 0:1]
    rexp = bass.AP(rap.tensor, rap.offset, [list(rap.ap[0]), [1, 144], [0, 16]])
    P.op("dve", lambda: V.tensor_tensor(out=QH[:].rearrange("p (c s) -> p c s", s=16), in0=B[:].rearrange("p (c s) -> p c s", s=16), in1=rexp, op=ALU.subtract),
         ["g2LD%d" % d, "g2R%d" % d], ["g2QH%d" % d])
    P.op("act", lambda: nc.scalar.activation(out=QH[:], in_=QH[:], func=AF.Exp), ["g2QH%d" % d], ["g2QH%d" % d])
    P.op("pool", lambda: nc.gpsimd.tensor_tensor(out=QH[:], in0=QH[:], in1=q[:], op=ALU.mult), ["g2QH%d" % d, "g2q"], ["g2QH%d" % d])
    P.op("dve", lambda: V.memset(S0[:], 0.0), [], ["g2S0%d" % d])
    order = list(range(18)) if d == 0 else [1, 0] + list(range(17, 1, -1))
    mcol = (512 if excl else 768)
    yield
    def make_z(c):
        Z, Zk = t["Z"].next()
        r_c = Rref[:, c, 0:1]
        rb = bass.AP(r_c.tensor, r_c.offset, [list(r_c.ap[0]), [1, 8], [0, 128]])
        b_c = B[:, c * 128:c * 128 + 1]
        bb = bass.AP(b_c.tensor, b_c.offset, [list(b_c.ap[0]), [0, 8], [1, 128]])
        k_c = KK[:, c * 128:c * 128 + 1]
        kb = bass.AP(k_c.tensor, k_c.offset, [list(k_c.ap[0]), [0, 8], [1, 128]])
        P.op("dve", lambda: V.tensor_tensor(out=Z[:], in0=rb, in1=bb, op=ALU.subtract), ["g2R%d" % d, "g2LD%d" % d], [Zk])
        if t.get("clamp", True):
            P.op("dve", lambda: V.tensor_scalar(out=Z[:], in0=Z[:], scalar1=GLA_CAP, scalar2=None, op0=ALU.min), [Zk], [Zk])
        P.op("act", lambda: nc.scalar.activation(out=Z[:], in_=Z[:], func=AF.Exp), [Zk], [Zk])
        P.op("pool", lambda: nc.gpsimd.tensor_tensor(out=Z[:], in0=Z[:], in1=kb, op=ALU.mult), [Zk, kkn], [Zk])
        return Z, Zk

    znext = make_z(order[0])
    for ci, c in enumerate(order):
        cs = slice(c * 128, (c + 1) * 128)
        Z, Zk = znext
        if ci + 1 < len(order):
            znext = make_z(order[ci + 1])
        yield
        pS, pSk = P.bank()
        for h in range(2):
            hb = slice(64 * h, 64 * h + 64)
            for m in range(8):
                P.op("pe", lambda: nc.tensor.matmul(pS[:, h * 128 + 16 * m:h * 128 + 16 * m + 16], lhsT=Z[hb, m, :], rhs=QH[hb, c * 128 + 16 * m:c * 128 + 16 * m + 16],
                                                    start=True, stop=True), [Zk, "g2QH%d" % d], [pSk])
        P.op("dve", lambda: V.tensor_tensor(out=t["sc"][:], in0=pS[:, 0:256], in1=t["Mk"][:, d, mcol:mcol + 256], op=ALU.mult), [pSk, "g2M"], ["g2sc%d" % d])
        yield
        pt, ptk = P.bank()
        P.op("pe", lambda: nc.tensor.transpose(out=pt[:, 0:128], in_=KE[:, cs], identity=t["ident"][:]), ["g2KE%d" % d, "g2ident"], [ptk])
        P.op("act", lambda: nc.scalar.copy(out=t["ktok"][:], in_=pt[:, 0:128]), [ptk], ["g2ktok%d" % d])
        yield
        py, pyk = P.bank()
        for h in range(2):
            hb = slice(64 * h, 64 * h + 64)
            P.op("pe", lambda: nc.tensor.matmul(py[hb, 0:128], lhsT=S0[hb, :], rhs=QT[hb, cs], start=True, stop=False), ["g2S0%d" % d, "g2QT%d" % d], [pyk])
            P.op("pe", lambda: nc.tensor.matmul(py[hb, 0:128], lhsT=vtok[:, c, h * 64:(h + 1) * 64], rhs=t["sc"][:, h * 128:(h + 1) * 128], start=False, stop=True),
                 ["g2vtok", "g2sc%d" % d], [pyk])
        P.op("dve", lambda: V.tensor_tensor(out=yacc[:, cs], in0=yacc[:, cs], in1=py[:, 0:128], op=ALU.add), [pyk, ("g2yacc", c)], [("g2yacc", c)])
        yield
        pT_, pTk = P.bank()
        for h in range(2):
            hb = slice(64 * h, 64 * h + 64)
            P.op("pe", lambda: nc.tensor.matmul(pT_[hb, 0:64], lhsT=t["ktok"][:, hb], rhs=vtok[:, c, h * 64:(h + 1) * 64], start=True, stop=True), ["g2ktok%d" % d, "g2vtok"], [pTk])
        tl = (c * 128 + 127) if d == 0 else (c * 128)
        P.op("act", lambda: nc.scalar.activation(out=t["gc"][:], in_=B[:, tl:tl + 1], func=AF.Exp), ["g2LD%d" % d], ["g2gc%d" % d])
        P.op("dve", lambda: V.scalar_tensor_tensor(out=S0[:], in0=S0[:], scalar=t["gc"][:, 0:1], in1=pT_[:, 0:64], op0=ALU.mult, op1=ALU.add), ["g2S0%d" % d, "g2gc%d" % d, pTk], ["g2S0%d" % d])


def gla2_run(G, excl):
    P, nc = G.P, G.nc
    P.op("dve", lambda: nc.vector.memset(G.g2["yacc"][:], 0.0), [], [("g2yacc", c) for c in range(18)])
    gens = [gla2_core(G, 0, excl[0]), gla2_core(G, 1, excl[1])]
    live = list(gens)
    while live:
        for g_ in list(live):
            try:
                next(g_)
            except StopIteration:
                live.remove(g_)


def gla2_vtok(G):
    P, nc = G.P, G.nc
    t = G.g2
    for c in range(18):
        pt, ptk = P.bank()
        P.op("pe", lambda: nc.tensor.transpose(out=pt[:, 0:128], in_=t["vv"][:, c * 128:(c + 1) * 128], identity=t["ident"][:]), ["g2vv", "g2ident"], [ptk])
        P.op("act", lambda: nc.scalar.copy(out=t["vtok"][:, c, :], in_=pt[:, 0:128]), [ptk], ["g2vtok"])


def gla2_finish(G, l, b, hp, center, gn_name, yrow):
    P, nc = G.P, G.nc
    t = G.g2
    o, g = t["yacc"], t["g"]
    P.op("act", lambda: nc.scalar.activation(out=g[:], in_=g[:], func=AF.Silu), ["g2g"], ["g2g"])
    allk = [("g2yacc", c) for c in range(18)]
    P.op("dve", lambda: nc.vector.tensor_copy(out=o[:, 0:1], in_=o[:, 0:1]), allk, allk + ["g2yacc"])
    for bi, (t0, n) in enumerate(TBLK):
        head_norm_blk(G, o[:, t0:t0 + n], "g2yacc", n, center, 0)
    P.op("dve", lambda: nc.vector.scalar_tensor_tensor(out=o[:], in0=o[:], scalar=ppcol(G, gn_name, l * 2 + hp), in1=g[:],
                                                       op0=ALU.mult, op1=ALU.mult), ["g2yacc", "g2g", "pp"], ["g2yacc"] + allk)
    P.dma("pool", out=G.yT[b, yrow + hp * 128:yrow + (hp + 1) * 128, :], in_=o[:], reads=["g2yacc"], writes=[("yT", b, yrow, hp)])


def emit_hgrn2(G, l, b):
    P, nc = G.P, G.nc
    V = nc.vector
    t = G.g2
    for hp in range(2):
        r0 = OFF_HG + hp * 128
        P.dma("sp", out=t["q"][:], in_=G.pT[b, r0:r0 + 128, :], reads=[], writes=["g2q"])
        P.dma("sp", out=t["g"][:], in_=G.pT[b, r0 + 1024:r0 + 1024 + 128, :], reads=[], writes=["g2g"])
        P.dma("sp", out=t["vv"][:], in_=G.pT[b, r0 + 768:r0 + 768 + 128, :], reads=[], writes=["g2vv"])
        gla2_vtok(G)
        for d in range(2):
            f = t[d]["LD"]
            P.dma("sp", out=f[:], in_=G.pT[b, r0 + 256 * (d + 1):r0 + 256 * (d + 1) + 128, :], reads=[], writes=["g2LD%d" % d])
            o_, _ = G.off["hg_fb"]
            P.op("act", lambda: nc.scalar.activation(out=f[:], in_=f[:], func=AF.Sigmoid,
                                                     bias=G.pp[:, o_ + l * 4 + d * 2 + hp:o_ + l * 4 + d * 2 + hp + 1]), ["g2LD%d" % d, "pp"], ["g2LD%d" % d])
            P.op("dve", lambda: V.tensor_scalar(out=f[:], in0=f[:], scalar1=G.homl[:, hp, l:l + 1], scalar2=G.hlb[:, hp, l:l + 1],
                                                op0=ALU.mult, op1=ALU.add), ["g2LD%d" % d, "homl", "hlb"], ["g2LD%d" % d])
            P.op("dve", lambda: V.tensor_scalar(out=t[d]["KK"][:], in0=f[:], scalar1=-1.0, scalar2=1.0, op0=ALU.mult, op1=ALU.add), ["g2LD%d" % d], ["g2KK%d" % d])
            P.op("act", lambda: nc.scalar.activation(out=f[:], in_=f[:], func=AF.Ln), ["g2LD%d" % d], ["g2LD%d" % d])
        gla2_run(G, [False, False])
        gla2_finish(G, l, b, hp, False, "hg_gn", 768)


def emit_ret2(G, l, b):
    P, nc = G.P, G.nc
    V = nc.vector
    t = G.g2
    for hp in range(2):
        r0 = OFF_RET + hp * 128
        q, k = t["q"], t[0]["KK"]
        P.dma("sp", out=q[:], in_=G.pT[b, r0:r0 + 128, :], reads=[], writes=["g2q"])
        P.dma("sp", out=k[:], in_=G.pT[b, r0 + 256:r0 + 256 + 128, :], reads=[], writes=["g2KK0"])
        P.dma("sp", out=t["vv"][:], in_=G.pT[b, r0 + 512:r0 + 512 + 128, :], reads=[], writes=["g2vv"])
        P.dma("sp", out=t["g"][:], in_=G.pT[b, r0 + 768:r0 + 768 + 128, :], reads=[], writes=["g2g"])
        gla2_vtok(G)
        P.op("act", lambda: nc.scalar.mul(out=k[:], in_=k[:], mul=0.125), ["g2KK0"], ["g2KK0"])
        for x, xk in ((q, "g2q"), (k, "g2KK0")):
            for i in range(4):
                t0 = CTX + i * 512
                sw, swk = P.bank()
                P.op("pe", lambda: nc.tensor.matmul(sw[:, :512], lhsT=G.perm[:], rhs=x[:, t0:t0 + 512], start=True, stop=True), [xk, "perm"], [swk])
                tm, tk = G.tmp.next()
                P.op("dve", lambda: V.tensor_tensor(out=tm[:, :512], in0=sw[:, :512], in1=G.rope[:, 1, i * 512:(i + 1) * 512], op=ALU.mult), [swk, "rope"], [tk])
                P.op("pool", lambda: nc.gpsimd.tensor_tensor(out=x[:, t0:t0 + 512], in0=x[:, t0:t0 + 512], in1=G.rope[:, 0, i * 512:(i + 1) * 512], op=ALU.mult),
                     [xk, "rope"], [xk])
                P.op("dve", lambda: V.tensor_tensor(out=x[:, t0:t0 + 512], in0=x[:, t0:t0 + 512], in1=tm[:, :512], op=ALU.add), [xk, tk], [xk])
        for d in range(2):
            P.op("dve", lambda: V.tensor_copy(out=t[d]["LD"][:], in_=bc(ppcol(G, "ret_lg", hp), TT)), ["pp"], ["g2LD%d" % d])
        G.g2[1]["KK_save"] = G.g2[1]["KK"]
        G.g2[1]["KK"] = k
        G.g2[1]["KKkey"] = "g2KK0"
        G.g2[0]["clamp"] = G.g2[1]["clamp"] = False
        gla2_run(G, [False, True])
        G.g2[0]["clamp"] = G.g2[1]["clamp"] = True
        G.g2[1]["KK"] = G.g2[1].pop("KK_save")
        G.g2[1].pop("KKkey")
        gla2_finish(G, l, b, hp, True, "ret_gn", 256)


def emit_ret(G, l, b):
    P, nc = G.P, G.nc
    for hp in range(2):
        r0 = OFF_RET + hp * 128
        q, qk = G.gq.next()
        P.dma("sp", out=q[:], in_=G.pT[b, r0:r0 + 128, :], reads=[], writes=[qk])
        k, kk_ = G.gk.next()
        P.dma("sp", out=k[:], in_=G.pT[b, r0 + 256:r0 + 256 + 128, :], reads=[], writes=[kk_])
        g, gk = G.gg.next()
        P.dma("sp", out=g[:], in_=G.pT[b, r0 + 768:r0 + 768 + 128, :], reads=[], writes=[gk])
        P.op("act", lambda: nc.scalar.mul(out=k[:], in_=k[:], mul=0.125), [kk_], [kk_])
        for x, xk in ((q, qk), (k, kk_)):
            for i in range(4):
                t0 = CTX + i * 512
                sw, swk = P.bank()
                P.op("pe", lambda: nc.tensor.matmul(sw[:, :512], lhsT=G.perm[:], rhs=x[:, t0:t0 + 512], start=True, stop=True), [xk, "perm"], [swk])
                tm, tk = G.tmp.next()
                P.op("dve", lambda: nc.vector.tensor_tensor(out=tm[:, :512], in0=sw[:, :512], in1=G.rope[:, 1, i * 512:(i + 1) * 512], op=ALU.mult),
                     [swk, "rope"], [tk])
                P.op("pool", lambda: nc.gpsimd.tensor_tensor(out=x[:, t0:t0 + 512], in0=x[:, t0:t0 + 512], in1=G.rope[:, 0, i * 512:(i + 1) * 512], op=ALU.mult),
                     [xk, "rope"], [xk])
                P.op("dve", lambda: nc.vector.tensor_tensor(out=x[:, t0:t0 + 512], in0=x[:, t0:t0 + 512], in1=tm[:, :512], op=ALU.add), [xk, tk], [xk])
        accs = P.reserve(5)
        gam = ppcol(G, "ret_gam", hp)
        gla_hp(G, b, q, qk, [(k, kk_), (k, kk_)], [("c", gam, "pp"), ("c", gam, "pp")], r0 + 512, [False, True], accs)
        o, ok = G.go.next()
        for bi, (t0, n) in enumerate(TBLK):
            acc, ak = accs[bi]
            P.op("act", lambda: nc.scalar.copy(out=o[:, t0:t0 + n], in_=acc[:, :n]), [ak], [ok])
        P.release()
        P.op("act", lambda: nc.scalar.activation(out=g[:], in_=g[:], func=AF.Silu), [gk], [gk])
        for bi, (t0, n) in enumerate(TBLK):
            head_norm_blk(G, o[:, t0:t0 + n], ok, n, True, 0)
        P.op("dve", lambda: nc.vector.scalar_tensor_tensor(out=o[:], in0=o[:], scalar=ppcol(G, "ret_gn", l * 2 + hp), in1=g[:],
                                                           op0=ALU.mult, op1=ALU.mult), [ok, gk, "pp"], [ok])
        P.dma("pool", out=G.yT[b, 256 + hp * 128:256 + (hp + 1) * 128, :], in_=o[:], reads=[ok], writes=[("yT", b, 1, hp)])


def rope_tables():
    nf = 16
    inv = (np.float32(10000.0) ** (-np.arange(nf, dtype=np.float32) / np.float32(nf))).astype(np.float32)
    rows = np.repeat(np.arange(SEQ // 64, dtype=np.float32), 64)
    cols = np.tile(np.arange(64, dtype=np.float32), SEQ // 64)
    ang = np.concatenate([rows[:, None] * inv, cols[:, None] * inv], -1).astype(np.float32)
    cos, sin = np.cos(ang).astype(np.float32), np.sin(ang).astype(np.float32)
    tab = np.zeros((128, 2, SEQ), np.float32)
    perm = np.zeros((128, 128), np.float32)
    for p in range(128):
        d = p % 64
        tab[p, 0] = cos[:, d % 32]
        tab[p, 1] = sin[:, d % 32] * (-1.0 if d < 32 else 1.0)
        perm[p + 32 if d < 32 else p - 32, p] = 1.0
    return tab, perm


TWO_PI = 6.283185307179586


def s5_sin(G, out, ang, nm):
    P, nc = G.P, G.nc
    ki = G.s5_ki
    kf = G.s5_kf
    P.op("dve", lambda: nc.vector.tensor_scalar(out=kf[:], in0=ang, scalar1=1.0 / TWO_PI, scalar2=None, op0=ALU.mult), [nm], ["s5_kf"])
    P.op("dve", lambda: nc.vector.tensor_copy(out=ki[:], in_=kf[:]), ["s5_kf"], ["s5_ki"])
    P.op("dve", lambda: nc.vector.tensor_copy(out=kf[:], in_=ki[:]), ["s5_ki"], ["s5_kf"])
    P.op("dve", lambda: nc.vector.scalar_tensor_tensor(out=out, in0=kf[:], scalar=-TWO_PI, in1=ang, op0=ALU.mult, op1=ALU.add), ["s5_kf", nm], [nm + "_o"])
    P.op("dve", lambda: nc.vector.tensor_scalar(out=kf[:], in0=out, scalar1=3.141592653589793, scalar2=-TWO_PI, op0=ALU.is_gt, op1=ALU.mult), [nm + "_o"], ["s5_kf"])
    P.op("dve", lambda: nc.vector.tensor_tensor(out=out, in0=out, in1=kf[:], op=ALU.add), [nm + "_o", "s5_kf"], [nm + "_o"])
    P.op("dve", lambda: nc.vector.tensor_scalar(out=kf[:], in0=out, scalar1=-3.141592653589793, scalar2=TWO_PI, op0=ALU.is_lt, op1=ALU.mult), [nm + "_o"], ["s5_kf"])
    P.op("dve", lambda: nc.vector.tensor_tensor(out=out, in0=out, in1=kf[:], op=ALU.add), [nm + "_o", "s5_kf"], [nm + "_o"])
    P.op("dve", lambda: nc.vector.tensor_scalar(out=out, in0=out, scalar1=3.1415925, scalar2=-3.1415925, op0=ALU.min, op1=ALU.max), [nm + "_o"], [nm + "_o"])
    P.op("act", lambda: nc.scalar.activation(out=out, in_=out, func=AF.Sin), [nm + "_o"], [nm + "_o"])


def s5_params(G, l):
    P, nc = G.P, G.nc
    sb = lambda n: P.sb("s5p_" + n, [128, 16])
    G.s5_ki = G.P.scopes[-1].enter_context(nc.sbuf_tensor("s5_ki_%d" % l, [128, 16], mybir.dt.int32))
    G.s5_kf = sb("kf")
    o_lr, o_li, o_dt = G.off["s5_lr"][0] + l * 16, G.off["s5_li"][0] + l * 16, G.off["s5_ldt"][0] + l * 16
    lr, li = G.pp[:, o_lr:o_lr + 16], G.pp[:, o_li:o_li + 16]
    dt, th, ph = sb("dt"), sb("th"), sb("ph")
    G.s5rho, G.s5cs, G.s5sn, G.s5fre, G.s5fim = sb("rho"), sb("cs"), sb("sn"), sb("fre"), sb("fim")
    t1, t2, den = sb("t1"), sb("t2"), sb("den")
    V = nc.vector
    P.op("act", lambda: nc.scalar.activation(out=dt[:], in_=G.pp[:, o_dt:o_dt + 16], func=AF.Exp), ["pp"], ["dt"])
    P.op("dve", lambda: V.tensor_tensor(out=t1[:], in0=lr, in1=dt[:], op=ALU.mult), ["pp", "dt"], ["t1"])
    P.op("act", lambda: nc.scalar.activation(out=G.s5rho[:], in_=t1[:], func=AF.Exp), ["t1"], ["rho"])
    P.op("dve", lambda: V.tensor_tensor(out=th[:], in0=li, in1=dt[:], op=ALU.mult), ["pp", "dt"], ["th"])
    P.op("dve", lambda: V.tensor_scalar(out=ph[:], in0=th[:], scalar1=1.5707963267948966, scalar2=None, op0=ALU.add), ["th"], ["ph"])
    s5_sin(G, G.s5sn[:], th[:], "th")
    s5_sin(G, G.s5cs[:], ph[:], "ph")
    lbre, lbim = sb("lbre"), sb("lbim")
    P.op("dve", lambda: V.tensor_tensor(out=lbre[:], in0=G.s5rho[:], in1=G.s5cs[:], op=ALU.mult), ["rho", "ph_o"], ["lbre"])
    P.op("dve", lambda: V.tensor_tensor(out=lbim[:], in0=G.s5rho[:], in1=G.s5sn[:], op=ALU.mult), ["rho", "th_o"], ["lbim"])
    P.op("dve", lambda: V.tensor_tensor(out=den[:], in0=lr, in1=lr, op=ALU.mult), ["pp"], ["den"])
    P.op("dve", lambda: V.tensor_tensor(out=t1[:], in0=li, in1=li, op=ALU.mult), ["pp", "rho"], ["t1"])
    P.op("dve", lambda: V.tensor_tensor(out=den[:], in0=den[:], in1=t1[:], op=ALU.add), ["den", "t1"], ["den"])
    P.op("dve", lambda: V.reciprocal(out=den[:], in_=den[:]), ["den"], ["den"])
    P.op("dve", lambda: V.tensor_scalar(out=lbre[:], in0=lbre[:], scalar1=-1.0, scalar2=None, op0=ALU.add), ["lbre"], ["lbre"])
    P.op("dve", lambda: V.tensor_tensor(out=t1[:], in0=lbre[:], in1=lr, op=ALU.mult), ["lbre", "pp", "den"], ["t1"])
    P.op("dve", lambda: V.tensor_tensor(out=t2[:], in0=lbim[:], in1=li, op=ALU.mult), ["lbim", "pp"], ["t2"])
    P.op("dve", lambda: V.tensor_tensor(out=t1[:], in0=t1[:], in1=t2[:], op=ALU.add), ["t1", "t2"], ["t1"])
    P.op("dve", lambda: V.tensor_tensor(out=G.s5fre[:], in0=t1[:], in1=den[:], op=ALU.mult), ["t1", "den"], ["fre"])
    P.op("dve", lambda: V.tensor_tensor(out=t1[:], in0=lbim[:], in1=lr, op=ALU.mult), ["lbim", "pp", "fre"], ["t1"])
    P.op("dve", lambda: V.tensor_tensor(out=t2[:], in0=lbre[:], in1=li, op=ALU.mult), ["lbre", "pp"], ["t2"])
    P.op("dve", lambda: V.tensor_tensor(out=t1[:], in0=t1[:], in1=t2[:], op=ALU.subtract), ["t1", "t2"], ["t1"])
    P.op("dve", lambda: V.tensor_tensor(out=G.s5fim[:], in0=t1[:], in1=den[:], op=ALU.mult), ["t1", "den"], ["fim"])


def s5_tables(G, col):
    P, nc = G.P, G.nc
    V = nc.vector
    Er, Ei, Pr, Pi = G.s5E
    cc = G.s5cc
    tm, tk = G.s5tm, "s5tm"
    P.op("dve", lambda: V.memset(Er[:, 0:1], 1.0), [], ["E"])
    P.op("dve", lambda: V.memset(Ei[:, 0:1], 0.0), [], ["E"])
    P.op("dve", lambda: V.tensor_copy(out=cc[:, 0:1], in_=G.s5cs[:, col:col + 1]), ["ph_o"], ["cc"])
    P.op("dve", lambda: V.tensor_copy(out=cc[:, 1:2], in_=G.s5sn[:, col:col + 1]), ["th_o"], ["cc"])
    n = 1
    while n < TT:
        m = min(n, TT - n)
        P.op("dve", lambda: V.tensor_scalar(out=tm[:, :m], in0=Ei[:, 0:m], scalar1=cc[:, 1:2], scalar2=None, op0=ALU.mult), ["E", "cc"], [tk])
        P.op("dve", lambda: V.scalar_tensor_tensor(out=Er[:, n:n + m], in0=Er[:, 0:m], scalar=cc[:, 0:1], in1=tm[:, :m], op0=ALU.mult, op1=ALU.subtract),
             ["E", "cc", tk], ["E"])
        P.op("dve", lambda: V.tensor_scalar(out=tm[:, :m], in0=Er[:, 0:m], scalar1=cc[:, 1:2], scalar2=None, op0=ALU.mult), ["E", "cc"], [tk])
        P.op("dve", lambda: V.scalar_tensor_tensor(out=Ei[:, n:n + m], in0=Ei[:, 0:m], scalar=cc[:, 0:1], in1=tm[:, :m], op0=ALU.mult, op1=ALU.add),
             ["E", "cc", tk], ["E"])
        n *= 2
        if n < TT:
            P.op("dve", lambda: V.tensor_tensor(out=cc[:, 2:3], in0=cc[:, 0:1], in1=cc[:, 0:1], op=ALU.mult), ["cc"], ["cc"])
            P.op("dve", lambda: V.tensor_tensor(out=cc[:, 3:4], in0=cc[:, 1:2], in1=cc[:, 1:2], op=ALU.mult), ["cc"], ["cc"])
            P.op("dve", lambda: V.scalar_tensor_tensor(out=cc[:, 1:2], in0=cc[:, 0:1], scalar=2.0, in1=cc[:, 1:2], op0=ALU.mult, op1=ALU.mult), ["cc"], ["cc"])
            P.op("dve", lambda: V.tensor_tensor(out=cc[:, 0:1], in0=cc[:, 2:3], in1=cc[:, 3:4], op=ALU.subtract), ["cc"], ["cc"])
    fre, fim = G.s5fre[:, col:col + 1], G.s5fim[:, col:col + 1]
    P.op("dve", lambda: V.tensor_scalar(out=Pr[:], in0=Ei[:], scalar1=fim, scalar2=None, op0=ALU.mult), ["E", "fim"], ["Ep"])
    P.op("dve", lambda: V.scalar_tensor_tensor(out=Pr[:], in0=Er[:], scalar=fre, in1=Pr[:], op0=ALU.mult, op1=ALU.add), ["E", "fre", "Ep"], ["Ep"])
    P.op("pool", lambda: nc.gpsimd.tensor_scalar(out=Pi[:], in0=Ei[:], scalar1=fre, scalar2=None, op0=ALU.mult), ["E", "fre"], ["Epi"])
    P.op("dve", lambda: V.scalar_tensor_tensor(out=Pi[:], in0=Er[:], scalar=fim, in1=Pi[:], op0=ALU.mult, op1=ALU.subtract), ["E", "fim", "Epi"], ["Epi"])


def emit_s5(G, l):
    P, nc = G.P, G.nc
    V = nc.vector
    s5_params(G, l)
    Bt = P.sb("s5Bt", [128, 2, 8, 2, 128])
    Ct = P.sb("s5Ct", [128, 2, 8, 2, 64])
    Wt = P.sb("s5Wt", [128, 2, 256])
    P.dma("sp", out=Bt[:], in_=G.s5B[l], reads=[], writes=["s5Bt"])
    P.dma("sp", out=Ct[:], in_=G.s5C[l], reads=[], writes=["s5Ct"])
    P.dma("sp", out=Wt[:], in_=G.s5W[l], reads=[], writes=["s5Wt"])
    for d in range(2):
        P.op("dve", lambda: V.tensor_scalar(out=Ct[:, d, :, 1, :], in0=Ct[:, d, :, 1, :], scalar1=-1.0, scalar2=None, op0=ALU.mult), ["s5Ct"], ["s5Ct"])
    G.s5E = [P.sb("s5E%d" % i, [128, TT]) for i in range(4)]
    G.s5cc = P.sb("s5cc", [128, 4])
    G.s5tm = P.sb("s5tm", [128, 1024])
    uT = [P.sb("s5u%d" % b, [128, 2, TT]) for b in range(NB)]
    ya = [P.sb("s5y%d" % b, [128, 2, TT]) for b in range(NB)]
    bw = [P.sb("s5bw%d" % i, [128, TT]) for i in range(2)]
    ww = [P.sb("s5w%d" % i, [128, TT]) for i in range(2)]
    tmp = Pool_(P, "s5t", [128, 512], 4)
    for b in range(NB):
        P.dma("sp", out=uT[b][:], in_=G.pT[b, OFF_S5:OFF_S5 + 256, :].rearrange("(c p) t -> p c t", p=128), reads=[], writes=[("s5u", b)])
        for c in range(2):
            P.op("act", lambda: nc.scalar.activation(out=ya[b][:, c, :], in_=uT[b][:, c, :], func=AF.Identity, scale=ppcol(G, "s5_d", l * 2 + c)),
                 [("s5u", b), "pp"], [("s5y", b)])
    bg = None
    if l + 1 < NL and G.cfg.get("ada_ahead", False) and "ada" in G.cfg.get("phases", {"ada"}):
        abuf = Pool_(P, "s5ada", [128, 8, 128], 2)
        bg = ada_gen(G, l + 1, list(zip(abuf.t, abuf.k)), 128)
    for d in range(2):
        for j in range(8):
            col = d * 8 + j
            if bg is not None:
                for _ in range(5):
                    try:
                        next(bg)
                    except StopIteration:
                        bg = None
                        break
            s5_tables(G, col)
            Er, Ei, Pr, Pi = G.s5E
            base = 64 * ((j % 4) // 2)
            ch = j // 4
            for b in range(NB):
                for (t0, n) in BLOCKS:
                    if d == 0:
                        ta = t0
                        rv = lambda ap_fn: ap_fn
                    else:
                        ta = 0 if t0 == 0 else (CTX + TT - (t0 + n))
                    pr, prk = P.bank()
                    pi, pik = P.bank()
                    P.op("pe", lambda: nc.tensor.matmul(pr[:, :n], lhsT=Bt[base:base + 64, d, j, 0, :], rhs=uT[b][base:base + 64, ch, t0:t0 + n],
                                                        start=True, stop=True), ["s5Bt", ("s5u", b)], [prk])
                    P.op("pe", lambda: nc.tensor.matmul(pi[:, :n], lhsT=Bt[base:base + 64, d, j, 1, :], rhs=uT[b][base:base + 64, ch, t0:t0 + n],
                                                        start=True, stop=True), ["s5Bt", ("s5u", b)], [pik])
                    bre = pr[:, :n] if d == 0 else rev(pr, 0, n)
                    bim = pi[:, :n] if d == 0 else rev(pi, 0, n)
                    ta_, tb_ = tmp.next(), tmp.next()
                    P.op("dve", lambda: V.tensor_tensor(out=ta_[0][:, :n], in0=bim, in1=Pi[:, ta:ta + n], op=ALU.mult), [pik, "Epi"], [ta_[1]])
                    P.op("dve", lambda: V.tensor_tensor(out=bw[0][:, ta:ta + n], in0=bre, in1=Pr[:, ta:ta + n], op=ALU.mult), [prk, "Ep"], ["bw0"])
                    P.op("pool", lambda: nc.gpsimd.tensor_tensor(out=bw[0][:, ta:ta + n], in0=bw[0][:, ta:ta + n], in1=ta_[0][:, :n], op=ALU.subtract),
                         ["bw0", ta_[1]], ["bw0"])
                    P.op("dve", lambda: V.tensor_tensor(out=tb_[0][:, :n], in0=bim, in1=Pr[:, ta:ta + n], op=ALU.mult), [pik, "Ep"], [tb_[1]])
                    P.op("dve", lambda: V.tensor_tensor(out=bw[1][:, ta:ta + n], in0=bre, in1=Pi[:, ta:ta + n], op=ALU.mult), [prk, "Epi"], ["bw1"])
                    P.op("pool", lambda: nc.gpsimd.tensor_tensor(out=bw[1][:, ta:ta + n], in0=bw[1][:, ta:ta + n], in1=tb_[0][:, :n], op=ALU.add),
                         ["bw1", tb_[1]], ["bw1"])
                rho = bc(G.s5rho[:, col:col + 1], TT)
                for i in range(2):
                    P.op("dve", lambda: V.tensor_tensor_scan(out=ww[i][:], data0=rho, data1=bw[i][:], initial=0.0, op0=ALU.mult, op1=ALU.add),
                         ["bw%d" % i, "rho"], ["ww%d" % i])
                P.op("pool", lambda: nc.gpsimd.tensor_tensor(out=bw[0][:], in0=Er[:], in1=ww[0][:], op=ALU.mult), ["E", "ww0"], ["bw0"])
                P.op("pool", lambda: nc.gpsimd.tensor_tensor(out=bw[1][:], in0=Er[:], in1=ww[1][:], op=ALU.mult), ["E", "ww1"], ["bw1"])
                P.op("dve", lambda: V.tensor_tensor(out=ww[1][:], in0=Ei[:], in1=ww[1][:], op=ALU.mult), ["E", "ww1"], ["ww1"])
                P.op("dve", lambda: V.tensor_tensor(out=ww[0][:], in0=Ei[:], in1=ww[0][:], op=ALU.mult), ["E", "ww0"], ["ww0"])
                P.op("dve", lambda: V.tensor_tensor(out=bw[0][:], in0=bw[0][:], in1=ww[1][:], op=ALU.subtract), ["bw0", "ww1"], ["bw0"])
                P.op("dve", lambda: V.tensor_tensor(out=bw[1][:], in0=bw[1][:], in1=ww[0][:], op=ALU.add), ["bw1", "ww0"], ["bw1"])
                for (t0, n) in BLOCKS:
                    ta = t0 if d == 0 else (0 if t0 == 0 else (CTX + TT - (t0 + n)))
                    py, pyk = P.bank()
                    P.op("pe", lambda: nc.tensor.matmul(py[base:base + 64, :n], lhsT=Ct[:, d, j, 0, :], rhs=bw[0][:, ta:ta + n], start=True, stop=False),
                         ["s5Ct", "bw0"], [pyk])
                    P.op("pe", lambda: nc.tensor.matmul(py[base:base + 64, :n], lhsT=Ct[:, d, j, 1, :], rhs=bw[1][:, ta:ta + n], start=False, stop=True),
                         ["s5Ct", "bw1"], [pyk])
                    src = py[base:base + 64, :n] if d == 0 else rev(py[base:base + 64, :], 0, n)
                    P.op("dve", lambda: V.tensor_tensor(out=ya[b][base:base + 64, ch, t0:t0 + n], in0=ya[b][base:base + 64, ch, t0:t0 + n], in1=src, op=ALU.add),
                         [("s5y", b), pyk], [("s5y", b)])
    if bg is not None:
        for _ in bg:
            pass
    for b in range(NB):
        y = ya[b]
        yk = ("s5y", b)
        u = uT[b]
        uk = ("s5u", b)
        for c in range(2):
            P.op("act", lambda: nc.scalar.activation(out=u[:, c, :], in_=y[:, c, :], func=AF.Square), [yk], [uk])
            P.op("dve", lambda: V.tensor_scalar(out=u[:, c, :], in0=u[:, c, :], scalar1=0.044715, scalar2=1.0, op0=ALU.mult, op1=ALU.add), [uk], [uk])
            P.op("dve", lambda: V.tensor_tensor(out=u[:, c, :], in0=u[:, c, :], in1=y[:, c, :], op=ALU.mult), [uk, yk], [uk])
            P.op("act", lambda: nc.scalar.activation(out=u[:, c, :], in_=u[:, c, :], func=AF.Tanh, scale=0.7978845608028654), [uk], [uk])
            P.op("dve", lambda: V.tensor_scalar(out=u[:, c, :], in0=u[:, c, :], scalar1=1.0, scalar2=0.5, op0=ALU.add, op1=ALU.mult), [uk], [uk])
            P.op("dve", lambda: V.tensor_tensor(out=y[:, c, :], in0=u[:, c, :], in1=y[:, c, :], op=ALU.mult), [uk, yk], [yk])
        for c in range(2):
            for (t0, n) in TBLK:
                pg, pgk = P.bank()
                for kc in range(2):
                    P.op("pe", lambda: nc.tensor.matmul(pg[:, :n], lhsT=Wt[:, kc, c * 128:(c + 1) * 128], rhs=y[:, kc, t0:t0 + n], start=(kc == 0), stop=(kc == 1)),
                         ["s5Wt", yk], [pgk])
                P.op("act", lambda: nc.scalar.activation(out=u[:, c, t0:t0 + n], in_=pg[:, :n], func=AF.Sigmoid, bias=ppcol(G, "s5_bglu", l * 2 + c)),
                     [pgk, "pp"], [uk])
            P.op("dve", lambda: V.tensor_tensor(out=u[:, c, :], in0=u[:, c, :], in1=y[:, c, :], op=ALU.mult), [uk, yk], [uk])
        P.dma("pool", out=G.yT[b, 512:768, :].rearrange("(c p) t -> p c t", p=128), in_=u[:], reads=[uk], writes=[("yT", b, 2)])


def dbg(G, name, ap, key, shape):
    if name not in G.cfg.get("dbg", ()):
        return
    if name in G.dbgdone:
        return
    G.dbgdone.add(name)
    t = G.nc.dram_tensor("dbg_" + name, list(shape), F32, kind="ExternalOutput").ap()
    G.P.dma("sp", out=t, in_=ap, reads=[key], writes=[("dbg", name)])


def rw_shift(G, l, x, xk, ti, p0=0, p1=128):
    P, nc = G.P, G.nc
    V = nc.vector
    tm, tk = G.rwtmp.next()
    for (a, b_) in ((0, CTX), (CTX, TT)):
        P.op("dve", lambda: V.memset(tm[p0:p1, a:a + 1], 0.0), [], [tk])
        P.op("dve", lambda: V.tensor_copy(out=tm[p0:p1, a + 1:b_], in_=x[p0:p1, a:b_ - 1]), [xk], [tk])
        P.op("pool", lambda: nc.gpsimd.tensor_tensor(out=tm[p0:p1, a:b_ - 1], in0=tm[p0:p1, a:b_ - 1], in1=x[p0:p1, a + 1:b_], op=ALU.add), [xk, tk], [tk])
    P.op("dve", lambda: V.tensor_scalar(out=tm[p0:p1, :], in0=tm[p0:p1, :], scalar1=G.rwhmu[p0:p1, l * 9 + ti:l * 9 + ti + 1], scalar2=None, op0=ALU.mult),
         [tk, "rwhmu"], [tk])
    P.op("dve", lambda: V.scalar_tensor_tensor(out=x[p0:p1, :], in0=x[p0:p1, :], scalar=G.rwomm[p0:p1, l * 9 + ti:l * 9 + ti + 1], in1=tm[p0:p1, :],
                                               op0=ALU.mult, op1=ALU.add), [xk, tk, "rwomm"], [xk])


def emit_rwkv(G, l):
    P, nc = G.P, G.nc
    V = nc.vector
    T9 = lambda n: P.sb("rw_" + n, [128, TT])
    ident = P.sb("rw_ident", [128, 128])
    P.dma("sp", out=ident[:], in_=G.ident_d, reads=[], writes=["ident"])
    Mk = P.sb("rw_M", [128, 2, 768])
    P.dma("sp", out=Mk[:], in_=G.rwM[:, :, 512:1280], reads=[], writes=["rwM"])
    Mk2 = P.sb("rw_M2", [128, 2, 4, 256])
    P.dma("sp", out=Mk2[:], in_=G.rwM2, reads=[], writes=["rwM2"])
    Wt = P.sb("rw_W", [128, 2, 256])
    P.dma("sp", out=Wt[:], in_=G.rwW[l], reads=[], writes=["rwW"])
    Gt = P.sb("rw_G", [128, 2, 256])
    P.dma("sp", out=Gt[:], in_=G.rwG[l], reads=[], writes=["rwG"])
    onesB = P.sb("rw_onesB", [128, 128])
    P.op("dve", lambda: V.tensor_scalar(out=onesB[:], in0=G.onesH[:], scalar1=64.0, scalar2=None, op0=ALU.mult), ["onesH"], ["onesB"])
    o_mu = G.off["rw_mu"][0]
    G.rwhmu = P.sb("rw_hmu", [128, NL * 9])
    G.rwomm = P.sb("rw_omm", [128, NL * 9])
    P.op("dve", lambda: V.tensor_scalar(out=G.rwhmu[:], in0=G.pp[:, o_mu:o_mu + NL * 9], scalar1=0.5, scalar2=None, op0=ALU.mult), ["pp"], ["rwhmu"])
    P.op("dve", lambda: V.tensor_scalar(out=G.rwomm[:], in0=G.pp[:, o_mu:o_mu + NL * 9], scalar1=-1.0, scalar2=1.0, op0=ALU.mult, op1=ALU.add), ["pp"], ["rwomm"])
    omka = P.sb("rw_omka", [128, 2])
    o_ka = G.off["rw_ka"][0] + l * 2
    P.op("dve", lambda: V.tensor_scalar(out=omka[:], in0=G.pp[:, o_ka:o_ka + 2], scalar1=-1.0, scalar2=1.0, op0=ALU.mult, op1=ALU.add), ["pp"], ["omka"])
    G.tmp = Pool_(P, "tmp", [128, 512], 2)
    wa, th, gdA = T9("wa"), T9("th"), T9("gdA")
    gdB = wa
    r, k, v, kk = T9("r"), T9("k"), T9("v"), T9("kk")
    L1, A1, K1, B1 = T9("L1"), T9("A1"), T9("K1"), T9("B1")
    T2 = T9("T2")
    L2, kL2 = th, "th"
    T1, kT1 = gdA, "gdA"
    yacc, bonus = T9("yacc"), T9("bonus")

    class OneTile:
        def next(self_):
            return yacc, "yacc"
    G.rwtmp = OneTile()
    vtok = P.sb("rw_vtok", [128, 18, 128])
    iap = ident[:, 0:1]
    I4 = bass.AP(iap.tensor, iap.offset, [list(iap.ap[0]), [0, 4], [1, 128]])

    def v4(ap):
        return ap.rearrange("p (a h s) -> p a h s", a=2, h=2)

    def v44(ap):
        return ap.rearrange("p (a s) -> p a s", a=4)

    def m4(d, i):
        ap = Mk2[:, d, i, 0:1]
        return bass.AP(ap.tensor, ap.offset, [list(ap.ap[0]), [128, 2], [0, 2], [1, 128]])

    def quad(ps, X, Y, xk, yk, psk, need_n=True, need_t=True):
        for h in range(2):
            n_ = slice(h * 128, (h + 1) * 128)
            t_ = slice(256 + h * 128, 256 + (h + 1) * 128)
            if need_n:
                P.op("pe", lambda: nc.tensor.matmul(ps[:, n_], lhsT=X[:, t_], rhs=Y[:, n_], start=True, stop=True), [xk, yk], [psk])
            if need_t:
                P.op("pe", lambda: nc.tensor.matmul(ps[:, t_], lhsT=Y[:, n_], rhs=X[:, t_], start=True, stop=True), [xk, yk], [psk])

    ST = []
    for d in range(2):
        st = dict(d=d)
        for nm, shp in (("Nb", [128, 512]), ("Nq", [128, 512]), ("No0", [128, 512]), ("No1", [128, 512]), ("No2", [128, 512]), ("Tm", [128, 512]),
                        ("WV", [128, 512]), ("scB", [128, 512]), ("scC", [128, 256]), ("ktok", [128, 128]), ("btok", [128, 128]), ("Xs", [128, 128]),
                        ("S0", [128, 64]), ("gC", [128, 18])):
            st[nm] = P.sb("rw%d_%s" % (d, nm), shp)
        ST.append(st)
    ST[0].update(RT=L1, KT=A1, KH=K1, BH=B1, kRT="L1", kKT="A1", kKH="K1", kBH="B1")
    ST[1].update(RT=r, KT=kk, KH=k, BH=T2, kRT="r", kKT="kk", kKH="k", kBH="T2")

    def proj(d, hp, ld_t, ld_k, a_t, a_k):
        hs = slice(hp * 128, (hp + 1) * 128)
        for (t0, n) in TBLK:
            ps, pk_ = P.bank()
            P.op("pe", lambda: nc.tensor.matmul(ps[:, :n], lhsT=Wt[0:64, d, hs], rhs=th[0:64, t0:t0 + n], start=True, stop=True), ["rwW", "th"], [pk_])
            o_ = G.off["rw_w0"][0] + l * 4 + d * 2 + hp
            P.op("act", lambda: nc.scalar.activation(out=ld_t[:, t0:t0 + n], in_=ps[:, :n], func=AF.Sigmoid, bias=G.pp[:, o_:o_ + 1]), [pk_, "pp"], [ld_k])
            ps2, pk2 = P.bank()
            P.op("pe", lambda: nc.tensor.matmul(ps2[:, :n], lhsT=Wt[64:128, d, hs], rhs=wa[64:128, t0:t0 + n], start=True, stop=True), ["rwW", "wa"], [pk2])
            o2 = G.off["rw_a0"][0] + l * 4 + d * 2 + hp
            P.op("act", lambda: nc.scalar.activation(out=a_t[:, t0:t0 + n], in_=ps2[:, :n], func=AF.Sigmoid, bias=G.pp[:, o2:o2 + 1]), [pk2, "pp"], [a_k])
        P.op("dve", lambda: V.tensor_scalar(out=ld_t[:], in0=ld_t[:], scalar1=-0.6065306597126334, scalar2=None, op0=ALU.mult), [ld_k], [ld_k])

    def add_bonus(d, hp, kd_t, kd_k):
        tmf, tmk = G.rwtmp.next()
        P.op("dve", lambda: V.scalar_tensor_tensor(out=tmf[:], in0=r[:], scalar=ppcol(G, "rw_rk", l * 2 + hp), in1=kd_t[:], op0=ALU.mult, op1=ALU.mult),
             ["r", kd_k, "pp"], [tmk])
        for (t0, n) in TBLK:
            ps, pk_ = P.bank()
            P.op("pe", lambda: nc.tensor.matmul(ps[:, :n], lhsT=onesB[:], rhs=tmf[:, t0:t0 + n], start=True, stop=True), [tmk, "onesB"], [pk_])
            if d == 0:
                P.op("dve", lambda: V.tensor_tensor(out=bonus[:, t0:t0 + n], in0=ps[:, :n], in1=v[:, t0:t0 + n], op=ALU.mult), [pk_, "v"], ["bonus"])
            else:
                sq, sqk = G.tmp.next()
                P.op("dve", lambda: V.tensor_tensor(out=sq[:, :n], in0=ps[:, :n], in1=v[:, t0:t0 + n], op=ALU.mult), [pk_, "v"], [sqk])
                P.op("pool", lambda: nc.gpsimd.tensor_tensor(out=bonus[:, t0:t0 + n], in0=bonus[:, t0:t0 + n], in1=sq[:, :n], op=ALU.add), ["bonus", sqk], ["bonus"])

    def cumlog(d, ld_t, ld_k, cl_t, cl_k):
        for c in range(18):
            cs = slice(c * 128, (c + 1) * 128)
            if d == 0:
                P.op("dve", lambda: V.tensor_tensor_scan(out=cl_t[:, cs], data0=G.ones9[:, 0:128], data1=ld_t[:, cs], initial=0.0, op0=ALU.mult, op1=ALU.add),
                     [ld_k, "ones9"], [cl_k])
            else:
                P.op("dve", lambda: V.tensor_tensor_scan(out=rev(cl_t, c * 128, (c + 1) * 128), data0=G.ones9[:, 0:128], data1=rev(ld_t, c * 128, (c + 1) * 128),
                                                         initial=0.0, op0=ALU.mult, op1=ALU.add), [ld_k, "ones9"], [cl_k])

    def save_gc(d, er_t, er_k):
        ap = er_t[:, (127 if d == 0 else 0):(127 if d == 0 else 0) + 1]
        P.op("dve", lambda: V.tensor_copy(out=ST[d]["gC"][:], in_=bass.AP(ap.tensor, ap.offset, [list(ap.ap[0]), [128, 18]])), [er_k], ["gC%d" % d])

    def chunk_gen(st):
        d = st["d"]
        kt_, bh_, kh_, rt_ = st["KT"], st["BH"], st["KH"], st["RT"]
        kKT, kBH, kKH, kRT = st["kKT"], st["kBH"], st["kKH"], st["kRT"]
        K_ = lambda n: "%s%d" % (n, d)
        Nb, Nq, Tm, WV, scB, scC, ktok, btok, Xs, S0 = (st[n] for n in ("Nb", "Nq", "Tm", "WV", "scB", "scC", "ktok", "btok", "Xs", "S0"))
        No = [st["No0"], st["No1"], st["No2"]]
        P.op("dve", lambda: V.memset(S0[:], 0.0), [], [K_("S0")])
        order = list(range(18)) if d == 0 else [1, 0] + list(range(17, 1, -1))
        if d in G.cfg.get("skipdir", ()):
            order = []
        for c in order:
            cs = slice(c * 128, (c + 1) * 128)
            pA, pAk = P.bank()
            pB, pBk = P.bank()
            pC, pCk = P.bank()
            for h in range(2):
                hb = slice(64 * h, 64 * h + 64)
                mm = lambda out, lt, rh, ks, ok: P.op("pe", lambda: nc.tensor.matmul(out, lhsT=lt[hb, cs], rhs=rh[hb, cs], start=True, stop=True), ks, [ok])
                mm(pA[:, h * 128:(h + 1) * 128], kt_, bh_, [kKT, kBH], pAk)
                mm(pA[:, 256 + h * 128:256 + (h + 1) * 128], bh_, kt_, [kKT, kBH], pAk)
                mm(pB[:, h * 128:(h + 1) * 128], kh_, kt_, [kKH, kKT], pBk)
                mm(pB[:, 256 + h * 128:256 + (h + 1) * 128], kh_, rt_, [kKH, kRT], pBk)
                mm(pC[:, h * 128:(h + 1) * 128], bh_, rt_, [kBH, kRT], pCk)
            yield
            P.op("dve", lambda: V.tensor_tensor(out=v4(Nb[:]), in0=v4(pA[:]), in1=m4(d, 0), op=ALU.mult), [pAk, "rwM2"], [K_("Nb")])
            for i_ in range(3):
                P.op("dve" if i_ != 1 else "pool", (lambda: V.tensor_tensor(out=v4(No[i_][:]), in0=v4(pA[:]), in1=m4(d, i_ + 1), op=ALU.mult)) if i_ != 1 else
                     (lambda: V.tensor_tensor(out=v4(No[i_][:]), in0=v4(pA[:]), in1=m4(d, i_ + 1), op=ALU.mult)), [pAk, "rwM2"], [K_("No%d" % i_)]) if False else \
                    P.op("dve", lambda: V.tensor_tensor(out=v4(No[i_][:]), in0=v4(pA[:]), in1=m4(d, i_ + 1), op=ALU.mult), [pAk, "rwM2"], [K_("No%d" % i_)])
            P.op("dve", lambda: V.tensor_tensor(out=scB[:], in0=pB[:], in1=Mk[:, d, 0:512], op=ALU.mult), [pBk, "rwM"], [K_("scB")])
            P.op("dve", lambda: V.tensor_tensor(out=scC[:], in0=pC[:, 0:256], in1=Mk[:, d, 512:768], op=ALU.mult), [pCk, "rwM"], [K_("scC")])
            pt, ptk = P.bank()
            P.op("pe", lambda: nc.tensor.transpose(out=pt[:, 0:128], in_=kh_[:, cs], identity=ident[:]), [kKH, "ident"], [ptk])
            P.op("pe", lambda: nc.tensor.transpose(out=pt[:, 128:256], in_=bh_[:, cs], identity=ident[:]), [kBH, "ident"], [ptk])
            P.op("act", lambda: nc.scalar.copy(out=ktok[:], in_=pt[:, 0:128]), [ptk], [K_("ktok")])
            P.op("act", lambda: nc.scalar.mul(out=btok[:], in_=pt[:, 128:256], mul=-1.0), [ptk], [K_("btok")])
            yield
            P.op("dve", lambda: V.tensor_tensor(out=v44(Tm[:]), in0=v44(Nb[:]), in1=I4, op=ALU.add), [K_("Nb"), "ident"], [K_("Tm")])
            cur, curk = Nb, K_("Nb")
            for j in range(3):
                pq, pqk = P.bank()
                quad(pq, cur, cur, curk, curk, pqk, need_n=(j < 2))
                yield
                if j < 2:
                    P.op("act", lambda: nc.scalar.copy(out=Nq[:], in_=pq[:]), [pqk], [K_("Nq")])
                else:
                    P.op("act", lambda: nc.scalar.copy(out=Nq[:, 256:512], in_=pq[:, 256:512]), [pqk], [K_("Nq")])
                pw, pwk = P.bank()
                quad(pw, Nq, Tm, K_("Nq"), K_("Tm"), pwk)
                yield
                P.op("dve", lambda: V.tensor_tensor(out=Tm[:], in0=Tm[:], in1=pw[:], op=ALU.add), [K_("Tm"), pwk], [K_("Tm")])
                cur, curk = Nq, K_("Nq")
            for i_ in range(3):
                pw, pwk = P.bank()
                quad(pw, No[i_], Tm, K_("No%d" % i_), K_("Tm"), pwk, need_t=False)
                yield
                P.op("act", lambda: nc.scalar.copy(out=WV[:, 0:256], in_=pw[:, 0:256]), [pwk], [K_("WV")])
                pw2, pwk2 = P.bank()
                last_ = (i_ == 2)
                quad(pw2, Tm, WV, K_("Tm"), K_("WV"), pwk2, need_n=not last_)
                yield
                if last_:
                    P.op("dve", lambda: V.tensor_tensor(out=Tm[:, 256:512], in0=Tm[:, 256:512], in1=pw2[:, 256:512], op=ALU.add), [K_("Tm"), pwk2], [K_("Tm")])
                else:
                    P.op("dve", lambda: V.tensor_tensor(out=Tm[:], in0=Tm[:], in1=pw2[:], op=ALU.add), [K_("Tm"), pwk2], [K_("Tm")])
            px, pxk = P.bank()
            for h in range(2):
                hb = slice(64 * h, 64 * h + 64)
                P.op("pe", lambda: nc.tensor.matmul(px[:, h * 64:(h + 1) * 64], lhsT=kt_[hb, cs], rhs=S0[hb, :], start=True, stop=False), [kKT, K_("S0")], [pxk])
                P.op("pe", lambda: nc.tensor.matmul(px[:, h * 64:(h + 1) * 64], lhsT=scB[:, h * 128:(h + 1) * 128], rhs=vtok[:, c, h * 64:(h + 1) * 64], start=False, stop=True),
                     [K_("scB"), "vtok"], [pxk])
            yield
            P.op("dve", lambda: V.tensor_copy(out=Xs[:], in_=px[:, 0:128]), [pxk], [K_("Xs")])
            pz, pzk = P.bank()
            for h in range(2):
                P.op("pe", lambda: nc.tensor.matmul(pz[:, h * 64:(h + 1) * 64], lhsT=Tm[:, 256 + h * 128:256 + (h + 1) * 128], rhs=Xs[:, h * 64:(h + 1) * 64],
                                                    start=True, stop=True), [K_("Tm"), K_("Xs")], [pzk])
            yield
            P.op("dve", lambda: V.tensor_copy(out=Xs[:], in_=pz[:, 0:128]), [pzk], [K_("Xs")])
            py, pyk = P.bank()
            for h in range(2):
                hb = slice(64 * h, 64 * h + 64)
                P.op("pe", lambda: nc.tensor.matmul(py[hb, 0:128], lhsT=S0[hb, :], rhs=rt_[hb, cs], start=True, stop=False), [K_("S0"), kRT], [pyk])
                P.op("pe", lambda: nc.tensor.matmul(py[hb, 0:128], lhsT=vtok[:, c, h * 64:(h + 1) * 64], rhs=scB[:, 256 + h * 128:256 + (h + 1) * 128], start=False, stop=False),
                     ["vtok", K_("scB")], [pyk])
                P.op("pe", lambda: nc.tensor.matmul(py[hb, 0:128], lhsT=Xs[:, h * 64:(h + 1) * 64], rhs=scC[:, h * 128:(h + 1) * 128], start=False, stop=True),
                     [K_("Xs"), K_("scC")], [pyk])
            pS, pSk = P.bank()
            for h in range(2):
                hb = slice(64 * h, 64 * h + 64)
                P.op("pe", lambda: nc.tensor.matmul(pS[hb, 0:64], lhsT=ktok[:, hb], rhs=vtok[:, c, h * 64:(h + 1) * 64], start=True, stop=False), [K_("ktok"), "vtok"], [pSk])
                P.op("pe", lambda: nc.tensor.matmul(pS[hb, 0:64], lhsT=btok[:, hb], rhs=Xs[:, h * 64:(h + 1) * 64], start=False, stop=True), [K_("btok"), K_("Xs")], [pSk])
            yield
            P.op("dve", lambda: V.tensor_tensor(out=yacc[:, cs], in0=yacc[:, cs], in1=py[:, 0:128], op=ALU.add), [pyk, ("yacc", c)], [("yacc", c)])
            gC = st["gC"][:, c:c + 1]
            P.op("dve", lambda: V.tensor_scalar(out=S0[:], in0=S0[:], scalar1=gC, scalar2=None, op0=ALU.mult), [K_("S0"), K_("gC")], [K_("S0")])
            P.op("dve", lambda: V.scalar_tensor_tensor(out=S0[:], in0=pS[:, 0:64], scalar=gC, in1=S0[:], op0=ALU.mult, op1=ALU.add), [pSk, K_("gC"), K_("S0")], [K_("S0")])
            yield

    yall = [("yacc", c) for c in range(18)]
    for b in range(NB):
        for hp in range(2):
            hs = slice(hp * 128, (hp + 1) * 128)
            if True:
                P.dma("sp", out=wa[:], in_=G.pT[b, 768:896, :], reads=[], writes=["wa"])
                rw_shift(G, l, wa, "wa", 6)
                P.op("act", lambda: nc.scalar.activation(out=th[0:64, :], in_=wa[0:64, :], func=AF.Tanh), ["wa"], ["th"])
            for ti, (t_, nm) in enumerate(((r, "r"), (k, "k"), (v, "v"))):
                P.dma("sp", out=t_[:], in_=G.pT[b, ti * 256 + hp * 128:ti * 256 + (hp + 1) * 128, :], reads=[], writes=[nm])
                rw_shift(G, l, t_, nm, ti * 2 + hp)
            P.op("dve", lambda: V.tensor_scalar(out=kk[:], in0=k[:], scalar1=ppcol(G, "rw_kk", l * 2 + hp), scalar2=None, op0=ALU.mult), ["k", "pp"], ["kk"])
            for (t0, n) in TBLK:
                sq, sqk = G.tmp.next()
                P.op("act", lambda: nc.scalar.activation(out=sq[:, :n], in_=kk[:, t0:t0 + n], func=AF.Square), ["kk"], [sqk])
                ps, pk_ = P.bank()
                P.op("pe", lambda: nc.tensor.matmul(ps[:, :n], lhsT=onesB[:], rhs=sq[:, :n], start=True, stop=True), [sqk, "onesB"], [pk_])
                P.op("dve", lambda: V.tensor_scalar(out=sq[:, :n], in0=ps[:, :n], scalar1=1e-12, scalar2=None, op0=ALU.max), [pk_], [sqk])
                P.op("act", lambda: nc.scalar.activation(out=sq[:, :n], in_=sq[:, :n], func=AF.Sqrt), [sqk], [sqk])
                P.op("dve", lambda: V.reciprocal(out=sq[:, :n], in_=sq[:, :n]), [sqk], [sqk])
                P.op("dve", lambda: V.tensor_tensor(out=kk[:, t0:t0 + n], in0=kk[:, t0:t0 + n], in1=sq[:, :n], op=ALU.mult), ["kk", sqk], ["kk"])
            for c in range(18):
                pt, ptk = P.bank()
                P.op("pe", lambda: nc.tensor.transpose(out=pt[:, 0:128], in_=v[:, c * 128:(c + 1) * 128], identity=ident[:]), ["v", "ident"], [ptk])
                P.op("act", lambda: nc.scalar.copy(out=vtok[:, c, :], in_=pt[:, 0:128]), [ptk], ["vtok"])
            proj(0, hp, L1, "L1", A1, "A1")
            proj(1, hp, T1, kT1, T2, "T2")
            P.op("dve", lambda: V.tensor_scalar(out=K1[:], in0=A1[:], scalar1=ppcol(G, "rw_ka", l * 2 + hp), scalar2=omka[:, hp:hp + 1], op0=ALU.mult, op1=ALU.add),
                 ["A1", "pp", "omka"], ["K1"])
            P.op("pool", lambda: nc.gpsimd.tensor_tensor(out=K1[:], in0=K1[:], in1=k[:], op=ALU.mult), ["K1", "k"], ["K1"])
            P.op("dve", lambda: V.tensor_tensor(out=B1[:], in0=kk[:], in1=A1[:], op=ALU.mult), ["kk", "A1"], ["B1"])
            add_bonus(0, hp, K1, "K1")
            cumlog(0, L1, "L1", L2, kL2)
            P.op("dve", lambda: V.tensor_tensor(out=A1[:], in0=L2[:], in1=L1[:], op=ALU.subtract), [kL2, "L1"], ["A1"])
            P.op("act", lambda: nc.scalar.activation(out=A1[:], in_=A1[:], func=AF.Exp), ["A1"], ["A1"])
            P.op("dve", lambda: V.tensor_tensor(out=A1[:], in0=A1[:], in1=kk[:], op=ALU.mult), ["A1", "kk"], ["A1"])
            P.op("act", lambda: nc.scalar.activation(out=L1[:], in_=L2[:], func=AF.Exp, scale=-1.0), [kL2], ["L1"])
            P.op("pool", lambda: nc.gpsimd.tensor_tensor(out=B1[:], in0=B1[:], in1=L1[:], op=ALU.mult), ["B1", "L1"], ["B1"])
            P.op("dve", lambda: V.tensor_tensor(out=K1[:], in0=K1[:], in1=L1[:], op=ALU.mult), ["K1", "L1"], ["K1"])
            P.op("act", lambda: nc.scalar.activation(out=L2[:], in_=L2[:], func=AF.Exp), [kL2], [kL2])
            P.op("dve", lambda: V.tensor_tensor(out=L1[:], in0=r[:], in1=L2[:], op=ALU.mult), ["r", kL2, "B1", "K1"], ["L1"])
            save_gc(0, L2, kL2)
            P.op("dve", lambda: V.tensor_scalar(out=L2[:], in0=T2[:], scalar1=ppcol(G, "rw_ka", l * 2 + hp), scalar2=omka[:, hp:hp + 1], op0=ALU.mult, op1=ALU.add),
                 ["T2", "pp", "omka", "gC0"], [kL2])
            P.op("pool", lambda: nc.gpsimd.tensor_tensor(out=k[:], in0=k[:], in1=L2[:], op=ALU.mult), [kL2, "k", "K1"], ["k"])
            P.op("dve", lambda: V.tensor_tensor(out=T2[:], in0=kk[:], in1=T2[:], op=ALU.mult), ["kk", "T2"], ["T2"])
            add_bonus(1, hp, k, "k")
            cumlog(1, T1, kT1, L2, kL2)
            P.op("dve", lambda: V.tensor_tensor(out=v[:], in0=L2[:], in1=T1[:], op=ALU.subtract), [kL2, kT1, "bonus", "vtok"], ["v"])
            P.op("act", lambda: nc.scalar.activation(out=v[:], in_=v[:], func=AF.Exp), ["v"], ["v"])
            P.op("dve", lambda: V.tensor_tensor(out=kk[:], in0=kk[:], in1=v[:], op=ALU.mult), ["kk", "v", "A1", "B1"], ["kk"])
            P.op("act", lambda: nc.scalar.activation(out=v[:], in_=L2[:], func=AF.Exp, scale=-1.0), [kL2, "kk"], ["v"])
            P.op("pool", lambda: nc.gpsimd.tensor_tensor(out=T2[:], in0=T2[:], in1=v[:], op=ALU.mult), ["T2", "v"], ["T2"])
            P.op("dve", lambda: V.tensor_tensor(out=k[:], in0=k[:], in1=v[:], op=ALU.mult), ["k", "v"], ["k"])
            P.op("act", lambda: nc.scalar.activation(out=L2[:], in_=L2[:], func=AF.Exp), [kL2], [kL2])
            P.op("dve", lambda: V.tensor_tensor(out=r[:], in0=r[:], in1=L2[:], op=ALU.mult), ["r", kL2, "L1"], ["r"])
            save_gc(1, L2, kL2)
            P.dma("sp", out=gdA[:], in_=G.pT[b, 896:1024, :], reads=[], writes=["gdA"])
            rw_shift(G, l, gdA, "gdA", 7)
            P.dma("sp", out=gdB[64:96, :], in_=G.pT[b, 1024:1056, :], reads=[], writes=["wa"])
            rw_shift(G, l, gdB, "wa", 8, 64, 96)
            P.op("act", lambda: nc.scalar.activation(out=gdA[:], in_=gdA[:], func=AF.Sigmoid), ["gdA"], ["gdA"])
            P.op("act", lambda: nc.scalar.activation(out=gdB[64:96, :], in_=gdB[64:96, :], func=AF.Sigmoid), ["wa"], ["wa"])
            P.op("dve", lambda: V.memset(yacc[:], 0.0), ["yacc"], yall + ["yacc"])
            live = [chunk_gen(ST[0]), chunk_gen(ST[1])]
            while live:
                for g_ in list(live):
                    try:
                        next(g_)
                    except StopIteration:
                        live.remove(g_)
            P.op("dve", lambda: V.tensor_copy(out=yacc[:, 0:1], in_=yacc[:, 0:1]), yall, yall + ["yacc"])
            dbg(G, "yraw", yacc[:], "yacc", [128, TT]); dbg(G, "bonus", bonus[:], "bonus", [128, TT])
            for (t0, n) in TBLK:
                head_norm_blk(G, yacc[:, t0:t0 + n], "yacc", n, True, 1)
                P.op("dve", lambda: V.scalar_tensor_tensor(out=yacc[:, t0:t0 + n], in0=yacc[:, t0:t0 + n], scalar=ppcol(G, "rw_gn", l * 2 + hp), in1=bonus[:, t0:t0 + n],
                                                           op0=ALU.mult, op1=ALU.add), ["yacc", "bonus", "pp"], ["yacc"])
                pg, pgk = P.bank()
                P.op("pe", lambda: nc.tensor.matmul(pg[:, :n], lhsT=Gt[:, 0, hs], rhs=gdA[:, t0:t0 + n], start=True, stop=False), ["rwG", "gdA"], [pgk])
                P.op("pe", lambda: nc.tensor.matmul(pg[:, :n], lhsT=Gt[64:96, 1, hs], rhs=gdB[64:96, t0:t0 + n], start=False, stop=True), ["rwG", "wa"], [pgk])
                P.op("dve", lambda: V.tensor_tensor(out=yacc[:, t0:t0 + n], in0=yacc[:, t0:t0 + n], in1=pg[:, :n], op=ALU.mult), ["yacc", pgk], ["yacc"])
            P.dma("pool", out=G.yT[b, hp * 128:(hp + 1) * 128, :], in_=yacc[:], reads=["yacc"], writes=[("yT", b, 0, hp)] + yall)


def emit_mixers(G, l, which=("hgrn", "ret", "s5", "rwkv")):
    P = G.P
    if "rwkv" in which:
        with P.phase():
            emit_rwkv(G, l)
    if "s5" in which:
        with P.phase():
            emit_s5(G, l)
    if "ret" in which or "hgrn" in which:
        with P.phase():
            gla2_tiles(G)
            if "ret" in which:
                G.rope = P.sb("rope", [128, 2, SEQ])
                P.dma("sp", out=G.rope[:], in_=G.rope_d, reads=[], writes=["rope"])
                G.perm = P.sb("perm", [128, 128])
                P.dma("sp", out=G.perm[:], in_=G.perm_d, reads=[], writes=["perm"])
                for b in range(NB):
                    emit_ret2(G, l, b)
            if "hgrn" in which:
                for b in range(NB):
                    emit_hgrn2(G, l, b)


def host_inputs(inp):
    x, ctx, c, c_ctx = inp["x"], inp["ctx"], inp["c"], inp["c_ctx"]
    w = host_weights(inp)
    pp = pack_params(inp).build()
    rope, perm = rope_tables()
    maps = []
    for core in range(NCORES):
        bs = slice(core * NB, (core + 1) * NB)
        xin = np.concatenate([ctx[bs], x[bs]], axis=1).transpose(0, 2, 1)
        crow = np.concatenate([c[bs], c_ctx[None]], axis=0)
        cT = fm(crow).transpose(0, 2, 1)
        m = dict(xin=np.ascontiguousarray(xin, np.float32), cT=np.ascontiguousarray(cT, np.float32), pp=pp, rope=rope, perm=perm)
        m.update(w)
        maps.append(m)
    return maps


def kernel(**inputs):
    inp = {k: np.asarray(v) for k, v in inputs.items()}
    nc = build_program({})
    maps = host_inputs(inp)
    res = run_bass_kernel_spmd(nc, maps, core_ids=list(range(NCORES)))
    outs = [r["out"] for r in res.results]
    y = np.concatenate(outs, axis=0).transpose(0, 2, 1)
    return np.ascontiguousarray(y, np.float32)
```

```python
import contextlib
import numpy as np
import concourse.bass as bass
import concourse.mybir as mybir
from concourse.bass_utils import run_bass_kernel_spmd

F32 = mybir.dt.float32
ALU = mybir.AluOpType
AF = mybir.ActivationFunctionType

D = 1024
DC = 8
CTX = 256
SEQ = 2048
TT = CTX + SEQ
NB = 2
DFF = 2816
FC = 22
NL = 4
PIN = 3616
PINC = 29
ALPHA = float(8.0 ** 0.25)
LN_EPS = 1e-5
NCORES = 8
NDS = 24
STORE_DEFER = 4


class Prog:
    def __init__(self, nc, es):
        self.nc = nc
        self.es = es
        self.eng = dict(pe=nc.tensor, dve=nc.vector, act=nc.scalar, pool=nc.gpsimd, sp=nc.sync)
        self.sem = {k: es.enter_context(nc.semaphore("s_" + k)) for k in ("pe", "dve", "act", "pool")}
        self.cnt = {k: 0 for k in self.sem}
        self.dsem = [es.enter_context(nc.semaphore("d%d" % i)) for i in range(NDS)]
        self.dval = [0] * NDS
        self.dnext = 0
        self.seen = {e: {} for e in self.eng}
        self.lw = {}
        self.rd = {}
        self.banks = [es.enter_context(nc.psum_tensor("bank%d" % i, [128, 512], F32)) for i in range(8)]
        self.bnext = 0
        self.scopes = [es]

    def sb(self, name, shape):
        self.uid = getattr(self, "uid", 0) + 1
        return self.scopes[-1].enter_context(self.nc.sbuf_tensor("sb%d_%s" % (self.uid, name), list(shape), F32))

    @contextlib.contextmanager
    def phase(self):
        st = contextlib.ExitStack()
        self.scopes.append(st)
        try:
            yield
        finally:
            self.barrier()
            self.scopes.pop()
            st.close()

    def barrier(self):
        self.flush_stores()
        for e in self.eng:
            for e2 in self.sem:
                if e2 != e and self.cnt[e2] > 0:
                    self._wait(e, (e2, self.cnt[e2]))
            for j in range(NDS):
                if self.dval[j] > 0:
                    self._wait(e, (("d", j), self.dval[j]))

    def bank(self):
        fr = getattr(self, "free_banks", None)
        if fr is None:
            fr = self.free_banks = list(range(8))
        self.bnext = (self.bnext + 1) % len(fr)
        i = fr[self.bnext]
        return self.banks[i], ("bank", i)

    def reserve(self, n):
        self.bank()
        out = [self.free_banks.pop() for _ in range(n)]
        return [(self.banks[i], ("bank", i)) for i in out]

    def release(self):
        self.free_banks = list(range(8))

    def _semof(self, k):
        return self.dsem[k[1]] if isinstance(k, tuple) else self.sem[k]

    def _wait(self, e, dep):
        k, v = dep
        if self.seen[e].get(k, 0) >= v:
            return
        self.seen[e][k] = v
        self.eng[e].wait_ge(self._semof(k), v)

    def _deps(self, e, reads, writes):
        deps = set()
        for r in reads:
            if r in self.lw:
                deps.add(self.lw[r])
        for w in writes:
            if w in self.lw:
                deps.add(self.lw[w])
            for d in self.rd.get(w, ()):
                deps.add(d)
        for d in deps:
            if e == "pe" and d[0] == "pe":
                continue
            self._wait(e, d)

    def _commit(self, me, reads, writes):
        for r in reads:
            lst = self.rd.setdefault(r, [])
            lst[:] = [d for d in lst if d[0] != me[0]]
            lst.append(me)
        for w in writes:
            self.lw[w] = me
            self.rd[w] = []

    def _flush_conflicts(self, reads, writes):
        pend = getattr(self, "pending", None)
        if not pend:
            return
        ws = set(writes)
        rs = set(reads) | ws
        for (o_, i_, r_, w_, age) in pend:
            if ws.intersection(r_) or rs.intersection(w_):
                self.flush_stores()
                return

    def flush_stores(self, min_age=0):
        pend = getattr(self, "pending", None)
        if not pend:
            return
        keep = []
        self.pending = []
        for it in pend:
            if it[4] >= min_age and not keep:
                self._dma_now("sp", it[0], it[1], it[2], it[3])
            else:
                keep.append(it)
        self.pending = keep + self.pending

    def op(self, e, fn, reads, writes):
        self._flush_conflicts(reads, writes)
        self._deps(e, reads, writes)
        ins = fn()
        self.cnt[e] += 1
        ins.then_inc(self.sem[e], 1)
        self._commit((e, self.cnt[e]), reads, writes)

    def dma(self, q, out, in_, reads, writes):
        if not hasattr(self, "pending"):
            self.pending = []
        self._flush_conflicts(reads, writes)
        if q == "pool":
            self.pending.append([out, in_, list(reads), list(writes), 0])
            return
        self._dma_now(q, out, in_, reads, writes)
        for it in self.pending:
            it[4] += 1
        self.flush_stores(min_age=STORE_DEFER)

    def _dma_now(self, q, out, in_, reads, writes):
        j = self.dnext
        self.dnext = (j + 1) % NDS
        if self.dval[j] > 0:
            self._wait(q, (("d", j), self.dval[j]))
        self._deps(q, reads, writes)
        self.dval[j] += 16
        self.eng[q].dma_start(out=out, in_=in_).then_inc(self.dsem[j], 16)
        self._commit((("d", j), self.dval[j]), reads, writes)

    def finish(self):
        self.flush_stores()
        for j in range(NDS):
            if self.dval[j] > 0:
                self.nc.sync.wait_ge(self.dsem[j], self.dval[j])
        for e in ("pe", "dve", "act", "pool"):
            if self.cnt[e] > 0:
                self.nc.sync.wait_ge(self.sem[e], self.cnt[e])


class Pool_:
    def __init__(self, P, name, shape, n):
        self.t = [P.sb("%s_%d" % (name, i), shape) for i in range(n)]
        self.k = [(name, i) for i in range(n)]
        self.i = 0

    def next(self):
        i = self.i
        self.i = (i + 1) % len(self.t)
        return self.t[i], self.k[i]


def allk(k, n=8):
    return [(k, c) for c in range(n)]


def bc(ap, n):
    a = ap.ap
    return bass.AP(ap.tensor, ap.offset, [list(a[0]), [0, n]])


def fm(v):
    v = np.asarray(v, np.float32)
    n = v.shape[-1] // 128
    v = v.reshape(v.shape[:-1] + (n, 128))
    return np.moveaxis(v, -1, 0)


class Pack:
    def __init__(self):
        self.cols = []
        self.off = {}
        self.n = 0

    def add(self, name, arr):
        arr = np.ascontiguousarray(arr, np.float32).reshape(128, -1)
        self.off[name] = (self.n, arr.shape[1])
        self.cols.append(arr)
        self.n += arr.shape[1]

    def build(self):
        return np.ascontiguousarray(np.concatenate(self.cols, axis=1))


def pack_layout():
    pk = Pack()
    z = lambda *s: np.zeros(s, np.float32)
    pk.add("b_ada", z(128, NL * 72))
    pk.add("ln_g", z(128, NL * 3 * 8))
    pk.add("ln_b", z(128, NL * 3 * 8))
    pk.add("b_gate", z(128, NL * 32))
    pk.add("hg_lbl", z(128, 2 * NL))
    pk.add("hg_fb", z(128, NL * 2 * 2))
    pk.add("hg_gn", z(128, NL * 2))
    pk.add("ret_gn", z(128, NL * 2))
    pk.add("ret_gam", z(128, 2))
    pk.add("ret_lg", z(128, 2))
    pk.add("rw_mu", z(128, NL * 9))
    pk.add("rw_w0", z(128, NL * 4))
    pk.add("rw_a0", z(128, NL * 4))
    pk.add("rw_kk", z(128, NL * 2))
    pk.add("rw_ka", z(128, NL * 2))
    pk.add("rw_rk", z(128, NL * 2))
    pk.add("rw_gn", z(128, NL * 2))
    pk.add("s5_lr", z(128, NL * 16))
    pk.add("s5_li", z(128, NL * 16))
    pk.add("s5_ldt", z(128, NL * 16))
    pk.add("s5_d", z(128, NL * 2))
    pk.add("s5_bglu", z(128, NL * 2))
    return pk


def pack_params(inp):
    pk = Pack()
    pk.add("b_ada", fm(inp["b_ada"]).reshape(128, -1))
    pk.add("ln_g", fm(inp["ln_g"]).reshape(128, -1))
    pk.add("ln_b", fm(inp["ln_b"]).reshape(128, -1))
    pk.add("b_gate", fm(inp["b_gate"]).reshape(128, -1))
    pk.add("hg_lbl", fm(inp["hgrn_lb_logits"]).transpose(0, 2, 1))
    pk.add("hg_fb", fm(inp["hgrn_f_bias"]))
    pk.add("hg_gn", fm(inp["hgrn_gn"]))
    pk.add("ret_gn", fm(inp["ret_gn"]))
    gam = 1.0 - 2.0 ** (-5.0 - np.arange(4, dtype=np.float32))
    pk.add("ret_gam", fm(np.repeat(gam, 64)[None])[:, 0, :])
    pk.add("ret_lg", fm(np.repeat(np.log(gam).astype(np.float32), 64)[None])[:, 0, :])
    mu = np.zeros((NL, 9, 128), np.float32)
    mu[:, :8] = inp["rwkv_mu"][:, :1024].reshape(NL, 8, 128)
    mu[:, 8, 64:96] = inp["rwkv_mu"][:, 1024:1056]
    pk.add("rw_mu", mu.transpose(2, 0, 1))
    pk.add("rw_w0", fm(inp["rwkv_w0"]))
    pk.add("rw_a0", fm(inp["rwkv_a0"]))
    pk.add("rw_kk", fm(inp["rwkv_kk"]))
    pk.add("rw_ka", fm(inp["rwkv_ka"]))
    pk.add("rw_rk", fm(inp["rwkv_rk"]))
    pk.add("rw_gn", fm(inp["rwkv_gn"]))
    pk.add("s5_lr", fm(inp["s5_lam_re"].reshape(NL, 2, 1024)))
    pk.add("s5_li", fm(inp["s5_lam_im"].reshape(NL, 2, 1024)))
    pk.add("s5_ldt", fm(np.repeat(inp["s5_log_dt"], 64, axis=-1)))
    pk.add("s5_d", fm(inp["s5_d"]))
    pk.add("s5_bglu", fm(inp["s5_b_glu"]))
    return pk


def host_weights(inp):
    w = {}
    f = lambda a: np.ascontiguousarray(a, np.float32)
    w1 = inp["ffn_w1"].reshape(NL, 2, 8, 128, FC, 128).transpose(0, 1, 4, 3, 2, 5)
    w3 = inp["ffn_w3"].reshape(NL, 2, 8, 128, FC, 128).transpose(0, 1, 4, 3, 2, 5)
    w["w13"] = f(np.stack([w1, w3], axis=4))
    w["w2"] = f(inp["ffn_w2"].reshape(NL, 2, FC, 128, 8, 128).transpose(0, 1, 4, 3, 2, 5))
    win = np.zeros((NL, D, PINC * 128), np.float32)
    win[:, :, :PIN] = inp["w_in"]
    w["win"] = f(win.reshape(NL, 8, 128, PINC, 128).transpose(0, 3, 2, 1, 4))
    w["wg"] = f(inp["w_gate"].reshape(NL, 8, 128, 32, 128).transpose(0, 3, 2, 1, 4))
    w["wb"] = f(inp["w_branch"].reshape(NL, 4, 2, 128, 8, 128).transpose(0, 1, 4, 3, 2, 5))
    w["wo"] = f(inp["w_out"].reshape(NL, 8, 128, 8, 128).transpose(0, 3, 2, 1, 4))
    w["wada"] = f(inp["w_ada"])
    B = np.zeros((NL, 128, 2, 8, 2, 128), np.float32)
    C = np.zeros((NL, 128, 2, 8, 2, 64), np.float32)
    for ri, (bn, cn) in enumerate((("s5_b_re", "s5_c_re"), ("s5_b_im", "s5_c_im"))):
        bsrc, csrc = inp[bn], inp[cn]
        for j in range(8):
            base = 32 * (j % 4)
            for gg in range(2):
                g = 2 * j + gg
                B[:, base + gg * 16:base + gg * 16 + 16, :, j, ri, gg * 64:(gg + 1) * 64] = bsrc[:, :, g].transpose(0, 3, 1, 2)
                co = 32 * (j % 2) + gg * 16
                C[:, gg * 64:(gg + 1) * 64, :, j, ri, co:co + 16] = csrc[:, :, g].transpose(0, 3, 1, 2)
    rwW = np.zeros((NL, 128, 2, 256), np.float32)
    rwW[:, 0:64] = inp["rwkv_w2"].transpose(0, 2, 1, 3)
    rwW[:, 64:128] = inp["rwkv_a2"].transpose(0, 2, 1, 3)
    w["rwW"] = rwW
    rwG = np.zeros((NL, 128, 2, 256), np.float32)
    rwG[:, :, 0] = inp["rwkv_g2"][:, 0:128]
    rwG[:, 64:96, 1] = inp["rwkv_g2"][:, 128:160]
    w["rwG"] = rwG
    p = np.arange(128)[:, None]
    fr = np.arange(128)[None, :]
    lo_s, up_s, lo_i, up_i = (fr < p), (fr > p), (fr <= p), (fr >= p)
    M = np.zeros((128, 2, 1280), np.float32)
    M[:, 0] = np.concatenate([-1.0 * lo_s, -1.0 * lo_s, -1.0 * up_s, -1.0 * up_s, up_s, up_s, up_i, up_i, -1.0 * up_i, -1.0 * up_i], axis=1)
    M[:, 1] = np.concatenate([-1.0 * up_s, -1.0 * up_s, -1.0 * lo_s, -1.0 * lo_s, lo_s, lo_s, lo_i, lo_i, -1.0 * lo_i, -1.0 * lo_i], axis=1)
    w["rwM"] = M
    tt = np.arange(128)[:, None]
    ss = np.arange(128)[None, :]
    M2 = np.zeros((128, 2, 4, 256), np.float32)
    for d in range(2):
        before = (ss < tt) if d == 0 else (ss > tt)
        pats = [before & ((tt // 16) == (ss // 16))]
        for kk_ in range(3):
            m = 16 * 2 ** kk_
            same = (tt // (2 * m)) == (ss // (2 * m))
            if d == 0:
                pats.append(same & ((tt % (2 * m)) >= m) & ((ss % (2 * m)) < m))
            else:
                pats.append(same & ((tt % (2 * m)) < m) & ((ss % (2 * m)) >= m))
        for i, pm in enumerate(pats):
            M2[:, d, i, 0:128] = -1.0 * pm
            M2[:, d, i, 128:256] = -1.0 * pm.T
    w["rwM2"] = M2
    w["ident"] = np.eye(128, dtype=np.float32)
    w["s5B"] = B
    w["s5C"] = C
    w["s5W"] = f(inp["s5_w_glu"].reshape(NL, 2, 128, 256).transpose(0, 2, 1, 3))
    return w


class Ctx:
    pass


def build_program(cfg):
    nc = bass.Bass("TRN2", target_bir_lowering=False)
    es = contextlib.ExitStack()
    G = Ctx()
    G.nc = nc
    G.cfg = cfg
    G.dbgdone = set()
    dump = cfg.get("dump", set())
    WL = 1 if cfg.get("small_w") else NL

    ext_in = cfg.get("ext_in", set())

    def dram(name, shape, kind=None):
        if kind is None:
            kind = "ExternalOutput" if name in dump else ("ExternalInput" if name in ext_in else "Internal")
        return nc.dram_tensor(name, list(shape), F32, kind=kind).ap()

    pkl = pack_layout()
    G.off = pkl.off
    G.xin = dram("xin", [NB, D, TT], "ExternalInput")
    G.cT = dram("cT", [128, 8, 3], "ExternalInput")
    G.pp_d = dram("pp", [128, pkl.n], "ExternalInput")
    G.wada = dram("wada", [WL, D, 9 * D], "ExternalInput")
    G.w13 = dram("w13", [WL, 2, FC, 128, 2, 8, 128], "ExternalInput")
    G.w2 = dram("w2", [WL, 2, 8, 128, FC, 128], "ExternalInput")
    G.win = dram("win", [WL, PINC, 128, 8, 128], "ExternalInput")
    G.wg = dram("wg", [WL, 32, 128, 8, 128], "ExternalInput")
    G.wb = dram("wb", [WL, 4, 8, 128, 2, 128], "ExternalInput")
    G.wo = dram("wo", [WL, 8, 128, 8, 128], "ExternalInput")
    G.rwW = dram("rwW", [NL, 128, 2, 256], "ExternalInput")
    G.rwG = dram("rwG", [NL, 128, 2, 256], "ExternalInput")
    G.rwM = dram("rwM", [128, 2, 1280], "ExternalInput")
    G.ident_d = dram("ident", [128, 128], "ExternalInput")
    G.rwM2 = dram("rwM2", [128, 2, 4, 256], "ExternalInput")
    G.s5B = dram("s5B", [NL, 128, 2, 8, 2, 128], "ExternalInput")
    G.s5C = dram("s5C", [NL, 128, 2, 8, 2, 64], "ExternalInput")
    G.s5W = dram("s5W", [NL, 128, 2, 256], "ExternalInput")
    G.rope_d = dram("rope", [128, 2, SEQ], "ExternalInput")
    G.perm_d = dram("perm", [128, 128], "ExternalInput")
    G.out = dram("out", [NB, D, SEQ], "ExternalOutput")
    G.xs = dram("xs", [NB, D, TT])
    G.pT = dram("pT", [NB, PINC * 128, TT])
    G.yT = dram("yT", [NB, D, TT])

    with es:
        P = Prog(nc, es)
        G.P = P
        setup_tiles(G)
        emit_all(G, cfg)
        P.finish()
    return nc


def setup_tiles(G):
    P = G.P
    nc = G.nc
    G.pp = P.sb("pp", [128, pack_layout().n])
    P.dma("sp", out=G.pp[:], in_=G.pp_d, reads=[], writes=["pp"])
    G.scT = P.sb("scT", [128, 8, 3])
    P.dma("sp", out=G.scT[:], in_=G.cT, reads=[], writes=["scT"])
    P.op("act", lambda: nc.scalar.activation(out=G.scT[:], in_=G.scT[:], func=AF.Silu), ["scT"], ["scT"])
    G.modT = P.sb("modT", [128, 72, 3])
    G.onesM = P.sb("onesM", [128, 128])
    P.op("dve", lambda: nc.vector.memset(G.onesM[:], 1.0 / D), [], ["onesM"])


def dense_tiles(G):
    P = G.P
    G.sT = Pool_(P, "sT", [128, 8, 512], 2)
    G.hT = Pool_(P, "hT", [128, 8, 512], 1)
    G.uT = Pool_(P, "uT", [128, 8, 512], 1)
    G.qT = Pool_(P, "qT", [128, 8, 512], 1)
    G.gT = P.sb("gT", [128, FC, 512])
    G.w13t = Pool_(P, "w13t", [128, 2, 8, 128], 3)
    G.w2t = Pool_(P, "w2t", [128, FC, 128], 2)
    G.wat = Pool_(P, "wat", [128, 8, 512], 1)
    G.tmp = Pool_(P, "tmp", [128, 512], 4)
    G.stat = Pool_(P, "stat", [128, 512], 2)


def merge_tiles(G):
    P = G.P
    G.sT = Pool_(P, "sT", [128, 8, 512], 2)
    G.hT = Pool_(P, "hT", [128, 8, 512], 1)
    G.uT = Pool_(P, "uT", [128, 8, 512], 1)
    G.qT = Pool_(P, "qT", [128, 8, 512], 2)
    G.aT = Pool_(P, "aT", [128, 8, 512], 1)
    G.w4t = Pool_(P, "w4t", [128, 8, 128], 4)
    G.wbt = Pool_(P, "wbt", [128, 2, 128], 3)
    G.tmp = Pool_(P, "tmp", [128, 512], 6)
    G.stat = Pool_(P, "stat", [128, 512], 2)


def ppcol(G, name, idx):
    o, n = G.off[name]
    return G.pp[:, o + idx:o + idx + 1]


def emit_ada(G, l):
    P, nc = G.P, G.nc
    for blk in range(18):
        wt, wk = G.wat.next()
        src = G.wada[l, :, blk * 512:(blk + 1) * 512].rearrange("(c p) n -> p c n", p=128)
        P.dma("sp", out=wt[:], in_=src, reads=[], writes=[wk])
        ps, pk = P.bank()
        for q in range(4):
            for c in range(8):
                P.op("pe", lambda: nc.tensor.matmul(ps[:, q * 4:q * 4 + 3], lhsT=wt[:, c, q * 128:(q + 1) * 128],
                                                    rhs=G.scT[:, c, :], start=(c == 0), stop=(c == 7)),
                     [wk, "scT"], [pk])
        for q in range(4):
            j = blk * 4 + q
            P.op("dve", lambda: nc.vector.tensor_scalar(out=G.modT[:, j, :], in0=ps[:, q * 4:q * 4 + 3],
                                                        scalar1=ppcol(G, "b_ada", l * 72 + j), scalar2=None,
                                                        op0=ALU.add), [pk, "pp"], ["modT"])
    for i in (1, 4, 7):
        P.op("dve", lambda: nc.vector.tensor_scalar(out=G.modT[:, i * 8:(i + 1) * 8, :], in0=G.modT[:, i * 8:(i + 1) * 8, :],
                                                    scalar1=1.0, scalar2=None, op0=ALU.add), ["modT"], ["modT"])
    for i in (2, 8):
        P.op("dve", lambda: nc.vector.tensor_scalar(out=G.modT[:, i * 8:(i + 1) * 8, :], in0=G.modT[:, i * 8:(i + 1) * 8, :],
                                                    scalar1=0.5, scalar2=None, op0=ALU.mult), ["modT"], ["modT"])


def mod(G, i, c, r):
    return G.modT[:, i * 8 + c, r:r + 1]


def load_block(G, src, skey, TB):
    P = G.P
    sT, sk = G.sT.next()
    P.dma("sp", out=sT[:, :, :TB], in_=src.rearrange("(c p) t -> p c t", p=128), reads=[skey], writes=[sk])
    return sT, sk


def modulate(G, sT, sk, TB, i_shift, r):
    P, nc = G.P, G.nc
    hT, hk = G.hT.next()
    for c in range(8):
        P.op("act", lambda: nc.scalar.activation(out=hT[:, c, :TB], in_=sT[:, c, :TB], func=AF.Identity,
                                                 bias=mod(G, i_shift, c, r), scale=mod(G, i_shift + 1, c, r)),
             [sk, "modT"], [(hk, c)])
    return hT, hk


def epilogue(G, l, jln, uT, uk, TB, dst, dkey):
    P, nc = G.P, G.nc
    mean, mk = P.bank()
    for c in range(8):
        P.op("pe", lambda: nc.tensor.matmul(mean[:, :TB], lhsT=G.onesM[:], rhs=uT[:, c, :TB], start=(c == 0), stop=(c == 7)),
             [(uk, c), "onesM"], [mk])
    qT, qk = G.qT.next()
    for c in range(8):
        P.op("dve", lambda: nc.vector.tensor_tensor(out=uT[:, c, :TB], in0=uT[:, c, :TB], in1=mean[:, :TB], op=ALU.subtract),
             [(uk, c), mk], [(uk, c)])
        P.op("act", lambda: nc.scalar.activation(out=qT[:, c, :TB], in_=uT[:, c, :TB], func=AF.Square), [(uk, c)], [(qk, c)])
    var, vk = P.bank()
    for c in range(8):
        P.op("pe", lambda: nc.tensor.matmul(var[:, :TB], lhsT=G.onesM[:], rhs=qT[:, c, :TB], start=(c == 0), stop=(c == 7)),
             [(qk, c), "onesM"], [vk])
    rs, rk = G.stat.next()
    P.op("act", lambda: nc.scalar.activation(out=rs[:, :TB], in_=var[:, :TB], func=AF.Sqrt, bias=G.epsc[:, 0:1]), [vk, "epsc"], [rk])
    P.op("dve", lambda: nc.vector.reciprocal(out=rs[:, :TB], in_=rs[:, :TB]), [rk], [rk])
    for c in range(8):
        P.op("dve", lambda: nc.vector.tensor_tensor(out=uT[:, c, :TB], in0=uT[:, c, :TB], in1=rs[:, :TB], op=ALU.mult),
             [(uk, c), rk], [(uk, c)])
        P.op("act", lambda: nc.scalar.activation(out=uT[:, c, :TB], in_=uT[:, c, :TB], func=AF.Identity,
                                                 bias=ppcol(G, "ln_b", (l * 3 + jln) * 8 + c),
                                                 scale=ppcol(G, "ln_g", (l * 3 + jln) * 8 + c)), [(uk, c), "pp"], [(uk, c)])
    P.dma("pool", out=dst.rearrange("(c p) t -> p c t", p=128), in_=uT[:, :, :TB], reads=allk(uk), writes=[dkey])


def emit_ffn_block(G, l, jln, src, skey, dst, dkey, TB, r):
    P, nc = G.P, G.nc
    jj = jln // 2
    i0 = 3 * jln
    sT, sk = load_block(G, src, skey, TB)
    hT, hk = modulate(G, sT, sk, TB, i0, r)
    for m in range(FC):
        wt, wk = G.w13t.next()
        P.dma("sp", out=wt[:], in_=G.w13[l, jj, m], reads=[], writes=[wk])
        pa, pak = P.bank()
        pb, pbk = P.bank()
        for c in range(8):
            P.op("pe", lambda: nc.tensor.matmul(pa[:, :TB], lhsT=wt[:, 0, c, :], rhs=hT[:, c, :TB], start=(c == 0), stop=(c == 7)),
                 [wk, (hk, c)], [pak])
        for c in range(8):
            P.op("pe", lambda: nc.tensor.matmul(pb[:, :TB], lhsT=wt[:, 1, c, :], rhs=hT[:, c, :TB], start=(c == 0), stop=(c == 7)),
                 [wk, (hk, c)], [pbk])
        tm, tk = G.tmp.next()
        P.op("act", lambda: nc.scalar.activation(out=tm[:, :TB], in_=pa[:, :TB], func=AF.Silu), [pak], [tk])
        P.op("dve", lambda: nc.vector.tensor_tensor(out=G.gT[:, m, :TB], in0=tm[:, :TB], in1=pb[:, :TB], op=ALU.mult),
             [tk, pbk], [("gT", m)])
    uT, uk = G.uT.next()
    gkeys = [("gT", m) for m in range(FC)]
    for n in range(8):
        wt, wk = G.w2t.next()
        P.dma("sp", out=wt[:], in_=G.w2[l, jj, n], reads=[], writes=[wk])
        py, pyk = P.bank()
        for m in range(FC):
            P.op("pe", lambda: nc.tensor.matmul(py[:, :TB], lhsT=wt[:, m, :], rhs=G.gT[:, m, :TB], start=(m == 0), stop=(m == FC - 1)),
                 [wk] + (gkeys if m == 0 else []), [pyk])
        P.op("act", lambda: nc.scalar.activation(out=uT[:, n, :TB], in_=py[:, :TB], func=AF.Identity, scale=mod(G, i0 + 2, n, r)),
             [pyk, "modT"], [(uk, n)])
        P.op("dve", lambda: nc.vector.scalar_tensor_tensor(out=uT[:, n, :TB], in0=sT[:, n, :TB], scalar=ALPHA, in1=uT[:, n, :TB],
                                                           op0=ALU.mult, op1=ALU.add), [sk, (uk, n)], [(uk, n)])
    epilogue(G, l, jln, uT, uk, TB, dst, dkey)


def emit_mixin_block(G, l, b, bi, t0, TB, r):
    P, nc = G.P, G.nc
    sT, sk = load_block(G, G.xs[b, :, t0:t0 + TB], ("xs", b, bi), TB)
    hT, hk = modulate(G, sT, sk, TB, 3, r)
    for m in range(PINC):
        wt, wk = G.w4t.next()
        P.dma("sp", out=wt[:], in_=G.win[l, m], reads=[], writes=[wk])
        ps, pk = P.bank()
        for c in range(8):
            P.op("pe", lambda: nc.tensor.matmul(ps[:, :TB], lhsT=wt[:, c, :], rhs=hT[:, c, :TB], start=(c == 0), stop=(c == 7)),
                 [wk, (hk, c)], [pk])
        tm, tk = G.tmp.next()
        if m % 2 == 0:
            P.op("act", lambda: nc.scalar.copy(out=tm[:, :TB], in_=ps[:, :TB]), [pk], [tk])
        else:
            P.op("dve", lambda: nc.vector.tensor_copy(out=tm[:, :TB], in_=ps[:, :TB]), [pk], [tk])
        P.dma("pool", out=G.pT[b, m * 128:(m + 1) * 128, t0:t0 + TB], in_=tm[:, :TB], reads=[tk], writes=[("pT", b, m, bi)])


def emit_merge_block(G, l, b, bi, t0, TB, r):
    P, nc = G.P, G.nc
    sT, sk = load_block(G, G.xs[b, :, t0:t0 + TB], ("xs", b, bi), TB)
    hT, hk = modulate(G, sT, sk, TB, 3, r)
    yt, yk = G.qT.next()
    P.dma("sp", out=yt[:, :, :TB], in_=G.yT[b, :, t0:t0 + TB].rearrange("(c p) t -> p c t", p=128), reads=[("yT", b)], writes=allk(yk))
    aT, ak = G.aT.next()
    for n in range(8):
        for br in range(4):
            wt, wk = G.w4t.next()
            P.dma("sp", out=wt[:], in_=G.wg[l, br * 8 + n], reads=[], writes=[wk])
            pg, pgk = P.bank()
            for c in range(8):
                P.op("pe", lambda: nc.tensor.matmul(pg[:, :TB], lhsT=wt[:, c, :], rhs=hT[:, c, :TB], start=(c == 0), stop=(c == 7)),
                     [wk, (hk, c)], [pgk])
            gt, gk = G.tmp.next()
            P.op("act", lambda: nc.scalar.activation(out=gt[:, :TB], in_=pg[:, :TB], func=AF.Sigmoid,
                                                     bias=ppcol(G, "b_gate", l * 32 + br * 8 + n)), [pgk, "pp"], [gk])
            wb, wbk = G.wbt.next()
            P.dma("sp", out=wb[:], in_=G.wb[l, br, n], reads=[], writes=[wbk])
            pb, pbk = P.bank()
            for kc in range(2):
                P.op("pe", lambda: nc.tensor.matmul(pb[:, :TB], lhsT=wb[:, kc, :], rhs=yt[:, br * 2 + kc, :TB], start=(kc == 0), stop=(kc == 1)),
                     [wbk, (yk, br * 2 + kc)], [pbk])
            if br == 0:
                P.op("dve", lambda: nc.vector.tensor_tensor(out=aT[:, n, :TB], in0=gt[:, :TB], in1=pb[:, :TB], op=ALU.mult), [gk, pbk], [ak])
            else:
                P.op("dve", lambda: nc.vector.tensor_tensor(out=gt[:, :TB], in0=gt[:, :TB], in1=pb[:, :TB], op=ALU.mult), [gk, pbk], [gk])
                P.op("pool", lambda: nc.gpsimd.tensor_tensor(out=aT[:, n, :TB], in0=aT[:, n, :TB], in1=gt[:, :TB], op=ALU.add), [gk, ak], [ak])
    uT, uk = G.uT.next()
    for n in range(8):
        wt, wk = G.w4t.next()
        P.dma("sp", out=wt[:], in_=G.wo[l, n], reads=[], writes=[wk])
        po, pok = P.bank()
        for c in range(8):
            P.op("pe", lambda: nc.tensor.matmul(po[:, :TB], lhsT=wt[:, c, :], rhs=aT[:, c, :TB], start=(c == 0), stop=(c == 7)),
                 [wk, ak], [pok])
        P.op("act", lambda: nc.scalar.activation(out=uT[:, n, :TB], in_=po[:, :TB], func=AF.Identity, scale=mod(G, 5, n, r)),
             [pok, "modT"], [(uk, n)])
        P.op("dve", lambda: nc.vector.scalar_tensor_tensor(out=uT[:, n, :TB], in0=sT[:, n, :TB], scalar=ALPHA, in1=uT[:, n, :TB],
                                                           op0=ALU.mult, op1=ALU.add), [sk, (uk, n)], [(uk, n)])
    epilogue(G, l, 1, uT, uk, TB, G.xs[b, :, t0:t0 + TB], ("xs", b, bi))


def emit_blocks(G, fn, l, skip_ctx=False):
    for b in range(NB):
        for bi, (t0, TB) in enumerate(BLOCKS):
            if bi == 0 and skip_ctx:
                continue
            fn(G, l, b, bi, t0, TB, 2 if bi == 0 else b)


BLOCKS = [(0, 256)] + [(CTX + i * 512, 512) for i in range(4)]


def emit_ffn(G, l, jln, src_t, src_name, dst_t, dst_name, skip_ctx=False, final=False):
    for b in range(NB):
        for bi, (t0, TB) in enumerate(BLOCKS):
            if bi == 0 and skip_ctx:
                continue
            r = 2 if bi == 0 else b
            src = src_t[b, :, t0:t0 + TB]
            if final:
                dst = dst_t[b, :, t0 - CTX:t0 - CTX + TB]
            else:
                dst = dst_t[b, :, t0:t0 + TB]
            emit_ffn_block(G, l, jln, src, (src_name, b, bi), dst, (dst_name, b, bi), TB, r)


def emit_all(G, cfg):
    P, nc = G.P, G.nc
    setup_consts(G)
    phases = cfg.get("phases", {"ada", "ffn0", "mixin", "mixers", "merge", "ffn2"})
    for l in cfg.get("layers", range(NL)):
        last = l == NL - 1
        with P.phase():
            dense_tiles(G)
            if "ada" in phases:
                emit_ada(G, l)
            if "ffn0" in phases:
                emit_ffn(G, l, 0, G.xin if l == 0 else G.xs, "xin" if l == 0 else "xs", G.xs, "xs")
        with P.phase():
            merge_tiles(G)
            if "mixin" in phases:
                emit_blocks(G, emit_mixin_block, l)
        if "mixers" in phases:
            emit_mixers(G, l, cfg.get("mixers", ("hgrn", "ret", "s5", "rwkv")))
        with P.phase():
            merge_tiles(G)
            if "merge" in phases:
                emit_blocks(G, emit_merge_block, l, skip_ctx=last)
        with P.phase():
            dense_tiles(G)
            if "ffn2" in phases:
                if last:
                    emit_ffn(G, l, 2, G.xs, "xs", G.out, "out", skip_ctx=True, final=True)
                else:
                    emit_ffn(G, l, 2, G.xs, "xs", G.xs, "xs")


TBLK = [(0, 512), (512, 512), (1024, 512), (1536, 512), (2048, 256)]
OFF_RWKV, OFF_RET, OFF_S5, OFF_HG = 0, 1056, 2080, 2336


def rev(t, a, b):
    ap = t[:, b - 1:b]
    return bass.AP(ap.tensor, ap.offset, [list(ap.ap[0]), [-1, b - a]])


def setup_consts(G):
    P, nc = G.P, G.nc
    G.epsc = P.sb("epsc", [128, 2])
    P.op("dve", lambda: nc.vector.memset(G.epsc[:, 0:1], LN_EPS), [], ["epsc"])
    P.op("dve", lambda: nc.vector.memset(G.epsc[:, 1:2], 64e-5), [], ["epsc"])
    G.ones9 = P.sb("ones9", [128, 128])
    P.op("dve", lambda: nc.vector.memset(G.ones9[:], 1.0), [], ["ones9"])
    G.onesH = P.sb("onesH", [128, 128])
    P.op("dve", lambda: nc.vector.memset(G.onesH[:], 0.0), [], ["onesH"])
    P.op("dve", lambda: nc.vector.memset(G.onesH[0:64, 0:64], 1.0 / 64), [], ["onesH"])
    P.op("dve", lambda: nc.vector.memset(G.onesH[64:128, 64:128], 1.0 / 64), [], ["onesH"])
    G.hlb = P.sb("hlb", [128, 2, NL])
    G.homl = P.sb("homl", [128, 2, NL])
    e = P.sb("hlb_e", [128, 2, NL])
    ssum = P.sb("hlb_s", [128, 2])
    o, n = G.off["hg_lbl"]
    P.op("act", lambda: nc.scalar.activation(out=e[:].rearrange("p a b -> p (a b)"), in_=G.pp[:, o:o + n], func=AF.Exp), ["pp"], ["hlb_e"])
    for hp in range(2):
        P.op("dve", lambda: nc.vector.tensor_reduce(out=ssum[:, hp:hp + 1], in_=e[:, hp, :], axis=mybir.AxisListType.X, op=ALU.add),
             ["hlb_e"], ["hlb_s"])
    P.op("dve", lambda: nc.vector.reciprocal(out=ssum[:], in_=ssum[:]), ["hlb_s"], ["hlb_s"])
    for hp in range(2):
        P.op("dve", lambda: nc.vector.tensor_scalar(out=e[:, hp, :], in0=e[:, hp, :], scalar1=ssum[:, hp:hp + 1], scalar2=None, op0=ALU.mult),
             ["hlb_e", "hlb_s"], ["hlb_e"])
        P.op("dve", lambda: nc.vector.memset(G.hlb[:, hp, 0:1], 0.0), [], ["hlb"])
        for l in range(1, NL):
            P.op("dve", lambda: nc.vector.tensor_tensor(out=G.hlb[:, hp, l:l + 1], in0=G.hlb[:, hp, l - 1:l], in1=e[:, hp, l:l + 1], op=ALU.add),
                 ["hlb", "hlb_e"], ["hlb"])
    P.op("dve", lambda: nc.vector.tensor_scalar(out=G.homl[:].rearrange("p a b -> p (a b)"), in0=G.hlb[:].rearrange("p a b -> p (a b)"),
                                                scalar1=-1.0, scalar2=1.0, op0=ALU.mult, op1=ALU.add), ["hlb"], ["homl"])


def head_norm_blk(G, x, xk, n, center, eps_i):
    P, nc = G.P, G.nc
    if center:
        mean, mk = P.bank()
        P.op("pe", lambda: nc.tensor.matmul(mean[:, :n], lhsT=G.onesH[:], rhs=x, start=True, stop=True), [xk, "onesH"], [mk])
        P.op("dve", lambda: nc.vector.tensor_tensor(out=x, in0=x, in1=mean[:, :n], op=ALU.subtract), [xk, mk], [xk])
    sq, sqk = G.tmp.next()
    P.op("act", lambda: nc.scalar.activation(out=sq[:, :n], in_=x, func=AF.Square), [xk], [sqk])
    var, vk = P.bank()
    P.op("pe", lambda: nc.tensor.matmul(var[:, :n], lhsT=G.onesH[:], rhs=sq[:, :n], start=True, stop=True), [sqk, "onesH"], [vk])
    P.op("act", lambda: nc.scalar.activation(out=sq[:, :n], in_=var[:, :n], func=AF.Sqrt, bias=G.epsc[:, eps_i:eps_i + 1]), [vk, "epsc"], [sqk])
    P.op("dve", lambda: nc.vector.reciprocal(out=sq[:, :n], in_=sq[:, :n]), [sqk], [sqk])
    P.op("dve", lambda: nc.vector.tensor_tensor(out=x, in0=x, in1=sq[:, :n], op=ALU.mult), [xk, sqk], [xk])


def gla_tiles(G):
    P = G.P
    G.gq = Pool_(P, "gq", [128, TT], 1)
    G.gk = Pool_(P, "gk", [128, TT], 2)
    G.gf = Pool_(P, "gf", [128, TT], 2)
    G.gg = Pool_(P, "gg", [128, TT], 1)
    G.go = Pool_(P, "go", [128, TT], 1)
    G.vb = Pool_(P, "vb", [128, TT], 2)
    G.kv = Pool_(P, "kv", [128, TT], 2)
    G.S = Pool_(P, "S", [128, TT], 2)
    G.qs = Pool_(P, "qs", [128, TT], 2)
    G.tmp = Pool_(P, "tmp", [128, 512], 4)


def gla_hp(G, b, q, qk, kk, ff, vrow, excl, accs):
    P, nc = G.P, G.nc
    for v in range(G.cfg.get("nv", 64)):
        vb, vbk = G.vb.next()
        for h in range(2):
            src = G.pT[b, vrow + h * 64 + v:vrow + h * 64 + v + 1, :]
            P.dma("sp", out=vb[h * 64:(h + 1) * 64, :], in_=bass.AP(src.tensor, src.offset, [[0, 64], [1, TT]]), reads=[], writes=[vbk])
        for d in range(2):
            kt, ktk = kk[d]
            kv, kvk = G.kv.next()
            P.op("pool", lambda: nc.gpsimd.tensor_tensor(out=kv[:], in0=kt[:], in1=vb[:], op=ALU.mult), [ktk, vbk], [kvk])
            S, Sk = G.S.next()
            kind, fobj, fkey = ff[d]
            if d == 0:
                d0 = fobj[:, 0:TT] if kind == "t" else bc(fobj, TT)
                P.op("dve", lambda: nc.vector.tensor_tensor_scan(out=S[:, 0:TT], data0=d0, data1=kv[:, 0:TT], initial=0.0,
                                                                 op0=ALU.mult, op1=ALU.add), [fkey, kvk], [Sk])
            else:
                d0 = rev(fobj, 0, CTX) if kind == "t" else bc(fobj, CTX)
                P.op("dve", lambda: nc.vector.tensor_tensor_scan(out=rev(S, 0, CTX), data0=d0, data1=rev(kv, 0, CTX), initial=0.0,
                                                                 op0=ALU.mult, op1=ALU.add), [fkey, kvk], [Sk])
                d0 = rev(fobj, CTX, TT) if kind == "t" else bc(fobj, SEQ)
                P.op("dve", lambda: nc.vector.tensor_tensor_scan(out=rev(S, CTX, TT), data0=d0, data1=rev(kv, CTX, TT), initial=S[:, 0:1],
                                                                 op0=ALU.mult, op1=ALU.add), [fkey, kvk, Sk], [Sk])
            if excl[d]:
                P.op("pool", lambda: nc.gpsimd.tensor_tensor(out=S[:], in0=S[:], in1=kv[:], op=ALU.subtract), [Sk, kvk], [Sk])
            qs, qsk = G.qs.next()
            P.op("dve", lambda: nc.vector.tensor_tensor(out=qs[:], in0=q[:], in1=S[:], op=ALU.mult), [qk, Sk], [qsk])
            for bi, (t0, n) in enumerate(TBLK):
                acc, ak = accs[bi]
                P.op("pe", lambda: nc.tensor.matmul(acc[:, :n], lhsT=G.G2[:, 63 - v:191 - v], rhs=qs[:, t0:t0 + n],
                                                    start=(v == 0 and d == 0), stop=(v == G.cfg.get("nv", 64) - 1 and d == 1)), [qsk, "G2"], [ak])


def emit_hgrn(G, l, b):
    P, nc = G.P, G.nc
    for hp in range(2):
        r0 = OFF_HG + hp * 128
        q, qk = G.gq.next()
        P.dma("sp", out=q[:], in_=G.pT[b, r0:r0 + 128, :], reads=[], writes=[qk])
        g, gk = G.gg.next()
        P.dma("sp", out=g[:], in_=G.pT[b, r0 + 1024:r0 + 1024 + 128, :], reads=[], writes=[gk])
        kk, ff = [], []
        for d in range(2):
            f, fk = G.gf.next()
            P.dma("sp", out=f[:], in_=G.pT[b, r0 + 256 * (d + 1):r0 + 256 * (d + 1) + 128, :], reads=[], writes=[fk])
            o_, _ = G.off["hg_fb"]
            P.op("act", lambda: nc.scalar.activation(out=f[:], in_=f[:], func=AF.Sigmoid,
                                                     bias=G.pp[:, o_ + l * 4 + d * 2 + hp:o_ + l * 4 + d * 2 + hp + 1]), [fk, "pp"], [fk])
            P.op("dve", lambda: nc.vector.tensor_scalar(out=f[:], in0=f[:], scalar1=G.homl[:, hp, l:l + 1], scalar2=G.hlb[:, hp, l:l + 1],
                                                        op0=ALU.mult, op1=ALU.add), [fk, "homl", "hlb"], [fk])
            k, kk_ = G.gk.next()
            P.op("dve", lambda: nc.vector.tensor_scalar(out=k[:], in0=f[:], scalar1=-1.0, scalar2=1.0, op0=ALU.mult, op1=ALU.add), [fk], [kk_])
            kk.append((k, kk_))
            ff.append(("t", f, fk))
        accs = P.reserve(5)
        gla_hp(G, b, q, qk, kk, ff, r0 + 768, [False, False], accs)
        o, ok = G.go.next()
        for bi, (t0, n) in enumerate(TBLK):
            acc, ak = accs[bi]
            P.op("act", lambda: nc.scalar.copy(out=o[:, t0:t0 + n], in_=acc[:, :n]), [ak], [ok])
        P.release()
        P.op("act", lambda: nc.scalar.activation(out=g[:], in_=g[:], func=AF.Silu), [gk], [gk])
        for bi, (t0, n) in enumerate(TBLK):
            head_norm_blk(G, o[:, t0:t0 + n], ok, n, False, 0)
        P.op("dve", lambda: nc.vector.scalar_tensor_tensor(out=o[:], in0=o[:], scalar=ppcol(G, "hg_gn", l * 2 + hp), in1=g[:],
                                                           op0=ALU.mult, op1=ALU.mult), [ok, gk, "pp"], [ok])
        P.dma("pool", out=G.yT[b, 768 + hp * 128:768 + (hp + 1) * 128, :], in_=o[:], reads=[ok], writes=[("yT", b, 3, hp)])


GLA_CAP = 80.0


def gla2_tiles(G):
    P = G.P
    T9 = lambda n: P.sb("g2_" + n, [128, TT])
    G.g2 = dict(q=T9("q"), g=T9("g"), vv=T9("vv"), yacc=T9("yacc"),
                vtok=P.sb("g2_vtok", [128, 18, 128]),
                ident=P.sb("g2_ident", [128, 128]), Mk=P.sb("g2_M", [128, 2, 1280]))
    for d in range(2):
        G.g2[d] = dict(LD=T9("LD%d" % d), KK=T9("KK%d" % d), QT=T9("QT%d" % d), KE=T9("KE%d" % d), QH=T9("QH%d" % d),
                       Rref=P.sb("g2_Rref%d" % d, [128, 18, 8]), S0=P.sb("g2_S0%d" % d, [128, 64]),
                       sc=P.sb("g2_sc%d" % d, [128, 256]), ktok=P.sb("g2_ktok%d" % d, [128, 128]), gc=P.sb("g2_gc%d" % d, [128, 1]),
                       Z=Pool_(P, "g2Z%d" % d, [128, 8, 128], 2))
    G.tmp = Pool_(P, "tmp", [128, 512], 4)
    P.dma("sp", out=G.g2["ident"][:], in_=G.ident_d, reads=[], writes=["g2ident"])
    P.dma("sp", out=G.g2["Mk"][:], in_=G.rwM, reads=[], writes=["g2M"])


def gla2_core(G, d, excl):
    P, nc = G.P, G.nc
    V = nc.vector
    t = dict(G.g2)
    t.update(G.g2[d])
    q, KK, B, QT, KE, QH, Rref, S0, vtok, yacc = t["q"], t["KK"], t["LD"], t["QT"], t["KE"], t["QH"], t["Rref"], t["S0"], t["vtok"], t["yacc"]
    kkn = t.get("KKkey", "g2KK%d" % d)
    for c in range(18):
        if d == 0:
            P.op("dve", lambda: V.tensor_tensor_scan(out=B[:, c * 128:(c + 1) * 128], data0=G.ones9[:, 0:128], data1=B[:, c * 128:(c + 1) * 128], initial=0.0,
                                                     op0=ALU.mult, op1=ALU.add), ["g2LD%d" % d, "ones9"], ["g2LD%d" % d])
        else:
            P.op("dve", lambda: V.tensor_tensor_scan(out=rev(B, c * 128, (c + 1) * 128), data0=G.ones9[:, 0:128], data1=rev(B, c * 128, (c + 1) * 128), initial=0.0,
                                                     op0=ALU.mult, op1=ALU.add), ["g2LD%d" % d, "ones9"], ["g2LD%d" % d])
    bap = B[:, 0:1]
    ps_ = list(bap.ap[0])
    mk = lambda off, dims: bass.AP(bap.tensor, bap.offset + off, [ps_] + dims)
    if d == 0:
        P.op("dve", lambda: V.memset(Rref[:, :, 0:1], 0.0), [], ["g2R%d" % d])
        P.op("dve", lambda: V.tensor_copy(out=Rref[:, :, 1:8], in_=mk(15, [[128, 18], [16, 7]])), ["g2LD%d" % d], ["g2R%d" % d])
        bc_off = 127
    else:
        P.op("dve", lambda: V.memset(Rref[:, :, 7:8], 0.0), [], ["g2R%d" % d])
        P.op("dve", lambda: V.tensor_copy(out=Rref[:, :, 0:7], in_=mk(16, [[128, 18], [16, 7]])), ["g2LD%d" % d], ["g2R%d" % d])
        bc_off = 0
    P.op("act", lambda: nc.scalar.activation(out=QT[:], in_=B[:], func=AF.Exp), ["g2LD%d" % d], ["g2QT%d" % d])
    P.op("pool", lambda: nc.gpsimd.tensor_tensor(out=QT[:], in0=QT[:], in1=q[:], op=ALU.mult), ["g2QT%d" % d, "g2q"], ["g2QT%d" % d])
    v3 = lambda ap: ap.rearrange("p (c s) -> p c s", c=18)
    P.op("dve", lambda: V.tensor_tensor(out=v3(KE[:]), in0=mk(bc_off, [[128, 18], [0, 128]]), in1=v3(B[:]), op=ALU.subtract), ["g2LD%d" % d], ["g2KE%d" % d])
    P.op("act", lambda: nc.scalar.activation(out=KE[:], in_=KE[:], func=AF.Exp), ["g2KE%d" % d], ["g2KE%d" % d])
    P.op("dve", lambda: V.tensor_tensor(out=KE[:], in0=KE[:], in1=KK[:], op=ALU.mult), ["g2KE%d" % d, kkn], ["g2KE%d" % d])
    rap = Rref[:, 0:1, 0:1]
    rexp = bass.AP(rap.tensor, rap.offset, [list(rap.ap[0]), [1, 144], [0, 16]])
    P.op("dve", lambda: V.tensor_tensor(out=QH[:].rearrange("p (c s) -> p c s", s=16), in0=B[:].rearrange("p (c s) -> p c s", s=16), in1=rexp, op=ALU.subtract),
         ["g2LD%d" % d, "g2R%d" % d], ["g2QH%d" % d])
    P.op("act", lambda: nc.scalar.activation(out=QH[:], in_=QH[:], func=AF.Exp), ["g2QH%d" % d], ["g2QH%d" % d])
    P.op("pool", lambda: nc.gpsimd.tensor_tensor(out=QH[:], in0=QH[:], in1=q[:], op=ALU.mult), ["g2QH%d" % d, "g2q"], ["g2QH%d" % d])
    P.op("dve", lambda: V.memset(S0[:], 0.0), [], ["g2S0%d" % d])
    order = list(range(18)) if d == 0 else [1, 0] + list(range(17, 1, -1))
    mcol = (512 if excl else 768)
    yield
    def make_z(c):
        Z, Zk = t["Z"].next()
        r_c = Rref[:, c, 0:1]
        rb = bass.AP(r_c.tensor, r_c.offset, [list(r_c.ap[0]), [1, 8], [0, 128]])
        b_c = B[:, c * 128:c * 128 + 1]
        bb = bass.AP(b_c.tensor, b_c.offset, [list(b_c.ap[0]), [0, 8], [1, 128]])
        k_c = KK[:, c * 128:c * 128 + 1]
        kb = bass.AP(k_c.tensor, k_c.offset, [list(k_c.ap[0]), [0, 8], [1, 128]])
        P.op("dve", lambda: V.tensor_tensor(out=Z[:], in0=rb, in1=bb, op=ALU.subtract), ["g2R%d" % d, "g2LD%d" % d], [Zk])
        if t.get("clamp", True):
            P.op("dve", lambda: V.tensor_scalar(out=Z[:], in0=Z[:], scalar1=GLA_CAP, scalar2=None, op0=ALU.min), [Zk], [Zk])
        P.op("act", lambda: nc.scalar.activation(out=Z[:], in_=Z[:], func=AF.Exp), [Zk], [Zk])
        P.op("pool", lambda: nc.gpsimd.tensor_tensor(out=Z[:], in0=Z[:], in1=kb, op=ALU.mult), [Zk, kkn], [Zk])
        return Z, Zk

    znext = make_z(order[0])
    for ci, c in enumerate(order):
        cs = slice(c * 128, (c + 1) * 128)
        Z, Zk = znext
        if ci + 1 < len(order):
            znext = make_z(order[ci + 1])
        yield
        pS, pSk = P.bank()
        for h in range(2):
            hb = slice(64 * h, 64 * h + 64)
            for m in range(8):
                P.op("pe", lambda: nc.tensor.matmul(pS[:, h * 128 + 16 * m:h * 128 + 16 * m + 16], lhsT=Z[hb, m, :], rhs=QH[hb, c * 128 + 16 * m:c * 128 + 16 * m + 16],
                                                    start=True, stop=True), [Zk, "g2QH%d" % d], [pSk])
        P.op("dve", lambda: V.tensor_tensor(out=t["sc"][:], in0=pS[:, 0:256], in1=t["Mk"][:, d, mcol:mcol + 256], op=ALU.mult), [pSk, "g2M"], ["g2sc%d" % d])
        yield
        pt, ptk = P.bank()
        P.op("pe", lambda: nc.tensor.transpose(out=pt[:, 0:128], in_=KE[:, cs], identity=t["ident"][:]), ["g2KE%d" % d, "g2ident"], [ptk])
        P.op("act", lambda: nc.scalar.copy(out=t["ktok"][:], in_=pt[:, 0:128]), [ptk], ["g2ktok%d" % d])
        yield
        py, pyk = P.bank()
        for h in range(2):
            hb = slice(64 * h, 64 * h + 64)
            P.op("pe", lambda: nc.tensor.matmul(py[hb, 0:128], lhsT=S0[hb, :], rhs=QT[hb, cs], start=True, stop=False), ["g2S0%d" % d, "g2QT%d" % d], [pyk])
            P.op("pe", lambda: nc.tensor.matmul(py[hb, 0:128], lhsT=vtok[:, c, h * 64:(h + 1) * 64], rhs=t["sc"][:, h * 128:(h + 1) * 128], start=False, stop=True),
                 ["g2vtok", "g2sc%d" % d], [pyk])
        P.op("dve", lambda: V.tensor_tensor(out=yacc[:, cs], in0=yacc[:, cs], in1=py[:, 0:128], op=ALU.add), [pyk, ("g2yacc", c)], [("g2yacc", c)])
        yield
        pT_, pTk = P.bank()
        for h in range(2):
            hb = slice(64 * h, 64 * h + 64)
            P.op("pe", lambda: nc.tensor.matmul(pT_[hb, 0:64], lhsT=t["ktok"][:, hb], rhs=vtok[:, c, h * 64:(h + 1) * 64], start=True, stop=True), ["g2ktok%d" % d, "g2vtok"], [pTk])
        tl = (c * 128 + 127) if d == 0 else (c * 128)
        P.op("act", lambda: nc.scalar.activation(out=t["gc"][:], in_=B[:, tl:tl + 1], func=AF.Exp), ["g2LD%d" % d], ["g2gc%d" % d])
        P.op("dve", lambda: V.scalar_tensor_tensor(out=S0[:], in0=S0[:], scalar=t["gc"][:, 0:1], in1=pT_[:, 0:64], op0=ALU.mult, op1=ALU.add), ["g2S0%d" % d, "g2gc%d" % d, pTk], ["g2S0%d" % d])


def gla2_run(G, excl):
    P, nc = G.P, G.nc
    P.op("dve", lambda: nc.vector.memset(G.g2["yacc"][:], 0.0), [], [("g2yacc", c) for c in range(18)])
    gens = [gla2_core(G, 0, excl[0]), gla2_core(G, 1, excl[1])]
    live = list(gens)
    while live:
        for g_ in list(live):
            try:
                next(g_)
            except StopIteration:
                live.remove(g_)


def gla2_vtok(G):
    P, nc = G.P, G.nc
    t = G.g2
    for c in range(18):
        pt, ptk = P.bank()
        P.op("pe", lambda: nc.tensor.transpose(out=pt[:, 0:128], in_=t["vv"][:, c * 128:(c + 1) * 128], identity=t["ident"][:]), ["g2vv", "g2ident"], [ptk])
        P.op("act", lambda: nc.scalar.copy(out=t["vtok"][:, c, :], in_=pt[:, 0:128]), [ptk], ["g2vtok"])


def gla2_finish(G, l, b, hp, center, gn_name, yrow):
    P, nc = G.P, G.nc
    t = G.g2
    o, g = t["yacc"], t["g"]
    P.op("act", lambda: nc.scalar.activation(out=g[:], in_=g[:], func=AF.Silu), ["g2g"], ["g2g"])
    allk = [("g2yacc", c) for c in range(18)]
    P.op("dve", lambda: nc.vector.tensor_copy(out=o[:, 0:1], in_=o[:, 0:1]), allk, allk + ["g2yacc"])
    for bi, (t0, n) in enumerate(TBLK):
        head_norm_blk(G, o[:, t0:t0 + n], "g2yacc", n, center, 0)
    P.op("dve", lambda: nc.vector.scalar_tensor_tensor(out=o[:], in0=o[:], scalar=ppcol(G, gn_name, l * 2 + hp), in1=g[:],
                                                       op0=ALU.mult, op1=ALU.mult), ["g2yacc", "g2g", "pp"], ["g2yacc"] + allk)
    P.dma("pool", out=G.yT[b, yrow + hp * 128:yrow + (hp + 1) * 128, :], in_=o[:], reads=["g2yacc"], writes=[("yT", b, yrow, hp)])


def emit_hgrn2(G, l, b):
    P, nc = G.P, G.nc
    V = nc.vector
    t = G.g2
    for hp in range(2):
        r0 = OFF_HG + hp * 128
        P.dma("sp", out=t["q"][:], in_=G.pT[b, r0:r0 + 128, :], reads=[], writes=["g2q"])
        P.dma("sp", out=t["g"][:], in_=G.pT[b, r0 + 1024:r0 + 1024 + 128, :], reads=[], writes=["g2g"])
        P.dma("sp", out=t["vv"][:], in_=G.pT[b, r0 + 768:r0 + 768 + 128, :], reads=[], writes=["g2vv"])
        gla2_vtok(G)
        for d in range(2):
            f = t[d]["LD"]
            P.dma("sp", out=f[:], in_=G.pT[b, r0 + 256 * (d + 1):r0 + 256 * (d + 1) + 128, :], reads=[], writes=["g2LD%d" % d])
            o_, _ = G.off["hg_fb"]
            P.op("act", lambda: nc.scalar.activation(out=f[:], in_=f[:], func=AF.Sigmoid,
                                                     bias=G.pp[:, o_ + l * 4 + d * 2 + hp:o_ + l * 4 + d * 2 + hp + 1]), ["g2LD%d" % d, "pp"], ["g2LD%d" % d])
            P.op("dve", lambda: V.tensor_scalar(out=f[:], in0=f[:], scalar1=G.homl[:, hp, l:l + 1], scalar2=G.hlb[:, hp, l:l + 1],
                                                op0=ALU.mult, op1=ALU.add), ["g2LD%d" % d, "homl", "hlb"], ["g2LD%d" % d])
            P.op("dve", lambda: V.tensor_scalar(out=t[d]["KK"][:], in0=f[:], scalar1=-1.0, scalar2=1.0, op0=ALU.mult, op1=ALU.add), ["g2LD%d" % d], ["g2KK%d" % d])
            P.op("act", lambda: nc.scalar.activation(out=f[:], in_=f[:], func=AF.Ln), ["g2LD%d" % d], ["g2LD%d" % d])
        gla2_run(G, [False, False])
        gla2_finish(G, l, b, hp, False, "hg_gn", 768)


def emit_ret2(G, l, b):
    P, nc = G.P, G.nc
    V = nc.vector
    t = G.g2
    for hp in range(2):
        r0 = OFF_RET + hp * 128
        q, k = t["q"], t[0]["KK"]
        P.dma("sp", out=q[:], in_=G.pT[b, r0:r0 + 128, :], reads=[], writes=["g2q"])
        P.dma("sp", out=k[:], in_=G.pT[b, r0 + 256:r0 + 256 + 128, :], reads=[], writes=["g2KK0"])
        P.dma("sp", out=t["vv"][:], in_=G.pT[b, r0 + 512:r0 + 512 + 128, :], reads=[], writes=["g2vv"])
        P.dma("sp", out=t["g"][:], in_=G.pT[b, r0 + 768:r0 + 768 + 128, :], reads=[], writes=["g2g"])
        gla2_vtok(G)
        P.op("act", lambda: nc.scalar.mul(out=k[:], in_=k[:], mul=0.125), ["g2KK0"], ["g2KK0"])
        for x, xk in ((q, "g2q"), (k, "g2KK0")):
            for i in range(4):
                t0 = CTX + i * 512
                sw, swk = P.bank()
                P.op("pe", lambda: nc.tensor.matmul(sw[:, :512], lhsT=G.perm[:], rhs=x[:, t0:t0 + 512], start=True, stop=True), [xk, "perm"], [swk])
                tm, tk = G.tmp.next()
                P.op("dve", lambda: V.tensor_tensor(out=tm[:, :512], in0=sw[:, :512], in1=G.rope[:, 1, i * 512:(i + 1) * 512], op=ALU.mult), [swk, "rope"], [tk])
                P.op("pool", lambda: nc.gpsimd.tensor_tensor(out=x[:, t0:t0 + 512], in0=x[:, t0:t0 + 512], in1=G.rope[:, 0, i * 512:(i + 1) * 512], op=ALU.mult),
                     [xk, "rope"], [xk])
                P.op("dve", lambda: V.tensor_tensor(out=x[:, t0:t0 + 512], in0=x[:, t0:t0 + 512], in1=tm[:, :512], op=ALU.add), [xk, tk], [xk])
        for d in range(2):
            P.op("dve", lambda: V.tensor_copy(out=t[d]["LD"][:], in_=bc(ppcol(G, "ret_lg", hp), TT)), ["pp"], ["g2LD%d" % d])
        G.g2[1]["KK_save"] = G.g2[1]["KK"]
        G.g2[1]["KK"] = k
        G.g2[1]["KKkey"] = "g2KK0"
        G.g2[0]["clamp"] = G.g2[1]["clamp"] = False
        gla2_run(G, [False, True])
        G.g2[0]["clamp"] = G.g2[1]["clamp"] = True
        G.g2[1]["KK"] = G.g2[1].pop("KK_save")
        G.g2[1].pop("KKkey")
        gla2_finish(G, l, b, hp, True, "ret_gn", 256)


def emit_ret(G, l, b):
    P, nc = G.P, G.nc
    for hp in range(2):
        r0 = OFF_RET + hp * 128
        q, qk = G.gq.next()
        P.dma("sp", out=q[:], in_=G.pT[b, r0:r0 + 128, :], reads=[], writes=[qk])
        k, kk_ = G.gk.next()
        P.dma("sp", out=k[:], in_=G.pT[b, r0 + 256:r0 + 256 + 128, :], reads=[], writes=[kk_])
        g, gk = G.gg.next()
        P.dma("sp", out=g[:], in_=G.pT[b, r0 + 768:r0 + 768 + 128, :], reads=[], writes=[gk])
        P.op("act", lambda: nc.scalar.mul(out=k[:], in_=k[:], mul=0.125), [kk_], [kk_])
        for x, xk in ((q, qk), (k, kk_)):
            for i in range(4):
                t0 = CTX + i * 512
                sw, swk = P.bank()
                P.op("pe", lambda: nc.tensor.matmul(sw[:, :512], lhsT=G.perm[:], rhs=x[:, t0:t0 + 512], start=True, stop=True), [xk, "perm"], [swk])
                tm, tk = G.tmp.next()
                P.op("dve", lambda: nc.vector.tensor_tensor(out=tm[:, :512], in0=sw[:, :512], in1=G.rope[:, 1, i * 512:(i + 1) * 512], op=ALU.mult),
                     [swk, "rope"], [tk])
                P.op("pool", lambda: nc.gpsimd.tensor_tensor(out=x[:, t0:t0 + 512], in0=x[:, t0:t0 + 512], in1=G.rope[:, 0, i * 512:(i + 1) * 512], op=ALU.mult),
                     [xk, "rope"], [xk])
                P.op("dve", lambda: nc.vector.tensor_tensor(out=x[:, t0:t0 + 512], in0=x[:, t0:t0 + 512], in1=tm[:, :512], op=ALU.add), [xk, tk], [xk])
        accs = P.reserve(5)
        gam = ppcol(G, "ret_gam", hp)
        gla_hp(G, b, q, qk, [(k, kk_), (k, kk_)], [("c", gam, "pp"), ("c", gam, "pp")], r0 + 512, [False, True], accs)
        o, ok = G.go.next()
        for bi, (t0, n) in enumerate(TBLK):
            acc, ak = accs[bi]
            P.op("act", lambda: nc.scalar.copy(out=o[:, t0:t0 + n], in_=acc[:, :n]), [ak], [ok])
        P.release()
        P.op("act", lambda: nc.scalar.activation(out=g[:], in_=g[:], func=AF.Silu), [gk], [gk])
        for bi, (t0, n) in enumerate(TBLK):
            head_norm_blk(G, o[:, t0:t0 + n], ok, n, True, 0)
        P.op("dve", lambda: nc.vector.scalar_tensor_tensor(out=o[:], in0=o[:], scalar=ppcol(G, "ret_gn", l * 2 + hp), in1=g[:],
                                                           op0=ALU.mult, op1=ALU.mult), [ok, gk, "pp"], [ok])
        P.dma("pool", out=G.yT[b, 256 + hp * 128:256 + (hp + 1) * 128, :], in_=o[:], reads=[ok], writes=[("yT", b, 1, hp)])


def rope_tables():
    nf = 16
    inv = (np.float32(10000.0) ** (-np.arange(nf, dtype=np.float32) / np.float32(nf))).astype(np.float32)
    rows = np.repeat(np.arange(SEQ // 64, dtype=np.float32), 64)
    cols = np.tile(np.arange(64, dtype=np.float32), SEQ // 64)
    ang = np.concatenate([rows[:, None] * inv, cols[:, None] * inv], -1).astype(np.float32)
    cos, sin = np.cos(ang).astype(np.float32), np.sin(ang).astype(np.float32)
    tab = np.zeros((128, 2, SEQ), np.float32)
    perm = np.zeros((128, 128), np.float32)
    for p in range(128):
        d = p % 64
        tab[p, 0] = cos[:, d % 32]
        tab[p, 1] = sin[:, d % 32] * (-1.0 if d < 32 else 1.0)
        perm[p + 32 if d < 32 else p - 32, p] = 1.0
    return tab, perm


TWO_PI = 6.283185307179586


def s5_sin(G, out, ang, nm):
    P, nc = G.P, G.nc
    ki = G.s5_ki
    kf = G.s5_kf
    P.op("dve", lambda: nc.vector.tensor_scalar(out=kf[:], in0=ang, scalar1=1.0 / TWO_PI, scalar2=None, op0=ALU.mult), [nm], ["s5_kf"])
    P.op("dve", lambda: nc.vector.tensor_copy(out=ki[:], in_=kf[:]), ["s5_kf"], ["s5_ki"])
    P.op("dve", lambda: nc.vector.tensor_copy(out=kf[:], in_=ki[:]), ["s5_ki"], ["s5_kf"])
    P.op("dve", lambda: nc.vector.scalar_tensor_tensor(out=out, in0=kf[:], scalar=-TWO_PI, in1=ang, op0=ALU.mult, op1=ALU.add), ["s5_kf", nm], [nm + "_o"])
    P.op("dve", lambda: nc.vector.tensor_scalar(out=kf[:], in0=out, scalar1=3.141592653589793, scalar2=-TWO_PI, op0=ALU.is_gt, op1=ALU.mult), [nm + "_o"], ["s5_kf"])
    P.op("dve", lambda: nc.vector.tensor_tensor(out=out, in0=out, in1=kf[:], op=ALU.add), [nm + "_o", "s5_kf"], [nm + "_o"])
    P.op("dve", lambda: nc.vector.tensor_scalar(out=kf[:], in0=out, scalar1=-3.141592653589793, scalar2=TWO_PI, op0=ALU.is_lt, op1=ALU.mult), [nm + "_o"], ["s5_kf"])
    P.op("dve", lambda: nc.vector.tensor_tensor(out=out, in0=out, in1=kf[:], op=ALU.add), [nm + "_o", "s5_kf"], [nm + "_o"])
    P.op("dve", lambda: nc.vector.tensor_scalar(out=out, in0=out, scalar1=3.1415925, scalar2=-3.1415925, op0=ALU.min, op1=ALU.max), [nm + "_o"], [nm + "_o"])
    P.op("act", lambda: nc.scalar.activation(out=out, in_=out, func=AF.Sin), [nm + "_o"], [nm + "_o"])


def s5_params(G, l):
    P, nc = G.P, G.nc
    sb = lambda n: P.sb("s5p_" + n, [128, 16])
    G.s5_ki = G.P.scopes[-1].enter_context(nc.sbuf_tensor("s5_ki_%d" % l, [128, 16], mybir.dt.int32))
    G.s5_kf = sb("kf")
    o_lr, o_li, o_dt = G.off["s5_lr"][0] + l * 16, G.off["s5_li"][0] + l * 16, G.off["s5_ldt"][0] + l * 16
    lr, li = G.pp[:, o_lr:o_lr + 16], G.pp[:, o_li:o_li + 16]
    dt, th, ph = sb("dt"), sb("th"), sb("ph")
    G.s5rho, G.s5cs, G.s5sn, G.s5fre, G.s5fim = sb("rho"), sb("cs"), sb("sn"), sb("fre"), sb("fim")
    t1, t2, den = sb("t1"), sb("t2"), sb("den")
    V = nc.vector
    P.op("act", lambda: nc.scalar.activation(out=dt[:], in_=G.pp[:, o_dt:o_dt + 16], func=AF.Exp), ["pp"], ["dt"])
    P.op("dve", lambda: V.tensor_tensor(out=t1[:], in0=lr, in1=dt[:], op=ALU.mult), ["pp", "dt"], ["t1"])
    P.op("act", lambda: nc.scalar.activation(out=G.s5rho[:], in_=t1[:], func=AF.Exp), ["t1"], ["rho"])
    P.op("dve", lambda: V.tensor_tensor(out=th[:], in0=li, in1=dt[:], op=ALU.mult), ["pp", "dt"], ["th"])
    P.op("dve", lambda: V.tensor_scalar(out=ph[:], in0=th[:], scalar1=1.5707963267948966, scalar2=None, op0=ALU.add), ["th"], ["ph"])
    s5_sin(G, G.s5sn[:], th[:], "th")
    s5_sin(G, G.s5cs[:], ph[:], "ph")
    lbre, lbim = sb("lbre"), sb("lbim")
    P.op("dve", lambda: V.tensor_tensor(out=lbre[:], in0=G.s5rho[:], in1=G.s5cs[:], op=ALU.mult), ["rho", "ph_o"], ["lbre"])
    P.op("dve", lambda: V.tensor_tensor(out=lbim[:], in0=G.s5rho[:], in1=G.s5sn[:], op=ALU.mult), ["rho", "th_o"], ["lbim"])
    P.op("dve", lambda: V.tensor_tensor(out=den[:], in0=lr, in1=lr, op=ALU.mult), ["pp"], ["den"])
    P.op("dve", lambda: V.tensor_tensor(out=t1[:], in0=li, in1=li, op=ALU.mult), ["pp", "rho"], ["t1"])
    P.op("dve", lambda: V.tensor_tensor(out=den[:], in0=den[:], in1=t1[:], op=ALU.add), ["den", "t1"], ["den"])
    P.op("dve", lambda: V.reciprocal(out=den[:], in_=den[:]), ["den"], ["den"])
    P.op("dve", lambda: V.tensor_scalar(out=lbre[:], in0=lbre[:], scalar1=-1.0, scalar2=None, op0=ALU.add), ["lbre"], ["lbre"])
    P.op("dve", lambda: V.tensor_tensor(out=t1[:], in0=lbre[:], in1=lr, op=ALU.mult), ["lbre", "pp", "den"], ["t1"])
    P.op("dve", lambda: V.tensor_tensor(out=t2[:], in0=lbim[:], in1=li, op=ALU.mult), ["lbim", "pp"], ["t2"])
    P.op("dve", lambda: V.tensor_tensor(out=t1[:], in0=t1[:], in1=t2[:], op=ALU.add), ["t1", "t2"], ["t1"])
    P.op("dve", lambda: V.tensor_tensor(out=G.s5fre[:], in0=t1[:], in1=den[:], op=ALU.mult), ["t1", "den"], ["fre"])
    P.op("dve", lambda: V.tensor_tensor(out=t1[:], in0=lbim[:], in1=lr, op=ALU.mult), ["lbim", "pp", "fre"], ["t1"])
    P.op("dve", lambda: V.tensor_tensor(out=t2[:], in0=lbre[:], in1=li, op=ALU.mult), ["lbre", "pp"], ["t2"])
    P.op("dve", lambda: V.tensor_tensor(out=t1[:], in0=t1[:], in1=t2[:], op=ALU.subtract), ["t1", "t2"], ["t1"])
    P.op("dve", lambda: V.tensor_tensor(out=G.s5fim[:], in0=t1[:], in1=den[:], op=ALU.mult), ["t1", "den"], ["fim"])


def s5_tables(G, col):
    P, nc = G.P, G.nc
    V = nc.vector
    Er, Ei, Pr, Pi = G.s5E
    cc = G.s5cc
    tm, tk = G.s5tm, "s5tm"
    P.op("dve", lambda: V.memset(Er[:, 0:1], 1.0), [], ["E"])
    P.op("dve", lambda: V.memset(Ei[:, 0:1], 0.0), [], ["E"])
    P.op("dve", lambda: V.tensor_copy(out=cc[:, 0:1], in_=G.s5cs[:, col:col + 1]), ["ph_o"], ["cc"])
    P.op("dve", lambda: V.tensor_copy(out=cc[:, 1:2], in_=G.s5sn[:, col:col + 1]), ["th_o"], ["cc"])
    n = 1
    while n < TT:
        m = min(n, TT - n)
        P.op("dve", lambda: V.tensor_scalar(out=tm[:, :m], in0=Ei[:, 0:m], scalar1=cc[:, 1:2], scalar2=None, op0=ALU.mult), ["E", "cc"], [tk])
        P.op("dve", lambda: V.scalar_tensor_tensor(out=Er[:, n:n + m], in0=Er[:, 0:m], scalar=cc[:, 0:1], in1=tm[:, :m], op0=ALU.mult, op1=ALU.subtract),
             ["E", "cc", tk], ["E"])
        P.op("dve", lambda: V.tensor_scalar(out=tm[:, :m], in0=Er[:, 0:m], scalar1=cc[:, 1:2], scalar2=None, op0=ALU.mult), ["E", "cc"], [tk])
        P.op("dve", lambda: V.scalar_tensor_tensor(out=Ei[:, n:n + m], in0=Ei[:, 0:m], scalar=cc[:, 0:1], in1=tm[:, :m], op0=ALU.mult, op1=ALU.add),
             ["E", "cc", tk], ["E"])
        n *= 2
        if n < TT:
            P.op("dve", lambda: V.tensor_tensor(out=cc[:, 2:3], in0=cc[:, 0:1], in1=cc[:, 0:1], op=ALU.mult), ["cc"], ["cc"])
            P.op("dve", lambda: V.tensor_tensor(out=cc[:, 3:4], in0=cc[:, 1:2], in1=cc[:, 1:2], op=ALU.mult), ["cc"], ["cc"])
            P.op("dve", lambda: V.scalar_tensor_tensor(out=cc[:, 1:2], in0=cc[:, 0:1], scalar=2.0, in1=cc[:, 1:2], op0=ALU.mult, op1=ALU.mult), ["cc"], ["cc"])
            P.op("dve", lambda: V.tensor_tensor(out=cc[:, 0:1], in0=cc[:, 2:3], in1=cc[:, 3:4], op=ALU.subtract), ["cc"], ["cc"])
    fre, fim = G.s5fre[:, col:col + 1], G.s5fim[:, col:col + 1]
    P.op("dve", lambda: V.tensor_scalar(out=Pr[:], in0=Ei[:], scalar1=fim, scalar2=None, op0=ALU.mult), ["E", "fim"], ["Ep"])
    P.op("dve", lambda: V.scalar_tensor_tensor(out=Pr[:], in0=Er[:], scalar=fre, in1=Pr[:], op0=ALU.mult, op1=ALU.add), ["E", "fre", "Ep"], ["Ep"])
    P.op("pool", lambda: nc.gpsimd.tensor_scalar(out=Pi[:], in0=Ei[:], scalar1=fre, scalar2=None, op0=ALU.mult), ["E", "fre"], ["Epi"])
    P.op("dve", lambda: V.scalar_tensor_tensor(out=Pi[:], in0=Er[:], scalar=fim, in1=Pi[:], op0=ALU.mult, op1=ALU.subtract), ["E", "fim", "Epi"], ["Epi"])


def emit_s5(G, l):
    P, nc = G.P, G.nc
    V = nc.vector
    s5_params(G, l)
    Bt = P.sb("s5Bt", [128, 2, 8, 2, 128])
    Ct = P.sb("s5Ct", [128, 2, 8, 2, 64])
    Wt = P.sb("s5Wt", [128, 2, 256])
    P.dma("sp", out=Bt[:], in_=G.s5B[l], reads=[], writes=["s5Bt"])
    P.dma("sp", out=Ct[:], in_=G.s5C[l], reads=[], writes=["s5Ct"])
    P.dma("sp", out=Wt[:], in_=G.s5W[l], reads=[], writes=["s5Wt"])
    for d in range(2):
        P.op("dve", lambda: V.tensor_scalar(out=Ct[:, d, :, 1, :], in0=Ct[:, d, :, 1, :], scalar1=-1.0, scalar2=None, op0=ALU.mult), ["s5Ct"], ["s5Ct"])
    G.s5E = [P.sb("s5E%d" % i, [128, TT]) for i in range(4)]
    G.s5cc = P.sb("s5cc", [128, 4])
    G.s5tm = P.sb("s5tm", [128, 2048])
    uT = [P.sb("s5u%d" % b, [128, 2, TT]) for b in range(NB)]
    ya = [P.sb("s5y%d" % b, [128, 2, TT]) for b in range(NB)]
    bw = [P.sb("s5bw%d" % i, [128, TT]) for i in range(2)]
    ww = [P.sb("s5w%d" % i, [128, TT]) for i in range(2)]
    tmp = Pool_(P, "s5t", [128, 512], 4)
    for b in range(NB):
        P.dma("sp", out=uT[b][:], in_=G.pT[b, OFF_S5:OFF_S5 + 256, :].rearrange("(c p) t -> p c t", p=128), reads=[], writes=[("s5u", b)])
        for c in range(2):
            P.op("act", lambda: nc.scalar.activation(out=ya[b][:, c, :], in_=uT[b][:, c, :], func=AF.Identity, scale=ppcol(G, "s5_d", l * 2 + c)),
                 [("s5u", b), "pp"], [("s5y", b)])
    for d in range(2):
        for j in range(8):
            col = d * 8 + j
            s5_tables(G, col)
            Er, Ei, Pr, Pi = G.s5E
            base = 64 * ((j % 4) // 2)
            ch = j // 4
            for b in range(NB):
                for (t0, n) in BLOCKS:
                    if d == 0:
                        ta = t0
                        rv = lambda ap_fn: ap_fn
                    else:
                        ta = 0 if t0 == 0 else (CTX + TT - (t0 + n))
                    pr, prk = P.bank()
                    pi, pik = P.bank()
                    P.op("pe", lambda: nc.tensor.matmul(pr[:, :n], lhsT=Bt[base:base + 64, d, j, 0, :], rhs=uT[b][base:base + 64, ch, t0:t0 + n],
                                                        start=True, stop=True), ["s5Bt", ("s5u", b)], [prk])
                    P.op("pe", lambda: nc.tensor.matmul(pi[:, :n], lhsT=Bt[base:base + 64, d, j, 1, :], rhs=uT[b][base:base + 64, ch, t0:t0 + n],
                                                        start=True, stop=True), ["s5Bt", ("s5u", b)], [pik])
                    bre = pr[:, :n] if d == 0 else rev(pr, 0, n)
                    bim = pi[:, :n] if d == 0 else rev(pi, 0, n)
                    ta_, tb_ = tmp.next(), tmp.next()
                    P.op("dve", lambda: V.tensor_tensor(out=ta_[0][:, :n], in0=bim, in1=Pi[:, ta:ta + n], op=ALU.mult), [pik, "Epi"], [ta_[1]])
                    P.op("dve", lambda: V.tensor_tensor(out=bw[0][:, ta:ta + n], in0=bre, in1=Pr[:, ta:ta + n], op=ALU.mult), [prk, "Ep"], ["bw0"])
                    P.op("pool", lambda: nc.gpsimd.tensor_tensor(out=bw[0][:, ta:ta + n], in0=bw[0][:, ta:ta + n], in1=ta_[0][:, :n], op=ALU.subtract),
                         ["bw0", ta_[1]], ["bw0"])
                    P.op("dve", lambda: V.tensor_tensor(out=tb_[0][:, :n], in0=bim, in1=Pr[:, ta:ta + n], op=ALU.mult), [pik, "Ep"], [tb_[1]])
                    P.op("dve", lambda: V.tensor_tensor(out=bw[1][:, ta:ta + n], in0=bre, in1=Pi[:, ta:ta + n], op=ALU.mult), [prk, "Epi"], ["bw1"])
                    P.op("pool", lambda: nc.gpsimd.tensor_tensor(out=bw[1][:, ta:ta + n], in0=bw[1][:, ta:ta + n], in1=tb_[0][:, :n], op=ALU.add),
                         ["bw1", tb_[1]], ["bw1"])
                rho = bc(G.s5rho[:, col:col + 1], TT)
                for i in range(2):
                    P.op("dve", lambda: V.tensor_tensor_scan(out=ww[i][:], data0=rho, data1=bw[i][:], initial=0.0, op0=ALU.mult, op1=ALU.add),
                         ["bw%d" % i, "rho"], ["ww%d" % i])
                P.op("pool", lambda: nc.gpsimd.tensor_tensor(out=bw[0][:], in0=Er[:], in1=ww[0][:], op=ALU.mult), ["E", "ww0"], ["bw0"])
                P.op("pool", lambda: nc.gpsimd.tensor_tensor(out=bw[1][:], in0=Er[:], in1=ww[1][:], op=ALU.mult), ["E", "ww1"], ["bw1"])
                P.op("dve", lambda: V.tensor_tensor(out=ww[1][:], in0=Ei[:], in1=ww[1][:], op=ALU.mult), ["E", "ww1"], ["ww1"])
                P.op("dve", lambda: V.tensor_tensor(out=ww[0][:], in0=Ei[:], in1=ww[0][:], op=ALU.mult), ["E", "ww0"], ["ww0"])
                P.op("dve", lambda: V.tensor_tensor(out=bw[0][:], in0=bw[0][:], in1=ww[1][:], op=ALU.subtract), ["bw0", "ww1"], ["bw0"])
                P.op("dve", lambda: V.tensor_tensor(out=bw[1][:], in0=bw[1][:], in1=ww[0][:], op=ALU.add), ["bw1", "ww0"], ["bw1"])
                for (t0, n) in BLOCKS:
                    ta = t0 if d == 0 else (0 if t0 == 0 else (CTX + TT - (t0 + n)))
                    py, pyk = P.bank()
                    P.op("pe", lambda: nc.tensor.matmul(py[base:base + 64, :n], lhsT=Ct[:, d, j, 0, :], rhs=bw[0][:, ta:ta + n], start=True, stop=False),
                         ["s5Ct", "bw0"], [pyk])
                    P.op("pe", lambda: nc.tensor.matmul(py[base:base + 64, :n], lhsT=Ct[:, d, j, 1, :], rhs=bw[1][:, ta:ta + n], start=False, stop=True),
                         ["s5Ct", "bw1"], [pyk])
                    src = py[base:base + 64, :n] if d == 0 else rev(py[base:base + 64, :], 0, n)
                    P.op("dve", lambda: V.tensor_tensor(out=ya[b][base:base + 64, ch, t0:t0 + n], in0=ya[b][base:base + 64, ch, t0:t0 + n], in1=src, op=ALU.add),
                         [("s5y", b), pyk], [("s5y", b)])
    for b in range(NB):
        y = ya[b]
        yk = ("s5y", b)
        u = uT[b]
        uk = ("s5u", b)
        for c in range(2):
            P.op("act", lambda: nc.scalar.activation(out=u[:, c, :], in_=y[:, c, :], func=AF.Square), [yk], [uk])
            P.op("dve", lambda: V.tensor_scalar(out=u[:, c, :], in0=u[:, c, :], scalar1=0.044715, scalar2=1.0, op0=ALU.mult, op1=ALU.add), [uk], [uk])
            P.op("dve", lambda: V.tensor_tensor(out=u[:, c, :], in0=u[:, c, :], in1=y[:, c, :], op=ALU.mult), [uk, yk], [uk])
            P.op("act", lambda: nc.scalar.activation(out=u[:, c, :], in_=u[:, c, :], func=AF.Tanh, scale=0.7978845608028654), [uk], [uk])
            P.op("dve", lambda: V.tensor_scalar(out=u[:, c, :], in0=u[:, c, :], scalar1=1.0, scalar2=0.5, op0=ALU.add, op1=ALU.mult), [uk], [uk])
            P.op("dve", lambda: V.tensor_tensor(out=y[:, c, :], in0=u[:, c, :], in1=y[:, c, :], op=ALU.mult), [uk, yk], [yk])
        for c in range(2):
            for (t0, n) in TBLK:
                pg, pgk = P.bank()
                for kc in range(2):
                    P.op("pe", lambda: nc.tensor.matmul(pg[:, :n], lhsT=Wt[:, kc, c * 128:(c + 1) * 128], rhs=y[:, kc, t0:t0 + n], start=(kc == 0), stop=(kc == 1)),
                         ["s5Wt", yk], [pgk])
                P.op("act", lambda: nc.scalar.activation(out=u[:, c, t0:t0 + n], in_=pg[:, :n], func=AF.Sigmoid, bias=ppcol(G, "s5_bglu", l * 2 + c)),
                     [pgk, "pp"], [uk])
            P.op("dve", lambda: V.tensor_tensor(out=u[:, c, :], in0=u[:, c, :], in1=y[:, c, :], op=ALU.mult), [uk, yk], [uk])
        P.dma("pool", out=G.yT[b, 512:768, :].rearrange("(c p) t -> p c t", p=128), in_=u[:], reads=[uk], writes=[("yT", b, 2)])


def dbg(G, name, ap, key, shape):
    if name not in G.cfg.get("dbg", ()):
        return
    if name in G.dbgdone:
        return
    G.dbgdone.add(name)
    t = G.nc.dram_tensor("dbg_" + name, list(shape), F32, kind="ExternalOutput").ap()
    G.P.dma("sp", out=t, in_=ap, reads=[key], writes=[("dbg", name)])


def rw_shift(G, l, x, xk, ti, p0=0, p1=128):
    P, nc = G.P, G.nc
    V = nc.vector
    tm, tk = G.rwtmp.next()
    for (a, b_) in ((0, CTX), (CTX, TT)):
        P.op("dve", lambda: V.memset(tm[p0:p1, a:a + 1], 0.0), [], [tk])
        P.op("dve", lambda: V.tensor_copy(out=tm[p0:p1, a + 1:b_], in_=x[p0:p1, a:b_ - 1]), [xk], [tk])
        P.op("pool", lambda: nc.gpsimd.tensor_tensor(out=tm[p0:p1, a:b_ - 1], in0=tm[p0:p1, a:b_ - 1], in1=x[p0:p1, a + 1:b_], op=ALU.add), [xk, tk], [tk])
    P.op("dve", lambda: V.tensor_scalar(out=tm[p0:p1, :], in0=tm[p0:p1, :], scalar1=G.rwhmu[p0:p1, l * 9 + ti:l * 9 + ti + 1], scalar2=None, op0=ALU.mult),
         [tk, "rwhmu"], [tk])
    P.op("dve", lambda: V.scalar_tensor_tensor(out=x[p0:p1, :], in0=x[p0:p1, :], scalar=G.rwomm[p0:p1, l * 9 + ti:l * 9 + ti + 1], in1=tm[p0:p1, :],
                                               op0=ALU.mult, op1=ALU.add), [xk, tk, "rwomm"], [xk])


def emit_rwkv(G, l):
    P, nc = G.P, G.nc
    V = nc.vector
    T9 = lambda n: P.sb("rw_" + n, [128, TT])
    ident = P.sb("rw_ident", [128, 128])
    P.dma("sp", out=ident[:], in_=G.ident_d, reads=[], writes=["ident"])
    Mk = P.sb("rw_M", [128, 2, 768])
    P.dma("sp", out=Mk[:], in_=G.rwM[:, :, 512:1280], reads=[], writes=["rwM"])
    Mk2 = P.sb("rw_M2", [128, 2, 4, 256])
    P.dma("sp", out=Mk2[:], in_=G.rwM2, reads=[], writes=["rwM2"])
    Wt = P.sb("rw_W", [128, 2, 256])
    P.dma("sp", out=Wt[:], in_=G.rwW[l], reads=[], writes=["rwW"])
    Gt = P.sb("rw_G", [128, 2, 256])
    P.dma("sp", out=Gt[:], in_=G.rwG[l], reads=[], writes=["rwG"])
    onesB = P.sb("rw_onesB", [128, 128])
    P.op("dve", lambda: V.tensor_scalar(out=onesB[:], in0=G.onesH[:], scalar1=64.0, scalar2=None, op0=ALU.mult), ["onesH"], ["onesB"])
    o_mu = G.off["rw_mu"][0]
    G.rwhmu = P.sb("rw_hmu", [128, NL * 9])
    G.rwomm = P.sb("rw_omm", [128, NL * 9])
    P.op("dve", lambda: V.tensor_scalar(out=G.rwhmu[:], in0=G.pp[:, o_mu:o_mu + NL * 9], scalar1=0.5, scalar2=None, op0=ALU.mult), ["pp"], ["rwhmu"])
    P.op("dve", lambda: V.tensor_scalar(out=G.rwomm[:], in0=G.pp[:, o_mu:o_mu + NL * 9], scalar1=-1.0, scalar2=1.0, op0=ALU.mult, op1=ALU.add), ["pp"], ["rwomm"])
    omka = P.sb("rw_omka", [128, 2])
    o_ka = G.off["rw_ka"][0] + l * 2
    P.op("dve", lambda: V.tensor_scalar(out=omka[:], in0=G.pp[:, o_ka:o_ka + 2], scalar1=-1.0, scalar2=1.0, op0=ALU.mult, op1=ALU.add), ["pp"], ["omka"])
    G.tmp = Pool_(P, "tmp", [128, 512], 2)
    wa, th, gdA = T9("wa"), T9("th"), T9("gdA")
    gdB = wa
    r, k, v, kk = T9("r"), T9("k"), T9("v"), T9("kk")
    L1, A1, K1, B1 = T9("L1"), T9("A1"), T9("K1"), T9("B1")
    T2 = T9("T2")
    L2, kL2 = th, "th"
    T1, kT1 = gdA, "gdA"
    yacc, bonus = T9("yacc"), T9("bonus")

    class OneTile:
        def next(self_):
            return yacc, "yacc"
    G.rwtmp = OneTile()
    vtok = P.sb("rw_vtok", [128, 18, 128])
    iap = ident[:, 0:1]
    I4 = bass.AP(iap.tensor, iap.offset, [list(iap.ap[0]), [0, 4], [1, 128]])

    def v4(ap):
        return ap.rearrange("p (a h s) -> p a h s", a=2, h=2)

    def v44(ap):
        return ap.rearrange("p (a s) -> p a s", a=4)

    def m4(d, i):
        ap = Mk2[:, d, i, 0:1]
        return bass.AP(ap.tensor, ap.offset, [list(ap.ap[0]), [128, 2], [0, 2], [1, 128]])

    def quad(ps, X, Y, xk, yk, psk, need_n=True, need_t=True):
        for h in range(2):
            n_ = slice(h * 128, (h + 1) * 128)
            t_ = slice(256 + h * 128, 256 + (h + 1) * 128)
            if need_n:
                P.op("pe", lambda: nc.tensor.matmul(ps[:, n_], lhsT=X[:, t_], rhs=Y[:, n_], start=True, stop=True), [xk, yk], [psk])
            if need_t:
                P.op("pe", lambda: nc.tensor.matmul(ps[:, t_], lhsT=Y[:, n_], rhs=X[:, t_], start=True, stop=True), [xk, yk], [psk])

    ST = []
    for d in range(2):
        st = dict(d=d)
        for nm, shp in (("Nb", [128, 512]), ("Nq", [128, 512]), ("No0", [128, 512]), ("No1", [128, 512]), ("No2", [128, 512]), ("Tm", [128, 512]),
                        ("WV", [128, 512]), ("scB", [128, 512]), ("scC", [128, 256]), ("ktok", [128, 128]), ("btok", [128, 128]), ("Xs", [128, 128]),
                        ("S0", [128, 64]), ("gC", [128, 18])):
            st[nm] = P.sb("rw%d_%s" % (d, nm), shp)
        ST.append(st)
    ST[0].update(RT=L1, KT=A1, KH=K1, BH=B1, kRT="L1", kKT="A1", kKH="K1", kBH="B1")
    ST[1].update(RT=r, KT=kk, KH=k, BH=T2, kRT="r", kKT="kk", kKH="k", kBH="T2")

    def proj(d, hp, ld_t, ld_k, a_t, a_k):
        hs = slice(hp * 128, (hp + 1) * 128)
        for (t0, n) in TBLK:
            ps, pk_ = P.bank()
            P.op("pe", lambda: nc.tensor.matmul(ps[:, :n], lhsT=Wt[0:64, d, hs], rhs=th[0:64, t0:t0 + n], start=True, stop=True), ["rwW", "th"], [pk_])
            o_ = G.off["rw_w0"][0] + l * 4 + d * 2 + hp
            P.op("act", lambda: nc.scalar.activation(out=ld_t[:, t0:t0 + n], in_=ps[:, :n], func=AF.Sigmoid, bias=G.pp[:, o_:o_ + 1]), [pk_, "pp"], [ld_k])
            ps2, pk2 = P.bank()
            P.op("pe", lambda: nc.tensor.matmul(ps2[:, :n], lhsT=Wt[64:128, d, hs], rhs=wa[64:128, t0:t0 + n], start=True, stop=True), ["rwW", "wa"], [pk2])
            o2 = G.off["rw_a0"][0] + l * 4 + d * 2 + hp
            P.op("act", lambda: nc.scalar.activation(out=a_t[:, t0:t0 + n], in_=ps2[:, :n], func=AF.Sigmoid, bias=G.pp[:, o2:o2 + 1]), [pk2, "pp"], [a_k])
        P.op("dve", lambda: V.tensor_scalar(out=ld_t[:], in0=ld_t[:], scalar1=-0.6065306597126334, scalar2=None, op0=ALU.mult), [ld_k], [ld_k])

    def add_bonus(d, hp, kd_t, kd_k):
        tmf, tmk = G.rwtmp.next()
        P.op("dve", lambda: V.scalar_tensor_tensor(out=tmf[:], in0=r[:], scalar=ppcol(G, "rw_rk", l * 2 + hp), in1=kd_t[:], op0=ALU.mult, op1=ALU.mult),
             ["r", kd_k, "pp"], [tmk])
        for (t0, n) in TBLK:
            ps, pk_ = P.bank()
            P.op("pe", lambda: nc.tensor.matmul(ps[:, :n], lhsT=onesB[:], rhs=tmf[:, t0:t0 + n], start=True, stop=True), [tmk, "onesB"], [pk_])
            if d == 0:
                P.op("dve", lambda: V.tensor_tensor(out=bonus[:, t0:t0 + n], in0=ps[:, :n], in1=v[:, t0:t0 + n], op=ALU.mult), [pk_, "v"], ["bonus"])
            else:
                sq, sqk = G.tmp.next()
                P.op("dve", lambda: V.tensor_tensor(out=sq[:, :n], in0=ps[:, :n], in1=v[:, t0:t0 + n], op=ALU.mult), [pk_, "v"], [sqk])
                P.op("pool", lambda: nc.gpsimd.tensor_tensor(out=bonus[:, t0:t0 + n], in0=bonus[:, t0:t0 + n], in1=sq[:, :n], op=ALU.add), ["bonus", sqk], ["bonus"])

    def cumlog(d, ld_t, ld_k, cl_t, cl_k):
        for c in range(18):
            cs = slice(c * 128, (c + 1) * 128)
            if d == 0:
                P.op("dve", lambda: V.tensor_tensor_scan(out=cl_t[:, cs], data0=G.ones9[:, 0:128], data1=ld_t[:, cs], initial=0.0, op0=ALU.mult, op1=ALU.add),
                     [ld_k, "ones9"], [cl_k])
            else:
                P.op("dve", lambda: V.tensor_tensor_scan(out=rev(cl_t, c * 128, (c + 1) * 128), data0=G.ones9[:, 0:128], data1=rev(ld_t, c * 128, (c + 1) * 128),
                                                         initial=0.0, op0=ALU.mult, op1=ALU.add), [ld_k, "ones9"], [cl_k])

    def save_gc(d, er_t, er_k):
        ap = er_t[:, (127 if d == 0 else 0):(127 if d == 0 else 0) + 1]
        P.op("dve", lambda: V.tensor_copy(out=ST[d]["gC"][:], in_=bass.AP(ap.tensor, ap.offset, [list(ap.ap[0]), [128, 18]])), [er_k], ["gC%d" % d])

    def chunk_gen(st):
        d = st["d"]
        kt_, bh_, kh_, rt_ = st["KT"], st["BH"], st["KH"], st["RT"]
        kKT, kBH, kKH, kRT = st["kKT"], st["kBH"], st["kKH"], st["kRT"]
        K_ = lambda n: "%s%d" % (n, d)
        Nb, Nq, Tm, WV, scB, scC, ktok, btok, Xs, S0 = (st[n] for n in ("Nb", "Nq", "Tm", "WV", "scB", "scC", "ktok", "btok", "Xs", "S0"))
        No = [st["No0"], st["No1"], st["No2"]]
        P.op("dve", lambda: V.memset(S0[:], 0.0), [], [K_("S0")])
        order = list(range(18)) if d == 0 else [1, 0] + list(range(17, 1, -1))
        if d in G.cfg.get("skipdir", ()):
            order = []
        for c in order:
            cs = slice(c * 128, (c + 1) * 128)
            pA, pAk = P.bank()
            pB, pBk = P.bank()
            pC, pCk = P.bank()
            for h in range(2):
                hb = slice(64 * h, 64 * h + 64)
                mm = lambda out, lt, rh, ks, ok: P.op("pe", lambda: nc.tensor.matmul(out, lhsT=lt[hb, cs], rhs=rh[hb, cs], start=True, stop=True), ks, [ok])
                mm(pA[:, h * 128:(h + 1) * 128], kt_, bh_, [kKT, kBH], pAk)
                mm(pA[:, 256 + h * 128:256 + (h + 1) * 128], bh_, kt_, [kKT, kBH], pAk)
                mm(pB[:, h * 128:(h + 1) * 128], kh_, kt_, [kKH, kKT], pBk)
                mm(pB[:, 256 + h * 128:256 + (h + 1) * 128], kh_, rt_, [kKH, kRT], pBk)
                mm(pC[:, h * 128:(h + 1) * 128], bh_, rt_, [kBH, kRT], pCk)
            yield
            P.op("dve", lambda: V.tensor_tensor(out=v4(Nb[:]), in0=v4(pA[:]), in1=m4(d, 0), op=ALU.mult), [pAk, "rwM2"], [K_("Nb")])
            for i_ in range(3):
                P.op("dve" if i_ != 1 else "pool", (lambda: V.tensor_tensor(out=v4(No[i_][:]), in0=v4(pA[:]), in1=m4(d, i_ + 1), op=ALU.mult)) if i_ != 1 else
                     (lambda: V.tensor_tensor(out=v4(No[i_][:]), in0=v4(pA[:]), in1=m4(d, i_ + 1), op=ALU.mult)), [pAk, "rwM2"], [K_("No%d" % i_)]) if False else \
                    P.op("dve", lambda: V.tensor_tensor(out=v4(No[i_][:]), in0=v4(pA[:]), in1=m4(d, i_ + 1), op=ALU.mult), [pAk, "rwM2"], [K_("No%d" % i_)])
            P.op("dve", lambda: V.tensor_tensor(out=scB[:], in0=pB[:], in1=Mk[:, d, 0:512], op=ALU.mult), [pBk, "rwM"], [K_("scB")])
            P.op("dve", lambda: V.tensor_tensor(out=scC[:], in0=pC[:, 0:256], in1=Mk[:, d, 512:768], op=ALU.mult), [pCk, "rwM"], [K_("scC")])
            pt, ptk = P.bank()
            P.op("pe", lambda: nc.tensor.transpose(out=pt[:, 0:128], in_=kh_[:, cs], identity=ident[:]), [kKH, "ident"], [ptk])
            P.op("pe", lambda: nc.tensor.transpose(out=pt[:, 128:256], in_=bh_[:, cs], identity=ident[:]), [kBH, "ident"], [ptk])
            P.op("act", lambda: nc.scalar.copy(out=ktok[:], in_=pt[:, 0:128]), [ptk], [K_("ktok")])
            P.op("act", lambda: nc.scalar.mul(out=btok[:], in_=pt[:, 128:256], mul=-1.0), [ptk], [K_("btok")])
            yield
            P.op("dve", lambda: V.tensor_tensor(out=v44(Tm[:]), in0=v44(Nb[:]), in1=I4, op=ALU.add), [K_("Nb"), "ident"], [K_("Tm")])
            cur, curk = Nb, K_("Nb")
            for j in range(3):
                pq, pqk = P.bank()
                quad(pq, cur, cur, curk, curk, pqk, need_n=(j < 2))
                yield
                if j < 2:
                    P.op("act", lambda: nc.scalar.copy(out=Nq[:], in_=pq[:]), [pqk], [K_("Nq")])
                else:
                    P.op("act", lambda: nc.scalar.copy(out=Nq[:, 256:512], in_=pq[:, 256:512]), [pqk], [K_("Nq")])
                pw, pwk = P.bank()
                quad(pw, Nq, Tm, K_("Nq"), K_("Tm"), pwk)
                yield
                P.op("dve", lambda: V.tensor_tensor(out=Tm[:], in0=Tm[:], in1=pw[:], op=ALU.add), [K_("Tm"), pwk], [K_("Tm")])
                cur, curk = Nq, K_("Nq")
            for i_ in range(3):
                pw, pwk = P.bank()
                quad(pw, No[i_], Tm, K_("No%d" % i_), K_("Tm"), pwk, need_t=False)
                yield
                P.op("act", lambda: nc.scalar.copy(out=WV[:, 0:256], in_=pw[:, 0:256]), [pwk], [K_("WV")])
                pw2, pwk2 = P.bank()
                last_ = (i_ == 2)
                quad(pw2, Tm, WV, K_("Tm"), K_("WV"), pwk2, need_n=not last_)
                yield
                if last_:
                    P.op("dve", lambda: V.tensor_tensor(out=Tm[:, 256:512], in0=Tm[:, 256:512], in1=pw2[:, 256:512], op=ALU.add), [K_("Tm"), pwk2], [K_("Tm")])
                else:
                    P.op("dve", lambda: V.tensor_tensor(out=Tm[:], in0=Tm[:], in1=pw2[:], op=ALU.add), [K_("Tm"), pwk2], [K_("Tm")])
            px, pxk = P.bank()
            for h in range(2):
                hb = slice(64 * h, 64 * h + 64)
                P.op("pe", lambda: nc.tensor.matmul(px[:, h * 64:(h + 1) * 64], lhsT=kt_[hb, cs], rhs=S0[hb, :], start=True, stop=False), [kKT, K_("S0")], [pxk])
                P.op("pe", lambda: nc.tensor.matmul(px[:, h * 64:(h + 1) * 64], lhsT=scB[:, h * 128:(h + 1) * 128], rhs=vtok[:, c, h * 64:(h + 1) * 64], start=False, stop=True),
                     [K_("scB"), "vtok"], [pxk])
            yield
            P.op("dve", lambda: V.tensor_copy(out=Xs[:], in_=px[:, 0:128]), [pxk], [K_("Xs")])
            pz, pzk = P.bank()
            for h in range(2):
                P.op("pe", lambda: nc.tensor.matmul(pz[:, h * 64:(h + 1) * 64], lhsT=Tm[:, 256 + h * 128:256 + (h + 1) * 128], rhs=Xs[:, h * 64:(h + 1) * 64],
                                                    start=True, stop=True), [K_("Tm"), K_("Xs")], [pzk])
            yield
            P.op("dve", lambda: V.tensor_copy(out=Xs[:], in_=pz[:, 0:128]), [pzk], [K_("Xs")])
            py, pyk = P.bank()
            for h in range(2):
                hb = slice(64 * h, 64 * h + 64)
                P.op("pe", lambda: nc.tensor.matmul(py[hb, 0:128], lhsT=S0[hb, :], rhs=rt_[hb, cs], start=True, stop=False), [K_("S0"), kRT], [pyk])
                P.op("pe", lambda: nc.tensor.matmul(py[hb, 0:128], lhsT=vtok[:, c, h * 64:(h + 1) * 64], rhs=scB[:, 256 + h * 128:256 + (h + 1) * 128], start=False, stop=False),
                     ["vtok", K_("scB")], [pyk])
                P.op("pe", lambda: nc.tensor.matmul(py[hb, 0:128], lhsT=Xs[:, h * 64:(h + 1) * 64], rhs=scC[:, h * 128:(h + 1) * 128], start=False, stop=True),
                     [K_("Xs"), K_("scC")], [pyk])
            pS, pSk = P.bank()
            for h in range(2):
                hb = slice(64 * h, 64 * h + 64)
                P.op("pe", lambda: nc.tensor.matmul(pS[hb, 0:64], lhsT=ktok[:, hb], rhs=vtok[:, c, h * 64:(h + 1) * 64], start=True, stop=False), [K_("ktok"), "vtok"], [pSk])
                P.op("pe", lambda: nc.tensor.matmul(pS[hb, 0:64], lhsT=btok[:, hb], rhs=Xs[:, h * 64:(h + 1) * 64], start=False, stop=True), [K_("btok"), K_("Xs")], [pSk])
            yield
            P.op("dve", lambda: V.tensor_tensor(out=yacc[:, cs], in0=yacc[:, cs], in1=py[:, 0:128], op=ALU.add), [pyk, ("yacc", c)], [("yacc", c)])
            gC = st["gC"][:, c:c + 1]
            P.op("dve", lambda: V.tensor_scalar(out=S0[:], in0=S0[:], scalar1=gC, scalar2=None, op0=ALU.mult), [K_("S0"), K_("gC")], [K_("S0")])
            P.op("dve", lambda: V.scalar_tensor_tensor(out=S0[:], in0=pS[:, 0:64], scalar=gC, in1=S0[:], op0=ALU.mult, op1=ALU.add), [pSk, K_("gC"), K_("S0")], [K_("S0")])
            yield

    yall = [("yacc", c) for c in range(18)]
    for b in range(NB):
        for hp in range(2):
            hs = slice(hp * 128, (hp + 1) * 128)
            if True:
                P.dma("sp", out=wa[:], in_=G.pT[b, 768:896, :], reads=[], writes=["wa"])
                rw_shift(G, l, wa, "wa", 6)
                P.op("act", lambda: nc.scalar.activation(out=th[0:64, :], in_=wa[0:64, :], func=AF.Tanh), ["wa"], ["th"])
            for ti, (t_, nm) in enumerate(((r, "r"), (k, "k"), (v, "v"))):
                P.dma("sp", out=t_[:], in_=G.pT[b, ti * 256 + hp * 128:ti * 256 + (hp + 1) * 128, :], reads=[], writes=[nm])
                rw_shift(G, l, t_, nm, ti * 2 + hp)
            P.op("dve", lambda: V.tensor_scalar(out=kk[:], in0=k[:], scalar1=ppcol(G, "rw_kk", l * 2 + hp), scalar2=None, op0=ALU.mult), ["k", "pp"], ["kk"])
            for (t0, n) in TBLK:
                sq, sqk = G.tmp.next()
                P.op("act", lambda: nc.scalar.activation(out=sq[:, :n], in_=kk[:, t0:t0 + n], func=AF.Square), ["kk"], [sqk])
                ps, pk_ = P.bank()
                P.op("pe", lambda: nc.tensor.matmul(ps[:, :n], lhsT=onesB[:], rhs=sq[:, :n], start=True, stop=True), [sqk, "onesB"], [pk_])
                P.op("dve", lambda: V.tensor_scalar(out=sq[:, :n], in0=ps[:, :n], scalar1=1e-12, scalar2=None, op0=ALU.max), [pk_], [sqk])
                P.op("act", lambda: nc.scalar.activation(out=sq[:, :n], in_=sq[:, :n], func=AF.Sqrt), [sqk], [sqk])
                P.op("dve", lambda: V.reciprocal(out=sq[:, :n], in_=sq[:, :n]), [sqk], [sqk])
                P.op("dve", lambda: V.tensor_tensor(out=kk[:, t0:t0 + n], in0=kk[:, t0:t0 + n], in1=sq[:, :n], op=ALU.mult), ["kk", sqk], ["kk"])
            for c in range(18):
                pt, ptk = P.bank()
                P.op("pe", lambda: nc.tensor.transpose(out=pt[:, 0:128], in_=v[:, c * 128:(c + 1) * 128], identity=ident[:]), ["v", "ident"], [ptk])
                P.op("act", lambda: nc.scalar.copy(out=vtok[:, c, :], in_=pt[:, 0:128]), [ptk], ["vtok"])
            proj(0, hp, L1, "L1", A1, "A1")
            proj(1, hp, T1, kT1, T2, "T2")
            P.op("dve", lambda: V.tensor_scalar(out=K1[:], in0=A1[:], scalar1=ppcol(G, "rw_ka", l * 2 + hp), scalar2=omka[:, hp:hp + 1], op0=ALU.mult, op1=ALU.add),
                 ["A1", "pp", "omka"], ["K1"])
            P.op("pool", lambda: nc.gpsimd.tensor_tensor(out=K1[:], in0=K1[:], in1=k[:], op=ALU.mult), ["K1", "k"], ["K1"])
            P.op("dve", lambda: V.tensor_tensor(out=B1[:], in0=kk[:], in1=A1[:], op=ALU.mult), ["kk", "A1"], ["B1"])
            add_bonus(0, hp, K1, "K1")
            cumlog(0, L1, "L1", L2, kL2)
            P.op("dve", lambda: V.tensor_tensor(out=A1[:], in0=L2[:], in1=L1[:], op=ALU.subtract), [kL2, "L1"], ["A1"])
            P.op("act", lambda: nc.scalar.activation(out=A1[:], in_=A1[:], func=AF.Exp), ["A1"], ["A1"])
            P.op("dve", lambda: V.tensor_tensor(out=A1[:], in0=A1[:], in1=kk[:], op=ALU.mult), ["A1", "kk"], ["A1"])
            P.op("act", lambda: nc.scalar.activation(out=L1[:], in_=L2[:], func=AF.Exp, scale=-1.0), [kL2], ["L1"])
            P.op("pool", lambda: nc.gpsimd.tensor_tensor(out=B1[:], in0=B1[:], in1=L1[:], op=ALU.mult), ["B1", "L1"], ["B1"])
            P.op("dve", lambda: V.tensor_tensor(out=K1[:], in0=K1[:], in1=L1[:], op=ALU.mult), ["K1", "L1"], ["K1"])
            P.op("act", lambda: nc.scalar.activation(out=L2[:], in_=L2[:], func=AF.Exp), [kL2], [kL2])
            P.op("dve", lambda: V.tensor_tensor(out=L1[:], in0=r[:], in1=L2[:], op=ALU.mult), ["r", kL2, "B1", "K1"], ["L1"])
            save_gc(0, L2, kL2)
            P.op("dve", lambda: V.tensor_scalar(out=L2[:], in0=T2[:], scalar1=ppcol(G, "rw_ka", l * 2 + hp), scalar2=omka[:, hp:hp + 1], op0=ALU.mult, op1=ALU.add),
                 ["T2", "pp", "omka", "gC0"], [kL2])
            P.op("pool", lambda: nc.gpsimd.tensor_tensor(out=k[:], in0=k[:], in1=L2[:], op=ALU.mult), [kL2, "k", "K1"], ["k"])
            P.op("dve", lambda: V.tensor_tensor(out=T2[:], in0=kk[:], in1=T2[:], op=ALU.mult), ["kk", "T2"], ["T2"])
            add_bonus(1, hp, k, "k")
            cumlog(1, T1, kT1, L2, kL2)
            P.op("dve", lambda: V.tensor_tensor(out=v[:], in0=L2[:], in1=T1[:], op=ALU.subtract), [kL2, kT1, "bonus", "vtok"], ["v"])
            P.op("act", lambda: nc.scalar.activation(out=v[:], in_=v[:], func=AF.Exp), ["v"], ["v"])
            P.op("dve", lambda: V.tensor_tensor(out=kk[:], in0=kk[:], in1=v[:], op=ALU.mult), ["kk", "v", "A1", "B1"], ["kk"])
            P.op("act", lambda: nc.scalar.activation(out=v[:], in_=L2[:], func=AF.Exp, scale=-1.0), [kL2, "kk"], ["v"])
            P.op("pool", lambda: nc.gpsimd.tensor_tensor(out=T2[:], in0=T2[:], in1=v[:], op=ALU.mult), ["T2", "v"], ["T2"])
            P.op("dve", lambda: V.tensor_tensor(out=k[:], in0=k[:], in1=v[:], op=ALU.mult), ["k", "v"], ["k"])
            P.op("act", lambda: nc.scalar.activation(out=L2[:], in_=L2[:], func=AF.Exp), [kL2], [kL2])
            P.op("dve", lambda: V.tensor_tensor(out=r[:], in0=r[:], in1=L2[:], op=ALU.mult), ["r", kL2, "L1"], ["r"])
            save_gc(1, L2, kL2)
            P.dma("sp", out=gdA[:], in_=G.pT[b, 896:1024, :], reads=[], writes=["gdA"])
            rw_shift(G, l, gdA, "gdA", 7)
            P.dma("sp", out=gdB[64:96, :], in_=G.pT[b, 1024:1056, :], reads=[], writes=["wa"])
            rw_shift(G, l, gdB, "wa", 8, 64, 96)
            P.op("act", lambda: nc.scalar.activation(out=gdA[:], in_=gdA[:], func=AF.Sigmoid), ["gdA"], ["gdA"])
            P.op("act", lambda: nc.scalar.activation(out=gdB[64:96, :], in_=gdB[64:96, :], func=AF.Sigmoid), ["wa"], ["wa"])
            P.op("dve", lambda: V.memset(yacc[:], 0.0), ["yacc"], yall + ["yacc"])
            live = [chunk_gen(ST[0]), chunk_gen(ST[1])]
            while live:
                for g_ in list(live):
                    try:
                        next(g_)
                    except StopIteration:
                        live.remove(g_)
            P.op("dve", lambda: V.tensor_copy(out=yacc[:, 0:1], in_=yacc[:, 0:1]), yall, yall + ["yacc"])
            dbg(G, "yraw", yacc[:], "yacc", [128, TT]); dbg(G, "bonus", bonus[:], "bonus", [128, TT])
            for (t0, n) in TBLK:
                head_norm_blk(G, yacc[:, t0:t0 + n], "yacc", n, True, 1)
                P.op("dve", lambda: V.scalar_tensor_tensor(out=yacc[:, t0:t0 + n], in0=yacc[:, t0:t0 + n], scalar=ppcol(G, "rw_gn", l * 2 + hp), in1=bonus[:, t0:t0 + n],
                                                           op0=ALU.mult, op1=ALU.add), ["yacc", "bonus", "pp"], ["yacc"])
                pg, pgk = P.bank()
                P.op("pe", lambda: nc.tensor.matmul(pg[:, :n], lhsT=Gt[:, 0, hs], rhs=gdA[:, t0:t0 + n], start=True, stop=False), ["rwG", "gdA"], [pgk])
                P.op("pe", lambda: nc.tensor.matmul(pg[:, :n], lhsT=Gt[64:96, 1, hs], rhs=gdB[64:96, t0:t0 + n], start=False, stop=True), ["rwG", "wa"], [pgk])
                P.op("dve", lambda: V.tensor_tensor(out=yacc[:, t0:t0 + n], in0=yacc[:, t0:t0 + n], in1=pg[:, :n], op=ALU.mult), ["yacc", pgk], ["yacc"])
            P.dma("pool", out=G.yT[b, hp * 128:(hp + 1) * 128, :], in_=yacc[:], reads=["yacc"], writes=[("yT", b, 0, hp)] + yall)


def emit_mixers(G, l, which=("hgrn", "ret", "s5", "rwkv")):
    P = G.P
    if "rwkv" in which:
        with P.phase():
            emit_rwkv(G, l)
    if "s5" in which:
        with P.phase():
            emit_s5(G, l)
    if "ret" in which or "hgrn" in which:
        with P.phase():
            gla2_tiles(G)
            if "ret" in which:
                G.rope = P.sb("rope", [128, 2, SEQ])
                P.dma("sp", out=G.rope[:], in_=G.rope_d, reads=[], writes=["rope"])
                G.perm = P.sb("perm", [128, 128])
                P.dma("sp", out=G.perm[:], in_=G.perm_d, reads=[], writes=["perm"])
                for b in range(NB):
                    emit_ret2(G, l, b)
            if "hgrn" in which:
                for b in range(NB):
                    emit_hgrn2(G, l, b)


def host_inputs(inp):
    x, ctx, c, c_ctx = inp["x"], inp["ctx"], inp["c"], inp["c_ctx"]
    w = host_weights(inp)
    pp = pack_params(inp).build()
    rope, perm = rope_tables()
    maps = []
    for core in range(NCORES):
        bs = slice(core * NB, (core + 1) * NB)
        xin = np.concatenate([ctx[bs], x[bs]], axis=1).transpose(0, 2, 1)
        crow = np.concatenate([c[bs], c_ctx[None]], axis=0)
        cT = fm(crow).transpose(0, 2, 1)
        m = dict(xin=np.ascontiguousarray(xin, np.float32), cT=np.ascontiguousarray(cT, np.float32), pp=pp, rope=rope, perm=perm)
        m.update(w)
        maps.append(m)
    return maps


def kernel(**inputs):
    inp = {k: np.asarray(v) for k, v in inputs.items()}
    nc = build_program({})
    maps = host_inputs(inp)
    res = run_bass_kernel_spmd(nc, maps, core_ids=list(range(NCORES)))
    outs = [r["out"] for r in res.results]
    y = np.concatenate(outs, axis=0).transpose(0, 2, 1)
    return np.ascontiguousarray(y, np.float32)
```
